# Optimizing a Trainium2 kernel written in Bass

```python
import math
import jax, jax.numpy as jnp
from jax import lax
import numpy as np

D_MODEL = 1024
BATCH = 8
SEQ = 4096
DEPTH = 2
DEC_BATCH = 32
DEC_SEQ = 1
PAST_LEN = 16384
PAGE_SIZE = 128

N_A_LAYERS = (DEPTH + 1) // 2
N_B_LAYERS = DEPTH // 2
EPS = 1e-6

A_WIDTH = 2 * D_MODEL
A_CHUNK = 128
A_GROUPS = 16
A_GROUP_DIM = A_WIDTH // A_GROUPS

N_HEADS = 16
HEAD_DIM = D_MODEL // N_HEADS
N_KV = 4
Q_PER_KV = N_HEADS // N_KV
ROT_DIM = HEAD_DIM // 4
ROPE_THETA = 500000.0
CMP_BLOCK = 32
CMP_STRIDE = 16
CMP_HIDDEN = HEAD_DIM
SEL_BLOCK = 64
N_SELECT = 16
WINDOW = 512
NSA_Q_BLOCK = 32
N_BRANCH = 3
B_QW = N_HEADS * HEAD_DIM
B_KVW = 2 * N_KV * HEAD_DIM
B_IN = B_QW + N_BRANCH * B_KVW + N_BRANCH * B_QW + N_BRANCH * N_HEADS

kernel_name = 'hybrid_chunkmlp_nsa_decode_step'


def rmsnorm(x, g):
    xf = x.astype(jnp.float32)
    y = xf * lax.rsqrt(jnp.mean(xf * xf, axis=-1, keepdims=True) + EPS)
    return (y * g.astype(jnp.float32)).astype(x.dtype)


def layernorm(x, g, b):
    xf = x.astype(jnp.float32)
    xc = xf - jnp.mean(xf, axis=-1, keepdims=True)
    y = xc * lax.rsqrt(jnp.mean(xc * xc, axis=-1, keepdims=True) + EPS)
    return (y * g.astype(jnp.float32) + b.astype(jnp.float32)).astype(x.dtype)


def partial_rope(x, pos):
    half = ROT_DIM // 2
    freqs = jnp.exp(-math.log(ROPE_THETA) * jnp.arange(half, dtype=jnp.float32) * (2.0 / ROT_DIM))
    ang = pos.astype(jnp.float32)[:, None] * freqs[None, :]
    cos = jnp.cos(ang)[None, :, None, :]
    sin = jnp.sin(ang)[None, :, None, :]
    xr = x[..., :ROT_DIM].astype(jnp.float32)
    x1, x2 = xr[..., :half], xr[..., half:]
    rot = jnp.concatenate([x1 * cos - x2 * sin, x2 * cos + x1 * sin], axis=-1)
    return jnp.concatenate([rot.astype(x.dtype), x[..., ROT_DIM:]], axis=-1)


def masked_softmax(s, mask):
    s = jnp.where(mask, s, -jnp.inf)
    m = jnp.max(s, axis=-1, keepdims=True)
    m = jnp.where(jnp.isfinite(m), m, 0.0)
    e = jnp.where(mask, jnp.exp(s - m), 0.0)
    return e / jnp.maximum(jnp.sum(e, axis=-1, keepdims=True), jnp.finfo(jnp.float32).tiny)


def chunk_spatial_mix(v, w_s, b_s):
    B, L, _ = v.shape
    n_chunk = -(-L // A_CHUNK)
    pad = n_chunk * A_CHUNK - L
    vp = jnp.pad(v, ((0, 0), (0, pad), (0, 0))).reshape(B, n_chunk, A_CHUNK, A_GROUPS, A_GROUP_DIM)
    causal = jnp.tril(jnp.ones((A_CHUNK, A_CHUNK), dtype=bool))
    wm = jnp.where(causal[None], w_s, 0).astype(v.dtype)
    out = jnp.einsum('gts,bnsgc->bntgc', wm, vp) + b_s.T[None, None, :, :, None].astype(v.dtype)
    return out.reshape(B, n_chunk * A_CHUNK, A_WIDTH)[:, :L]


def chunk_mlp_mixer(xn, w_in, ln_g, ln_b, w_s, b_s, w_out):
    u, v, z = jnp.split(xn @ w_in, 3, axis=-1)
    u = jax.nn.gelu(u, approximate=False)
    v = layernorm(jax.nn.gelu(v, approximate=False), ln_g, ln_b)
    y = u * chunk_spatial_mix(v, w_s, b_s) * jax.nn.silu(z)
    return y @ w_out, v


def nsa_project(xn, w_in, pos):
    B, S, _ = xn.shape
    h = xn @ w_in
    o1 = B_QW
    o2 = o1 + N_BRANCH * B_KVW
    o3 = o2 + N_BRANCH * B_QW
    q = h[..., :o1].reshape(B, S, N_HEADS, HEAD_DIM)
    kv = h[..., o1:o2].reshape(B, S, N_BRANCH, 2, N_KV, HEAD_DIM)
    z = h[..., o2:o3].reshape(B, S, N_BRANCH, N_HEADS, HEAD_DIM)
    gl = h[..., o3:].reshape(B, S, N_BRANCH, N_HEADS)
    kv_c = kv[:, :, 0]
    kv_s = jnp.stack([partial_rope(kv[:, :, 1, 0], pos), kv[:, :, 1, 1]], axis=2)
    kv_w = jnp.stack([partial_rope(kv[:, :, 2, 0], pos), kv[:, :, 2, 1]], axis=2)
    return q, partial_rope(q, pos), kv_c, kv_s, kv_w, z, gl


def compress(k, pe, w1, b1, w2, b2):
    B, L = k.shape[:2]
    n_half = L // CMP_STRIDE
    kh = k[:, :n_half * CMP_STRIDE].reshape(B, n_half, CMP_STRIDE, N_KV, HEAD_DIM)
    lo = jnp.einsum('bmjgc,jch->bmgh', kh + pe[None, None, :CMP_STRIDE, None, :], w1[:CMP_STRIDE])
    hi = jnp.einsum('bmjgc,jch->bmgh', kh + pe[None, None, CMP_STRIDE:, None, :], w1[CMP_STRIDE:])
    hid = jax.nn.silu(lo[:, :-1] + hi[:, 1:] + b1)
    return hid @ w2 + b2


def to_sel_blocks(kv):
    B, L = kv.shape[:2]
    ns = -(-L // SEL_BLOCK)
    kvp = jnp.pad(kv, ((0, 0), (0, ns * SEL_BLOCK - L), (0, 0), (0, 0), (0, 0)))
    kvb = kvp.reshape(B, ns, SEL_BLOCK, 2, N_KV, HEAD_DIM)
    return kvb[:, :, :, 0].transpose(0, 3, 1, 2, 4), kvb[:, :, :, 1].transpose(0, 3, 1, 2, 4)


def gather_blocks(blk, idx):
    return jax.vmap(jax.vmap(lambda b_, i_: b_[i_]))(blk, idx)


def nsa_attend(q_c, q_r, pos_q, kc, vc, ks_blk, vs_blk, kw, vw, pos_kw):
    B, Q = q_c.shape[:2]
    scale = HEAD_DIM ** -0.5
    qc = q_c.reshape(B, Q, N_KV, Q_PER_KV, HEAD_DIM)
    qr = q_r.reshape(B, Q, N_KV, Q_PER_KV, HEAD_DIM)
    n_cmp = kc.shape[1]
    cmp_start = jnp.arange(n_cmp, dtype=jnp.int32) * CMP_STRIDE
    cmp_end = cmp_start + CMP_BLOCK - 1
    mask_c = cmp_end[None, :] <= pos_q[:, None]
    s_c = jnp.einsum('bqgrd,bngd->bgrqn', qc, kc).astype(jnp.float32) * scale
    p_c = masked_softmax(s_c, mask_c)
    o_c = jnp.einsum('bgrqn,bngd->bqgrd', p_c.astype(vc.dtype), vc)
    n_blk = ks_blk.shape[2]
    sel_start = jnp.arange(n_blk, dtype=jnp.int32) * SEL_BLOCK
    overlap = ((cmp_start[:, None] <= sel_start[None, :] + SEL_BLOCK - 1)
               & (cmp_end[:, None] >= sel_start[None, :])).astype(jnp.float32)
    imp = jnp.einsum('bgqn,nj->bgqj', jnp.sum(p_c, axis=2), overlap)
    blk = jnp.arange(n_blk, dtype=jnp.int32)[None, :]
    jq = (pos_q // SEL_BLOCK)[:, None]
    forced = (blk == 0) | (blk == jq) | (blk == jq - 1)
    score = jnp.where(blk > jq, -jnp.inf, jnp.where(forced, jnp.inf, imp))
    top_s, top_i = lax.top_k(score, min(N_SELECT, n_blk))
    n_s = top_i.shape[-1]
    k_g = gather_blocks(ks_blk, top_i).reshape(B, N_KV, Q, n_s * SEL_BLOCK, HEAD_DIM)
    v_g = gather_blocks(vs_blk, top_i).reshape(B, N_KV, Q, n_s * SEL_BLOCK, HEAD_DIM)
    pos_sel = top_i[..., None] * SEL_BLOCK + jnp.arange(SEL_BLOCK, dtype=jnp.int32)
    mask_s = (top_s > -jnp.inf)[..., None] & (pos_sel <= pos_q[None, None, :, None, None])
    mask_s = mask_s.reshape(B, N_KV, 1, Q, n_s * SEL_BLOCK)
    s_s = jnp.einsum('bqgrd,bgqkd->bgrqk', qr, k_g).astype(jnp.float32) * scale
    p_s = masked_softmax(s_s, mask_s)
    o_s = jnp.einsum('bgrqk,bgqkd->bqgrd', p_s.astype(v_g.dtype), v_g)
    dist = pos_q[:, None] - pos_kw[None, :]
    mask_w = (pos_kw[None, :] >= 0) & (dist >= 0) & (dist < WINDOW)
    s_w = jnp.einsum('bqgrd,bkgd->bgrqk', qr, kw).astype(jnp.float32) * scale
    p_w = masked_softmax(s_w, mask_w)
    o_w = jnp.einsum('bgrqk,bkgd->bqgrd', p_w.astype(vw.dtype), vw)
    shp = (B, Q, N_HEADS, HEAD_DIM)
    return jnp.stack([o_c.reshape(shp), o_s.reshape(shp), o_w.reshape(shp)], axis=2)


def nsa_merge(o, z, gl, gate_b, w_out):
    B, S = o.shape[:2]
    g = jax.nn.sigmoid(gl + gate_b)
    y = jnp.sum(o * jax.nn.silu(z) * g[..., None], axis=2).reshape(B, S, B_QW)
    return y @ w_out


def nsa_prompt(xn, w_in, pe, w1, b1, w2, b2, gate_b, w_out):
    B, S, _ = xn.shape
    pos = jnp.arange(S, dtype=jnp.int32)
    q, q_r, kv_c, kv_s, kv_w, z, gl = nsa_project(xn, w_in, pos)
    kc = compress(kv_c[:, :, 0], pe[0], w1[0], b1[0], w2[0], b2[0])
    vc = compress(kv_c[:, :, 1], pe[1], w1[1], b1[1], w2[1], b2[1])
    ks_blk, vs_blk = to_sel_blocks(kv_s)
    kw_pad = jnp.pad(kv_w, ((0, 0), (WINDOW, 0), (0, 0), (0, 0), (0, 0)))
    band = WINDOW + NSA_Q_BLOCK

    def query_block(s0):
        qb = lax.dynamic_slice_in_dim(q, s0, NSA_Q_BLOCK, axis=1)
        qrb = lax.dynamic_slice_in_dim(q_r, s0, NSA_Q_BLOCK, axis=1)
        wb = lax.dynamic_slice_in_dim(kw_pad, s0, band, axis=1)
        pos_q = s0 + jnp.arange(NSA_Q_BLOCK, dtype=jnp.int32)
        pos_kw = s0 - WINDOW + jnp.arange(band, dtype=jnp.int32)
        return nsa_attend(qb, qrb, pos_q, kc, vc, ks_blk, vs_blk, wb[:, :, 0], wb[:, :, 1], pos_kw)

    starts = jnp.arange(S // NSA_Q_BLOCK, dtype=jnp.int32) * NSA_Q_BLOCK
    o = lax.map(query_block, starts)
    o = jnp.moveaxis(o, 0, 1).reshape(B, S, N_BRANCH, N_HEADS, HEAD_DIM)
    y = nsa_merge(o, z, gl, gate_b, w_out)
    return y, kv_c, kv_s, kv_w[:, S - min(WINDOW, S):]


def nsa_sample(xn, pool_c, pool_s, win, page_table, w_in, pe, w1, b1, w2, b2, gate_b, w_out):
    B, T, _ = xn.shape
    past = page_table.shape[1] * PAGE_SIZE
    pos = past + jnp.arange(T, dtype=jnp.int32)
    q, q_r, kv_c, kv_s, kv_w, z, gl = nsa_project(xn, w_in, pos)
    old_c = pool_c[page_table].reshape(B, past, 2, N_KV, HEAD_DIM)
    old_s = pool_s[page_table].reshape(B, past, 2, N_KV, HEAD_DIM)
    full_c = jnp.concatenate([old_c, kv_c], axis=1)
    full_s = jnp.concatenate([old_s, kv_s], axis=1)
    kc = compress(full_c[:, :, 0], pe[0], w1[0], b1[0], w2[0], b2[0])
    vc = compress(full_c[:, :, 1], pe[1], w1[1], b1[1], w2[1], b2[1])
    ks_blk, vs_blk = to_sel_blocks(full_s)
    wlen = win.shape[1]
    kvw = jnp.concatenate([win, kv_w], axis=1)
    pos_kw = past - wlen + jnp.arange(wlen + T, dtype=jnp.int32)
    o = nsa_attend(q, q_r, pos, kc, vc, ks_blk, vs_blk, kvw[:, :, 0], kvw[:, :, 1], pos_kw)
    y = nsa_merge(o, z, gl, gate_b, w_out)
    return y, kv_c, kv_s, kvw[:, kvw.shape[1] - wlen:]


def setup_inputs(seed: int = 0) -> dict:
    key = jax.random.key(seed)
    ks = jax.random.split(key, 24)
    f32 = jnp.float32

    def nrm(k, shape, s=1.0):
        return s * jax.random.normal(k, shape, f32)

    n_pages = PAST_LEN // PAGE_SIZE
    n_used = DEC_BATCH * n_pages
    n_phys = n_used + max(1, n_used // 4)
    win_len = min(WINDOW, PAST_LEN)
    page_table = jax.random.permutation(ks[5], n_phys)[:n_used].reshape(DEC_BATCH, n_pages).astype(jnp.int32)
    return {
        'x_prompt': nrm(ks[0], (BATCH, SEQ, D_MODEL)),
        'x_sample': nrm(ks[1], (DEC_BATCH, DEC_SEQ, D_MODEL)),
        'cache_cmp_kv': nrm(ks[2], (N_B_LAYERS, n_phys, PAGE_SIZE, 2, N_KV, HEAD_DIM)),
        'cache_sel_kv': nrm(ks[3], (N_B_LAYERS, n_phys, PAGE_SIZE, 2, N_KV, HEAD_DIM)),
        'state_win_kv': nrm(ks[4], (N_B_LAYERS, DEC_BATCH, win_len, 2, N_KV, HEAD_DIM)),
        'page_table': page_table,
        'norm_g': 1.0 + nrm(ks[6], (DEPTH, D_MODEL), 0.1),
        'final_norm_g': 1.0 + nrm(ks[7], (D_MODEL,), 0.1),
        'a_w_in': nrm(ks[8], (N_A_LAYERS, D_MODEL, 3 * A_WIDTH), D_MODEL ** -0.5),
        'a_ln_g': 1.0 + nrm(ks[9], (N_A_LAYERS, A_WIDTH), 0.1),
        'a_ln_b': nrm(ks[10], (N_A_LAYERS, A_WIDTH), 0.1),
        'a_w_s': nrm(ks[11], (N_A_LAYERS, A_GROUPS, A_CHUNK, A_CHUNK), 0.5 * A_CHUNK ** -0.5),
        'a_b_s': 1.0 + nrm(ks[12], (N_A_LAYERS, A_GROUPS, A_CHUNK), 0.1),
        'a_w_out': nrm(ks[13], (N_A_LAYERS, A_WIDTH, D_MODEL), A_WIDTH ** -0.5),
        'b_w_in': nrm(ks[14], (N_B_LAYERS, D_MODEL, B_IN), D_MODEL ** -0.5),
        'b_cmp_pe': nrm(ks[15], (N_B_LAYERS, 2, CMP_BLOCK, HEAD_DIM), 0.5),
        'b_cmp_w1': nrm(ks[16], (N_B_LAYERS, 2, CMP_BLOCK, HEAD_DIM, CMP_HIDDEN), (CMP_BLOCK * HEAD_DIM) ** -0.5),
        'b_cmp_b1': nrm(ks[17], (N_B_LAYERS, 2, CMP_HIDDEN), 0.1),
        'b_cmp_w2': nrm(ks[18], (N_B_LAYERS, 2, CMP_HIDDEN, HEAD_DIM), CMP_HIDDEN ** -0.5),
        'b_cmp_b2': nrm(ks[19], (N_B_LAYERS, 2, HEAD_DIM), 0.1),
        'b_gate_b': nrm(ks[20], (N_B_LAYERS, N_BRANCH, N_HEADS), 0.1),
        'b_w_out': nrm(ks[21], (N_B_LAYERS, B_QW, D_MODEL), B_QW ** -0.5),
    }


def reference(x_prompt, x_sample, cache_cmp_kv, cache_sel_kv, state_win_kv, page_table,
              norm_g, final_norm_g, a_w_in, a_ln_g, a_ln_b, a_w_s, a_b_s, a_w_out,
              b_w_in, b_cmp_pe, b_cmp_w1, b_cmp_b1, b_cmp_w2, b_cmp_b2, b_gate_b, b_w_out):
    xp, xs = x_prompt, x_sample
    chunk_v_s, cmp_p, cmp_s, sel_p, sel_s, win_p, win_s = [], [], [], [], [], [], []
    for i in range(DEPTH):
        j = i // 2
        hp = rmsnorm(xp, norm_g[i])
        hs = rmsnorm(xs, norm_g[i])
        if i % 2 == 0:
            yp, _ = chunk_mlp_mixer(hp, a_w_in[j], a_ln_g[j], a_ln_b[j], a_w_s[j], a_b_s[j], a_w_out[j])
            ys, v_new = chunk_mlp_mixer(hs, a_w_in[j], a_ln_g[j], a_ln_b[j], a_w_s[j], a_b_s[j], a_w_out[j])
            chunk_v_s.append(v_new)
        else:
            yp, c_p, s_p, w_p = nsa_prompt(hp, b_w_in[j], b_cmp_pe[j], b_cmp_w1[j], b_cmp_b1[j],
                                           b_cmp_w2[j], b_cmp_b2[j], b_gate_b[j], b_w_out[j])
            ys, c_s, s_s, w_s = nsa_sample(hs, cache_cmp_kv[j], cache_sel_kv[j], state_win_kv[j], page_table,
                                           b_w_in[j], b_cmp_pe[j], b_cmp_w1[j], b_cmp_b1[j],
                                           b_cmp_w2[j], b_cmp_b2[j], b_gate_b[j], b_w_out[j])
            cmp_p.append(c_p)
            cmp_s.append(c_s)
            sel_p.append(s_p)
            sel_s.append(s_s)
            win_p.append(w_p)
            win_s.append(w_s)
        xp = xp + yp
        xs = xs + ys
    y_prompt = rmsnorm(xp, final_norm_g)
    y_sample = rmsnorm(xs, final_norm_g)
    return (y_prompt, y_sample, jnp.stack(cmp_p), jnp.stack(cmp_s), jnp.stack(sel_p), jnp.stack(sel_s),
            jnp.stack(win_p), jnp.stack(win_s), jnp.stack(chunk_v_s))
```

```python
import math
import numpy as np
import ml_dtypes
import concourse.bass as bass
import concourse.mybir as mybir
from concourse.bass_utils import run_bass_kernel_spmd

F32 = mybir.dt.float32
BF16 = mybir.dt.bfloat16
I32 = mybir.dt.int32
AF = mybir.ActivationFunctionType
ALU = mybir.AluOpType
AX = mybir.AxisListType

D = 1024
SEQ = 4096
T = 256
NSUB = 2
NTILES = SEQ // T
AW = 2048
NH = 16
HD = 64
NKV = 4
B_IN = 5680
PAST = 16384
NPAGE = 128
NEG = -30000.0
EPS = 1e-6
NCORES = 8
SPC = 4


class Buf:
    __slots__ = ("name", "wr", "rd", "excl")

    def __init__(self, name, excl=False):
        self.name = name
        self.wr = None
        self.rd = []
        self.excl = excl


class Sync:
    def __init__(self, nc, same_engine=True, n_dma_sems=40):
        self.nc = nc
        self.eng = {"pe": nc.tensor, "act": nc.scalar, "dve": nc.vector, "pool": nc.gpsimd, "sp": nc.sync}
        self.sem = {e: nc.alloc_semaphore("s_" + e) for e in self.eng}
        self.cnt = {e: 0 for e in self.eng}
        self.pend = {e: False for e in self.eng}
        self.waited = {e: {} for e in self.eng}
        self.same_engine = same_engine
        self.dma_sems = [nc.alloc_semaphore("dq%d" % i) for i in range(n_dma_sems)]
        self.dma_val = [0] * n_dma_sems
        self.dma_rr = 0
        self.out_events = []
        self.n_ins = 0
        self.n_wait = 0

    def _wait(self, e, ev):
        sem, val, src = ev
        if src == e and (e == "pe" or not self.same_engine):
            return
        key = id(sem)
        if self.waited[e].get(key, 0) >= val:
            return
        self.eng[e].wait_ge(sem, val)
        self.n_wait += 1
        self.waited[e][key] = val

    def _deps(self, e, R, W):
        for b in R:
            if b.wr is not None:
                self._wait(e, b.wr)
            if b.excl:
                for ev in b.rd:
                    self._wait(e, ev)
        for b in W:
            if b.wr is not None:
                self._wait(e, b.wr)
            for ev in b.rd:
                self._wait(e, ev)

    def _record(self, ev, R, W):
        for b in R:
            if b.excl:
                b.wr = ev
                b.rd = []
            else:
                b.rd.append(ev)
                if len(b.rd) > 64:
                    b.rd = b.rd[-64:]
        for b in W:
            b.wr = ev
            b.rd = []

    def op(self, e, fn, R=(), W=(), signal=True):
        self._deps(e, R, W)
        ins = fn()
        self.n_ins += 1
        if signal:
            self.cnt[e] += 1
            ins.then_inc(self.sem[e], 1)
            ev = (self.sem[e], self.cnt[e], e)
            self.pend[e] = False
        else:
            ev = (self.sem[e], self.cnt[e] + 1, e)
            self.pend[e] = True
        self._record(ev, R, W)
        return ins

    def dma(self, q, fn, R=(), W=(), is_output=False):
        k = self.dma_rr
        self.dma_rr = (self.dma_rr + 1) % len(self.dma_sems)
        sem = self.dma_sems[k]
        if self.dma_val[k] > 0:
            self._wait(q, (sem, self.dma_val[k], "dma"))
        self._deps(q, R, W)
        ins = fn()
        self.n_ins += 1
        self.dma_val[k] += 16
        ins.then_inc(sem, 16)
        ev = (sem, self.dma_val[k], "dma")
        self._record(ev, R, W)
        if is_output:
            self.out_events.append(ev)
        return ins

    def barrier(self):
        evs = []
        for e in self.eng:
            assert not self.pend[e], "pending unsignalled instruction on " + e
            if self.cnt[e] > 0:
                evs.append((self.sem[e], self.cnt[e], e))
        for k, sem in enumerate(self.dma_sems):
            if self.dma_val[k] > 0:
                evs.append((sem, self.dma_val[k], "dma"))
        for e in self.eng:
            for ev in evs:
                if ev[2] != e:
                    self._wait(e, ev)

    def finish(self, e="sp"):
        for ev in self.out_events:
            self._wait(e, ev)


def _rope_tables():
    half = 8
    freqs = np.exp(-math.log(500000.0) * np.arange(half, dtype=np.float32) * np.float32(2.0 / 16)).astype(np.float32)
    pos = np.arange(SEQ, dtype=np.float32)
    ang = (pos[:, None] * freqs[None, :]).astype(np.float32)
    cos = np.cos(ang).astype(np.float32)
    sin = np.sin(ang).astype(np.float32)
    angs = (np.float32(PAST) * freqs).astype(np.float32)
    return cos, sin, np.cos(angs).astype(np.float32), np.sin(angs).astype(np.float32)


_CONST_CACHE = {}


def host_consts():
    if _CONST_CACHE:
        return _CONST_CACHE
    bf = ml_dtypes.bfloat16
    c = {}
    c["c_ident"] = np.eye(128, dtype=np.float32).astype(bf)
    kk = np.arange(128)[:, None]
    qq = np.arange(128)[None, :]
    c["c_tri"] = np.where(kk <= qq, 0.0, NEG).astype(np.float32).astype(bf)
    c["c_atri"] = np.where(kk > qq, 0.0, NEG).astype(np.float32).astype(bf)
    ps = np.zeros((128, 128), np.float32)
    for m in range(128):
        d = m % 64
        if d < 8:
            ps[m + 8, m] = 1.0
        elif d < 16:
            ps[m - 8, m] = 1.0
    c["c_pswap"] = ps.astype(bf)
    cos, sin, cos_s, sin_s = _rope_tables()
    cosF = np.ones((128, SEQ), np.float32)
    sinF = np.zeros((128, SEQ), np.float32)
    for hh in range(2):
        cosF[hh * 64:hh * 64 + 8] = cos.T
        cosF[hh * 64 + 8:hh * 64 + 16] = cos.T
        sinF[hh * 64:hh * 64 + 8] = -sin.T
        sinF[hh * 64 + 8:hh * 64 + 16] = sin.T
    c["c_cosF"] = cosF
    c["c_sinF"] = sinF
    c["c_costm4"] = np.ascontiguousarray(np.tile(cos[:, None, :], (1, 4, 1)))
    c["c_sintm4"] = np.ascontiguousarray(np.tile(sin[:, None, :], (1, 4, 1)))
    c["c_ropes16"] = np.ascontiguousarray(np.stack([np.tile(cos_s[None, None], (SPC, 16, 1)), np.tile(sin_s[None, None], (SPC, 16, 1))], 0).astype(np.float32))
    ind = np.zeros((64, SEQ), np.float32)
    for j in range(64):
        ind[j, j * 64:(j + 1) * 64] = -NEG
    c["c_ind"] = ind.astype(bf)
    fb = np.zeros((32, 128, 64), np.float32)
    for qt in range(32):
        t = qt * 128 + np.arange(128)
        jq = t // 64
        jj = np.arange(64)[None, :]
        forced = (jj == 0) | (jj == jq[:, None]) | (jj == jq[:, None] - 1)
        fb[qt] = np.where(jj > jq[:, None], -10.0, np.where(forced, 10.0, 0.0))
    c["c_force"] = fb
    cb = np.zeros((32, 2, 128, 128), np.float32)
    for qt in range(32):
        for nt in range(2):
            n = nt * 128 + np.arange(128)[:, None]
            t = qt * 128 + np.arange(128)[None, :]
            cb[qt, nt] = np.where(16 * n + 31 <= t, 0.0, NEG)
    c["c_cbias"] = cb.astype(bf)
    ov = np.zeros((2, 128, 65), np.float32)
    for nt in range(2):
        for p in range(128):
            n = nt * 128 + p
            for j in range(64):
                if 16 * n <= 64 * j + 63 and 16 * n + 31 >= 64 * j:
                    ov[nt, p, j] = 1.0
            ov[nt, p, 64] = 1.0
    c["c_ovaug"] = ov.astype(bf)
    E = np.zeros((48, 48, 64), np.float32)
    for k in range(48):
        E[k, k, :] = 1.0
    c["c_esel"] = E.astype(bf)
    ovs = np.zeros((8, 128, 258), np.float32)
    for nt in range(8):
        for p in range(128):
            n = nt * 128 + p
            if n >= 1023:
                continue
            j0 = max(0, (16 * n - 63 + 63) // 64 - 1)
            for j in range(j0, min(257, j0 + 4)):
                if 16 * n <= 64 * j + 63 and 16 * n + 31 >= 64 * j:
                    ovs[nt, p, j] = 1.0
            ovs[nt, p, 257] = 1.0
    c["c_ovaug_s"] = ovs.astype(bf)
    fs = np.zeros((4, 256), np.float32)
    fs[:, 0] = 10.0
    fs[:, 255] = 10.0
    c["c_force_s"] = fs
    gs = np.zeros((16, 4), np.float32)
    for h in range(16):
        gs[h, h // 4] = 1.0
    c["c_grpsel"] = gs
    oh = np.zeros((4, 4, 128), np.float32)
    for g in range(4):
        oh[g, g, :] = 1.0
    c["c_onehot4"] = oh
    _CONST_CACHE.update(c)
    return c


CONST_SPECS = {
    "c_ident": ([128, 128], BF16), "c_tri": ([128, 128], BF16), "c_atri": ([128, 128], BF16),
    "c_pswap": ([128, 128], BF16), "c_cosF": ([128, SEQ], F32), "c_sinF": ([128, SEQ], F32),
    "c_costm4": ([SEQ, 4, 8], F32), "c_sintm4": ([SEQ, 4, 8], F32), "c_ropes16": ([2, SPC, 16, 8], F32),
    "c_ind": ([64, SEQ], BF16), "c_force": ([32, 128, 64], F32), "c_cbias": ([32, 2, 128, 128], BF16),
    "c_ovaug": ([2, 128, 65], BF16), "c_esel": ([48, 48, 64], BF16), "c_ovaug_s": ([8, 128, 258], BF16),
    "c_force_s": ([4, 256], F32), "c_grpsel": ([16, 4], F32), "c_onehot4": ([4, 4, 128], F32),
}


class KB:
    def __init__(self, n_phys, ntiles=NTILES, do_sample=True):
        self.n_phys = n_phys
        self.ntiles = ntiles
        self.do_sample = do_sample
        self.nc = nc = bass.Bass("TRN2", target_bir_lowering=False)
        self.S = Sync(nc)
        self.sb_bytes = 0
        self._uid = 0
        di = lambda n, s, d: nc.dram_tensor(n, list(s), d, kind="ExternalInput").ap()
        do = lambda n, s, d: nc.dram_tensor(n, list(s), d, kind="ExternalOutput").ap()
        self.x_p = di("x_prompt", [SEQ, D], F32)
        self.x_s = di("x_sample", [SPC, D], F32)
        self.cache_c = di("cache_cmp_kv", [n_phys * 128, 512], F32)
        self.cache_s = di("cache_sel_kv", [n_phys * 128, 512], F32)
        self.state_w = di("state_win_kv", [SPC, 512, 512], F32)
        self.page_t = di("page_table", [SPC, NPAGE], I32)
        self.norm_g = di("norm_g", [2, D], F32)
        self.final_g = di("final_norm_g", [D], F32)
        self.a_w_in = di("a_w_in", [D, 3 * AW], F32)
        self.a_ln_g = di("a_ln_g", [AW], F32)
        self.a_ln_b = di("a_ln_b", [AW], F32)
        self.a_w_s = di("a_w_s", [16, 128, 128], F32)
        self.a_b_s = di("a_b_s", [16, 128], F32)
        self.a_w_out = di("a_w_out", [AW, D], F32)
        self.b_w_in = di("b_w_in", [D, B_IN], F32)
        self.b_pe = di("b_cmp_pe", [2, 32, 64], F32)
        self.b_w1 = di("b_cmp_w1", [2, 32, 64, 64], F32)
        self.b_b1 = di("b_cmp_b1", [2, 64], F32)
        self.b_w2 = di("b_cmp_w2", [2, 64, 64], F32)
        self.b_b2 = di("b_cmp_b2", [2, 64], F32)
        self.b_gate = di("b_gate_b", [48], F32)
        self.b_w_out = di("b_w_out", [D, D], F32)
        self.cst = {k: di(k, s, d) for k, (s, d) in CONST_SPECS.items()}
        self.y_p = do("y_prompt", [SEQ, D], F32)
        self.y_s = do("y_sample", [SPC, D], F32)
        self.o_cmp_p = do("cmp_kv_prompt", [SEQ, 512], F32)
        self.o_cmp_s = do("cmp_kv_sample", [SPC, 512], F32)
        self.o_sel_p = do("sel_kv_prompt", [SEQ, 512], F32)
        self.o_sel_s = do("sel_kv_sample", [SPC, 512], F32)
        self.o_win_p = do("win_kv_prompt", [512, 512], F32)
        self.o_win_s = do("win_kv_sample", [SPC, 512, 512], F32)
        self.o_chv = do("chunk_v_sample", [SPC, AW], F32)
        dt = lambda n, s: nc.dram_tensor(n, list(s), BF16).ap()
        self.wA_in = dt("wA_in", [12, 128, 8, 512])
        self.wA_out = dt("wA_out", [8, 128, 4, 512])
        self.wB_q = dt("wB_q", [2, 128, 8, 512])
        self.wB_kv = dt("wB_kv", [3, 128, 8, 512])
        self.wB_z = dt("wB_z", [6, 128, 8, 512])
        self.wB_gl = dt("wB_gl", [128, 8, 48])
        self.wB_out = dt("wB_out", [4, 64, 8, 512])
        self.w1bd = dt("w1bd", [2, 128, 32, 128])
        self.b_wscr = {k: Buf("wscr_" + k) for k in ["A_in", "A_out", "B_q", "B_kv", "B_z", "B_gl", "B_out", "w1bd"]}

    def sb(self, name, shape, dtype, excl=False):
        t = self.nc.alloc_sbuf_tensor(name, list(shape), dtype)
        n = 1
        for s in shape[1:]:
            n *= s
        self.sb_bytes += n * (2 if dtype == BF16 else 4)
        return t, Buf(name, excl)

    def nb(self, name):
        self._uid += 1
        return Buf("%s_%d" % (name, self._uid))

    def pe(self, fn, R=(), W=(), signal=True):
        return self.S.op("pe", fn, R, W, signal)

    def act(self, fn, R=(), W=()):
        return self.S.op("act", fn, R, W)

    def dve(self, fn, R=(), W=()):
        return self.S.op("dve", fn, R, W)

    def pool(self, fn, R=(), W=()):
        return self.S.op("pool", fn, R, W)

    def dma(self, fn, R=(), W=(), q="sp", out=False):
        return self.S.dma(q, fn, R, W, is_output=out)

    def ld(self, out_ap, in_ap, R=(), W=(), q="sp", slow=False):
        nc = self.nc
        eng = nc.sync if q == "sp" else nc.gpsimd
        if slow:
            return self.dma(lambda: eng.dma_start(out=out_ap, in_=in_ap, allow_slow_non_contiguous=True), R, W, q)
        return self.dma(lambda: eng.dma_start(out=out_ap, in_=in_ap), R, W, q)

    def st(self, out_ap, in_ap, R=(), W=(), q="sp"):
        nc = self.nc
        eng = nc.sync if q == "sp" else nc.gpsimd
        return self.dma(lambda: eng.dma_start(out=out_ap, in_=in_ap), R, W, q, out=True)

    def alloc(self):
        nc = self.nc
        sb = self.sb
        self.KA, self.b_KA = sb("KA", [128, 4, SEQ], BF16)
        self.VS, self.b_VS = sb("VS", [128, 32, 4, 128], BF16)
        self.KW, self.b_KW = sb("KW", [128, 4, 6 * 128], BF16)
        self.VW, self.b_VW = sb("VW", [128, 6, 4, 128], BF16)
        self.KCT, self.b_KCT = sb("KCT", [128, 4, 256], BF16)
        self.VC, self.b_VC = sb("VC", [128, 2, 4, 128], BF16)
        self.KRAW, self.b_KRAW = sb("KRAW", [128, 4, 16 + T], BF16)
        self.ident, self.b_ident = sb("ident", [128, 128], BF16)
        self.tri, self.b_tri = sb("tri", [128, 128], BF16)
        self.atri, self.b_atri = sb("atri", [128, 128], BF16)
        self.pswap, self.b_pswap = sb("pswap", [128, 128], BF16)
        self.wmT, self.b_wmT = sb("wmT", [128, 16, 128], BF16)
        self.Bmix, self.b_Bmix = sb("Bmix", [128, 16, 128], F32)
        self.lng, self.b_lng = sb("lng", [128, 16], F32)
        self.W2K, self.b_W2K = sb("W2K", [128, 2, 128], BF16)
        self.W2V, self.b_W2V = sb("W2V", [128, 128], BF16)
        self.b2K, self.b_b2K = sb("b2K", [128, 1], F32)
        self.b2Vrow, self.b_b2V = sb("b2Vrow", [1, 128], BF16)
        self.ones_row, self.b_ones = sb("ones_row", [1, 128], BF16)
        self.bias1, self.b_bias1 = sb("bias1", [128, 2], F32)
        self.esel, self.b_esel = sb("esel", [48, 48, 64], BF16)
        self.ovaug, self.b_ovaug = sb("ovaug", [128, 2, 65], BF16)
        self.fgbc, self.b_fgbc = sb("fgbc", [128, D], F32)
        self.gateb, self.b_gateb = sb("gateb", [48, 1], F32)
        self.gcol, self.b_gcol = sb("gcol", [128, 2, 8], F32)
        self.wgl, self.b_wgl = sb("wgl", [128, 8, 48], BF16)
        self.x_tm, self.b_x = sb("x_tm", [128, NSUB, D], F32)
        self.xhat, self.b_xhat = sb("xhat", [128, NSUB, D], BF16)
        self.xnT, self.b_xnT = sb("xnT", [128, 8, T], BF16)
        self.wst = []
        for i in range(2):
            t, b = sb("wst%d" % i, [128, 8, 512], BF16)
            self.wst.append((t, b))
        self.wst_i = 0
        self.st4, self.b_st4 = sb("st4", [128, 16], F32)
        self.AH_N = 19200
        self.AF_N = 7000
        self.arenaH, _ = sb("arenaH", [128, self.AH_N], BF16)
        self.arenaF, _ = sb("arenaF", [128, self.AF_N], F32)
        self.pm = []
        for i in range(3):
            self.pm.append((nc.alloc_psum_tensor("pm%d" % i, [128, 512], F32), Buf("pm%d" % i, True)))
        self.pm_i = 0
        self.ptr = []
        for i in range(2):
            self.ptr.append((nc.alloc_psum_tensor("ptr%d" % i, [128, 1024], BF16), Buf("ptr%d" % i, True)))
        self.ptr_i = 0
        self.po = []
        for i in range(2):
            self.po.append((nc.alloc_psum_tensor("po%d" % i, [128, 512], F32), Buf("po%d" % i, True)))
        self.px = (nc.alloc_psum_tensor("px", [128, 512], F32), Buf("px", True))

    def next_pm(self):
        r = self.pm[self.pm_i]
        self.pm_i = (self.pm_i + 1) % len(self.pm)
        return r

    def next_ptr(self):
        r = self.ptr[self.ptr_i]
        self.ptr_i = (self.ptr_i + 1) % len(self.ptr)
        return r

    def next_wst(self):
        r = self.wst[self.wst_i]
        self.wst_i = (self.wst_i + 1) % len(self.wst)
        return r

    class Carver:
        def __init__(self, kb):
            self.kb = kb
            self.h = 0
            self.f = 0

        def H(self, name, shape):
            n = 1
            for s in shape[1:]:
                n *= s
            assert self.h + n <= self.kb.AH_N, ("arenaH overflow", name, self.h + n)
            ap = self.kb.arenaH[0:shape[0], self.h:self.h + n]
            self.h += n
            return self._shape(ap, shape), self.kb.nb(name)

        def F(self, name, shape):
            n = 1
            for s in shape[1:]:
                n *= s
            assert self.f + n <= self.kb.AF_N, ("arenaF overflow", name, self.f + n)
            ap = self.kb.arenaF[0:shape[0], self.f:self.f + n]
            self.f += n
            return self._shape(ap, shape), self.kb.nb(name)

        @staticmethod
        def _shape(ap, shape):
            if len(shape) == 2:
                return ap
            if len(shape) == 3:
                return ap.rearrange("p (a b) -> p a b", b=shape[2])
            if len(shape) == 4:
                return ap.rearrange("p (a b c) -> p a b c", b=shape[2], c=shape[3])
            if len(shape) == 5:
                return ap.rearrange("p (a b c d) -> p a b c d", b=shape[2], c=shape[3], d=shape[4])
            raise ValueError(shape)

    def prologue(self):
        nc = self.nc
        cv = KB.Carver(self)
        st32 = [cv.F("st32_0", [128, 2048]), (self.x_tm[:].rearrange("p a b -> p (a b)"), self.b_x)]
        st16 = [cv.H("st16_%d" % i, [128, 2048]) for i in range(2)]
        c = self.cst
        self.ld(self.ident[:], c["c_ident"][:, :], W=[self.b_ident])
        self.ld(self.tri[:], c["c_tri"][:, :], W=[self.b_tri])
        self.ld(self.atri[:], c["c_atri"][:, :], W=[self.b_atri])
        self.ld(self.pswap[:], c["c_pswap"][:, :], W=[self.b_pswap])
        self.ld(self.esel[:], c["c_esel"][:, :, :], W=[self.b_esel])
        self.ld(self.ovaug[:], c["c_ovaug"].rearrange("n p c -> p n c"), W=[self.b_ovaug])
        self.ld(self.fgbc[:], self.final_g.partition_broadcast(128), W=[self.b_fgbc])
        self.ld(self.gateb[:], self.b_gate.rearrange("(p o) -> p o", o=1), W=[self.b_gateb])
        self.ld(self.gcol[:], self.norm_g.rearrange("l (kc p) -> p l kc", p=128), W=[self.b_gcol], slow=True)
        self.ld(self.lng[:], self.a_ln_g.rearrange("(g p) -> p g", p=128), W=[self.b_lng], slow=True)
        for g in range(4):
            self.ld(self.KA[64:128, g, :], c["c_ind"][:, :], W=[self.b_KA])
        self.pool(lambda: nc.gpsimd.memset(self.VS[:], 1.0), W=[self.b_VS])
        self.pool(lambda: nc.gpsimd.memset(self.VW[:], 1.0), W=[self.b_VW])
        self.pool(lambda: nc.gpsimd.memset(self.VC[:], 1.0), W=[self.b_VC])
        self.pool(lambda: nc.gpsimd.memset(self.VC[:, :, :, 0:64], 0.0), W=[self.b_VC])
        self.pool(lambda: nc.gpsimd.memset(self.KCT[:], 0.0), W=[self.b_KCT])
        self.pool(lambda: nc.gpsimd.memset(self.KRAW[:], 0.0), W=[self.b_KRAW])
        self.pool(lambda: nc.gpsimd.memset(self.KW[:], 0.0), W=[self.b_KW])
        self.pool(lambda: nc.gpsimd.memset(self.KA[0:64, :, :], 0.0), W=[self.b_KA])
        self.pool(lambda: nc.gpsimd.memset(self.ones_row[:], 1.0), W=[self.b_ones])

        self._cv_i = 0

        def conv(src, dst, p, a, b, scale=None, wbuf=None, pre=None):
            i = self._cv_i
            self._cv_i += 1
            (s32, b32), (s16, b16) = st32[i % 2], st16[i % 2]
            v32 = s32[0:p, 0:a * b].rearrange("p (a b) -> p a b", b=b)
            v16 = s16[0:p, 0:a * b].rearrange("p (a b) -> p a b", b=b)
            if pre is not None:
                pre(s32, b32)
            else:
                self.ld(v32, src, W=[b32])
            f32 = s32[0:p, 0:a * b]
            f16 = s16[0:p, 0:a * b]
            if scale is not None:
                self.dve(lambda: nc.vector.tensor_scalar(out=f16, in0=f32, scalar1=scale, scalar2=None, op0=ALU.mult),
                         R=[b32, self.b_gcol], W=[b16])
            elif i % 2 == 0:
                self.dve(lambda: nc.vector.tensor_copy(out=f16, in_=f32), R=[b32], W=[b16])
            else:
                self.act(lambda: nc.scalar.copy(out=f16, in_=f32), R=[b32], W=[b16])
            self.ld(dst, v16, R=[b16], q="pool")

        for kc in range(8):
            rows = slice(kc * 128, (kc + 1) * 128)
            sc0 = self.gcol[:, 0, kc:kc + 1]
            sc1 = self.gcol[:, 1, kc:kc + 1]
            for (c0, s0) in ((AW, 0), (0, 4), (2 * AW, 8)):
                conv(self.a_w_in[rows, c0:c0 + 2048].rearrange("p (a b) -> p a b", b=512),
                     self.wA_in[s0:s0 + 4, :, kc, :].rearrange("s p c -> p s c"), 128, 4, 512, sc0, self.b_wscr["A_in"])
            conv(self.b_w_in[rows, 0:1024].rearrange("p (a b) -> p a b", b=512),
                 self.wB_q[0:2, :, kc, :].rearrange("s p c -> p s c"), 128, 2, 512, sc1, self.b_wscr["B_q"])
            conv(self.b_w_in[rows, 1024:2560].rearrange("p (a b) -> p a b", b=512),
                 self.wB_kv[0:3, :, kc, :].rearrange("s p c -> p s c"), 128, 3, 512, sc1, self.b_wscr["B_kv"])
            for hz in range(2):
                conv(self.b_w_in[rows, 2560 + hz * 1536:2560 + (hz + 1) * 1536].rearrange("p (a b) -> p a b", b=512),
                     self.wB_z[hz * 3:hz * 3 + 3, :, kc, :].rearrange("s p c -> p s c"), 128, 3, 512, sc1, self.b_wscr["B_z"])
            conv(self.b_w_in[rows, 5632:5680].rearrange("p (a b) -> p a b", b=48),
                 self.wB_gl[:, kc:kc + 1, :], 128, 1, 48, sc1, self.b_wscr["B_gl"])
        for fc in range(16):
            conv(self.a_w_out[fc * 128:(fc + 1) * 128, :].rearrange("p (a b) -> p a b", b=512),
                 self.wA_out[2 * (fc // 4):2 * (fc // 4) + 2, :, fc % 4, :].rearrange("s p c -> p s c"), 128, 2, 512,
                 None, self.b_wscr["A_out"])
        for cb in range(2):
            for hh in range(2):
                for q in range(2):
                    h0 = hh * 8 + q * 4
                    conv(self.b_w_out[h0 * 64:(h0 + 4) * 64, cb * 512:(cb + 1) * 512].rearrange("(h d) c -> d h c", d=64),
                         self.wB_out[cb * 2 + hh, :, q * 4:q * 4 + 4, :], 64, 4, 512, None, self.b_wscr["B_out"])
        for kv in range(2):
            for jh in range(2):
                def pre(s32, b32, kv=kv, jh=jh):
                    v = s32[:, :].rearrange("p (a b) -> p a b", b=128)
                    self.dve(lambda: nc.vector.memset(s32[:, :], 0.0), W=[b32])
                    src = self.b_w1[kv, jh * 16:(jh + 1) * 16].rearrange("j c h -> c j h")
                    self.ld(v[0:64, :, 0:64], src, W=[b32])
                    self.ld(v[64:128, :, 64:128], src, W=[b32])
                conv(None, self.w1bd[kv, :, jh * 16:(jh + 1) * 16, :], 128, 16, 128, None, self.b_wscr["w1bd"], pre=pre)

        wtmp, b_wtmp = st32[0]
        lnb2, b_lnb2 = cv.F("lnb2", [2, 16, 128])
        rs2, b_rs2 = cv.F("rs2", [2, 16, 128])
        wbf, b_wbf = cv.H("wbf", [128, 128])
        onesc, b_onesc = cv.H("onesc", [128, 1])
        self.pool(lambda: nc.gpsimd.memset(onesc, 1.0), W=[b_onesc])
        self.pool(lambda: nc.gpsimd.memset(lnb2[:, :, :], 1.0), W=[b_lnb2])
        self.ld(lnb2[0:1, :, :], self.a_ln_b.rearrange("(o g c) -> o g c", o=1, c=128), W=[b_lnb2])
        self.ld(rs2[1:2, :, :], self.a_b_s.rearrange("(o g) t -> o g t", o=1), W=[b_rs2])
        for g in range(16):
            wv = wtmp[:, 0:128]
            self.ld(wv, self.a_w_s[g], W=[b_wtmp])
            self.pool(lambda: nc.gpsimd.affine_select(out=wv, in_=wv, pattern=[[-1, 128]], compare_op=ALU.is_ge,
                                                      fill=0.0, base=0, channel_multiplier=1), R=[b_wtmp], W=[b_wtmp])
            self.dve(lambda: nc.vector.tensor_copy(out=wbf, in_=wv), R=[b_wtmp], W=[b_wbf])
            pt, b_pt = self.next_ptr()
            self.pe(lambda: nc.tensor.transpose(out=pt[:, 0:128], in_=wbf, identity=self.ident[:]),
                    R=[b_wbf, self.b_ident], W=[b_pt])
            self.act(lambda: nc.scalar.copy(out=self.wmT[:, g, :], in_=pt[:, 0:128]), R=[b_pt], W=[self.b_wmT])
            pm, b_pm = self.next_pm()
            self.pe(lambda: nc.tensor.matmul(pm[0:1, 0:128], lhsT=onesc, rhs=self.wmT[:, g, :], start=True, stop=True),
                    R=[b_onesc, self.b_wmT], W=[b_pm])
            self.act(lambda: nc.scalar.copy(out=rs2[0:1, g, :], in_=pm[0:1, 0:128]), R=[b_pm], W=[b_rs2])
        for g in range(16):
            pm, b_pm = self.next_pm()
            self.pe(lambda: nc.tensor.matmul(pm[:, 0:128], lhsT=lnb2[:, g, :], rhs=rs2[:, g, :], start=True, stop=True),
                    R=[b_lnb2, b_rs2], W=[b_pm])
            self.act(lambda: nc.scalar.copy(out=self.Bmix[:, g, :], in_=pm[:, 0:128]), R=[b_pm], W=[self.b_Bmix])

        w2f, b_w2f = cv.F("w2f", [128, 2, 128])
        self.dve(lambda: nc.vector.memset(w2f[:, :, :], 0.0), W=[b_w2f])
        for a in range(2):
            for hcol in range(2):
                self.ld(w2f[a * 64:(a + 1) * 64, a, hcol * 64:(hcol + 1) * 64], self.b_w2[0], W=[b_w2f])
        self.dve(lambda: nc.vector.tensor_copy(out=self.W2K[:], in_=w2f[:, :, :]), R=[b_w2f], W=[self.b_W2K])
        w2v, b_w2v = cv.F("w2v", [128, 128])
        self.dve(lambda: nc.vector.memset(w2v, 0.0), W=[b_w2v])
        for a in range(2):
            self.ld(w2v[a * 64:(a + 1) * 64, a * 64:(a + 1) * 64], self.b_w2[1], W=[b_w2v])
        self.dve(lambda: nc.vector.tensor_copy(out=self.W2V[:], in_=w2v), R=[b_w2v], W=[self.b_W2V])
        for a in range(2):
            self.ld(self.b2K[a * 64:(a + 1) * 64, :], self.b_b2[0].rearrange("(p o) -> p o", o=1), W=[self.b_b2K])
        b2vf, b_b2vf = cv.F("b2vf", [1, 128])
        for a in range(2):
            self.ld(b2vf[0:1, a * 64:(a + 1) * 64], self.b_b2[1].rearrange("(o c) -> o c", o=1), W=[b_b2vf])
        self.dve(lambda: nc.vector.tensor_copy(out=self.b2Vrow[:], in_=b2vf), R=[b_b2vf], W=[self.b_b2V])
        pef, b_pef = cv.F("pef", [128, 2, 32])
        peb, b_peb = cv.H("peb", [128, 2, 32])
        b1f, b_b1f = cv.F("b1f", [128, 2])
        for a in range(2):
            self.ld(pef[a * 64:(a + 1) * 64, :, :], self.b_pe.rearrange("k j c -> c k j"), W=[b_pef], slow=True)
            self.ld(b1f[a * 64:(a + 1) * 64, :], self.b_b1.rearrange("k h -> h k"), W=[b_b1f], slow=True)
        self.dve(lambda: nc.vector.tensor_copy(out=peb[:, :, :], in_=pef[:, :, :]), R=[b_pef], W=[b_peb])
        self.S.barrier()
        self.ld(self.wgl[:], self.wB_gl[:, :, :], W=[self.b_wgl])
        for kv in range(2):
            wt, b_wt = self.next_wst()
            w1v = wt[:].rearrange("p a b -> p (a b)").rearrange("p (j c) -> p j c", c=128)
            self.ld(w1v, self.w1bd[kv], R=[self.b_wscr["w1bd"]], W=[b_wt])
            pm, b_pm = self.next_pm()
            for j in range(32):
                self.pe(lambda: nc.tensor.matmul(pm[:, 0:1], lhsT=w1v[:, j, :], rhs=peb[:, kv, j:j + 1],
                                                 start=(j == 0), stop=(j == 31)), R=[b_wt, b_peb], W=[b_pm], signal=(j == 31))
            self.dve(lambda: nc.vector.tensor_tensor(out=self.bias1[:, kv:kv + 1], in0=pm[:, 0:1], in1=b1f[:, kv:kv + 1], op=ALU.add),
                     R=[b_pm, b_b1f], W=[self.b_bias1])
        self.S.barrier()

    def rms_T(self, npart=128, ncol=T):
        nc = self.nc
        nsub = (ncol + 127) // 128
        st4, b_st4 = self.st4, self.b_st4
        for s in range(nsub):
            self.act(lambda: nc.scalar.activation(out=self.xhat[0:npart, s, :], in_=self.x_tm[0:npart, s, :], func=AF.Square,
                                                  accum_out=st4[0:npart, s:s + 1]), R=[self.b_x], W=[self.b_xhat, b_st4])
        self.dve(lambda: nc.vector.tensor_scalar(out=st4[0:npart, 2:2 + nsub], in0=st4[0:npart, 0:nsub], scalar1=1.0 / D, scalar2=EPS,
                                                 op0=ALU.mult, op1=ALU.add), R=[b_st4], W=[b_st4])
        self.act(lambda: nc.scalar.activation(out=st4[0:npart, 2:2 + nsub], in_=st4[0:npart, 2:2 + nsub], func=AF.Sqrt), R=[b_st4], W=[b_st4])
        self.dve(lambda: nc.vector.reciprocal(out=st4[0:npart, 4:4 + nsub], in_=st4[0:npart, 2:2 + nsub]), R=[b_st4], W=[b_st4])

    def xhat_T(self, npart=128, ncol=T):
        nc = self.nc
        nsub = (ncol + 127) // 128
        st4, b_st4 = self.st4, self.b_st4
        for s in range(nsub):
            self.act(lambda: nc.scalar.mul(out=self.xhat[0:npart, s, :], in_=self.x_tm[0:npart, s, :], mul=st4[0:npart, 4 + s:5 + s]),
                     R=[self.b_x, b_st4], W=[self.b_xhat])
        w = min(npart, 128)
        for kc in range(8):
            pt, b_pt = self.next_ptr()
            for s in range(nsub):
                self.pe(lambda: nc.tensor.transpose(out=pt[:, s * 128:s * 128 + w], in_=self.xhat[0:npart, s, kc * 128:(kc + 1) * 128],
                                                    identity=self.ident[0:npart, 0:npart]), R=[self.b_xhat, self.b_ident], W=[b_pt], signal=(s == nsub - 1))
            if kc % 2 == 0:
                self.act(lambda: nc.scalar.copy(out=self.xnT[:, kc, 0:ncol], in_=pt[:, 0:ncol]), R=[b_pt], W=[self.b_xnT])
            else:
                self.dve(lambda: nc.vector.tensor_copy(out=self.xnT[:, kc, 0:ncol], in_=pt[:, 0:ncol]), R=[b_pt], W=[self.b_xnT])

    def layer_a(self, i):
        nc = self.nc
        cv = KB.Carver(self)
        gv, b_gv = cv.H("gv", [128, NSUB, AW])
        gu = [cv.H("gu%d" % k, [128, 4, T]) for k in range(2)]
        sz = [cv.H("sz%d" % k, [128, 4, T]) for k in range(2)]
        gs = [cv.H("gs%d" % k, [128, 4, T]) for k in range(2)]
        yT, b_yT = cv.H("yT", [128, 16, T])
        tmpf = [cv.F("tmpf%d" % k, [128, 128]) for k in range(2)]
        stats, b_stats = cv.F("stats", [128, NSUB, 4, 6])
        mv, b_mv = cv.F("mv", [128, NSUB, 2])
        rv, b_rv = cv.F("rv", [128, 4])
        for s in range(NSUB):
            self.ld(self.x_tm[:, s, :], self.x_p[i * T + s * 128:i * T + (s + 1) * 128, :], W=[self.b_x])
        self.rms_T()
        self.xhat_T()
        for cb in range(4):
            wt, b_wt = self.next_wst()
            self.ld(wt[:], self.wA_in[cb], W=[b_wt])
            for s in range(NSUB):
                pm, b_pm = self.next_pm()
                for kc in range(8):
                    self.pe(lambda: nc.tensor.matmul(pm[:, :], lhsT=self.xnT[:, kc, s * 128:(s + 1) * 128], rhs=wt[:, kc, :],
                                                     start=(kc == 0), stop=(kc == 7)), R=[self.b_xnT, b_wt], W=[b_pm], signal=(kc == 7))
                self.act(lambda: nc.scalar.activation(out=gv[:, s, cb * 512:(cb + 1) * 512], in_=pm[:, :], func=AF.Gelu), R=[b_pm], W=[b_gv])
                self.dve(lambda: nc.vector.bn_stats(out=stats[:, s, cb, :], in_=gv[:, s, cb * 512:(cb + 1) * 512]), R=[b_gv], W=[b_stats])
        for s in range(NSUB):
            self.dve(lambda: nc.vector.bn_aggr(out=mv[:, s, :], in_=stats[:, s, :, :]), R=[b_stats], W=[b_mv])
        self.dve(lambda: nc.vector.tensor_scalar(out=rv[:, 0:NSUB], in0=mv[:, :, 1], scalar1=EPS, scalar2=None, op0=ALU.add), R=[b_mv], W=[b_rv])
        self.act(lambda: nc.scalar.activation(out=rv[:, 0:NSUB], in_=rv[:, 0:NSUB], func=AF.Sqrt), R=[b_rv], W=[b_rv])
        self.dve(lambda: nc.vector.reciprocal(out=rv[:, 2:2 + NSUB], in_=rv[:, 0:NSUB]), R=[b_rv], W=[b_rv])
        for s in range(NSUB):
            self.dve(lambda: nc.vector.tensor_scalar(out=gv[:, s, :], in0=gv[:, s, :], scalar1=mv[:, s, 0:1], scalar2=rv[:, 2 + s:3 + s],
                                                     op0=ALU.subtract, op1=ALU.mult), R=[b_gv, b_mv, b_rv], W=[b_gv])
        for q in range(4):
            (guq, b_gu), (szq, b_sz), (gsq, b_gs) = gu[q % 2], sz[q % 2], gs[q % 2]
            for (slab, dst, b_dst, fn) in ((4 + q, guq, b_gu, AF.Gelu), (8 + q, szq, b_sz, AF.Silu)):
                wt, b_wt = self.next_wst()
                self.ld(wt[:], self.wA_in[slab], W=[b_wt])
                for m in range(4):
                    pm, b_pm = self.next_pm()
                    for kc in range(8):
                        self.pe(lambda: nc.tensor.matmul(pm[:, 0:T], lhsT=wt[:, kc, m * 128:(m + 1) * 128], rhs=self.xnT[:, kc, :],
                                                         start=(kc == 0), stop=(kc == 7)), R=[self.b_xnT, b_wt], W=[b_pm], signal=(kc == 7))
                    self.act(lambda: nc.scalar.activation(out=dst[:, m, :], in_=pm[:, 0:T], func=fn), R=[b_pm], W=[b_dst])
            self.pool(lambda: nc.gpsimd.tensor_tensor(out=gsq[:, :, :], in0=guq[:, :, :], in1=szq[:, :, :], op=ALU.mult), R=[b_gu, b_sz], W=[b_gs])
            for s in range(NSUB):
                pm, b_pm = self.next_pm()
                for m in range(4):
                    g = 4 * q + m
                    self.pe(lambda: nc.tensor.matmul(pm[:, m * 128:(m + 1) * 128], lhsT=gv[:, s, g * 128:(g + 1) * 128], rhs=self.wmT[:, g, :],
                                                     start=(m == 0), stop=True, skip_group_check=True), R=[b_gv, self.b_wmT], W=[b_pm], signal=(m == 3))
                for m in range(4):
                    g = 4 * q + m
                    tf, b_tf = tmpf[m % 2]
                    self.dve(lambda: nc.vector.scalar_tensor_tensor(out=tf, in0=pm[:, m * 128:(m + 1) * 128], scalar=self.lng[:, g:g + 1],
                                                                    in1=self.Bmix[:, g, :], op0=ALU.mult, op1=ALU.add),
                             R=[b_pm, self.b_lng, self.b_Bmix], W=[b_tf])
                    self.dve(lambda: nc.vector.tensor_tensor(out=yT[:, g, s * 128:(s + 1) * 128], in0=tf, in1=gsq[:, m, s * 128:(s + 1) * 128],
                                                             op=ALU.mult), R=[b_tf, b_gs], W=[b_yT])
        for cb in range(2):
            pms = [self.next_pm() for _ in range(NSUB)]
            for fcg in range(4):
                wt, b_wt = self.next_wst()
                self.ld(wt[:, 0:4, :], self.wA_out[fcg * 2 + cb], W=[b_wt])
                for s in range(NSUB):
                    for fl in range(4):
                        fc = fcg * 4 + fl
                        self.pe(lambda: nc.tensor.matmul(pms[s][0][:, :], lhsT=yT[:, fc, s * 128:(s + 1) * 128], rhs=wt[:, fl, :],
                                                         start=(fc == 0), stop=(fc == 15)), R=[b_yT, b_wt], W=[pms[s][1]], signal=(fl == 3))
            for s in range(NSUB):
                self.dve(lambda: nc.vector.tensor_tensor(out=self.x_tm[:, s, cb * 512:(cb + 1) * 512], in0=self.x_tm[:, s, cb * 512:(cb + 1) * 512],
                                                         in1=pms[s][0][:, :], op=ALU.add), R=[pms[s][1], self.b_x], W=[self.b_x])

    def mm8(self, out_ap, b_out, lhs_fn, rhs_fn, R):
        nc = self.nc
        for kc in range(8):
            self.pe(lambda: nc.tensor.matmul(out_ap, lhsT=lhs_fn(kc), rhs=rhs_fn(kc), start=(kc == 0), stop=(kc == 7)),
                    R=R, W=[b_out], signal=(kc == 7))

    def layer_b(self, i):
        nc = self.nc
        cv = KB.Carver(self)
        kvtm, b_kvtm = cv.F("kvtm", [128, NSUB, 1536])
        cosF, b_cosF = cv.F("cosF", [128, T])
        sinF, b_sinF = cv.F("sinF", [128, T])
        ctm, b_ctm = cv.F("ctm", [128, NSUB, 4, 8])
        stm, b_stm = cv.F("stm", [128, NSUB, 4, 8])
        rtmp, b_rtmp = cv.F("rtmp", [128, 4, 4, 8])
        force, b_force = cv.F("force", [128, NSUB, 64])
        tM, b_tM = cv.F("tM", [128, 512])
        tA, b_tA = cv.F("tA", [128, 512])
        tB, b_tB = cv.F("tB", [128, 512])
        yacc, b_yacc = cv.F("yacc", [128, 512])
        impv, b_imp = cv.F("imp", [128, 64])
        score, b_score = cv.F("score", [128, 64])
        scr, b_scr = cv.F("scr", [128, 64])
        m8, b_m8 = cv.F("m8", [128, 16])
        rd, b_rd = cv.F("rd", [128, 8])
        tO_off = cv.f
        qtA, b_qtA = cv.F("qtA", [128, T])
        qtB, b_qtB = cv.F("qtB", [128, T])
        tO, b_tO = self.arenaF[:, tO_off:tO_off + 512], self.nb("tO")
        ktmp, b_ktmp = cv.H("ktmp", [128, NSUB, 1024])
        qrawT, b_qraw = cv.H("qrawT", [128, 8, T])
        qaug, b_qaug = cv.H("qaug", [128, 4, NSUB, 512])
        b_qbias = self.nb("qbias")
        sz64, b_sz64 = cv.H("sz64", [128, 3, NSUB, 512])
        gT, b_gT = cv.H("gT", [128, T])
        PT = [cv.H("PT%d" % k, [128, 512]) for k in range(2)]
        yB64, b_yB = cv.H("yB64", [128, 4, NSUB, 512])
        Btm, b_Btm = cv.H("Btm", [128, 128])
        cbias, b_cbias = cv.H("cbias", [128, NSUB, 2, 128])
        hidT, b_hidT = cv.H("hidT", [128, 4, 16])
        vstage, b_vst = cv.H("vstage", [128, 2, 128])
        qrot, b_qrot = cv.H("qrot", [128, T])
        c = self.cst
        t0 = i * T

        self.rms_T()
        self.xhat_T()
        self.ld(cosF, c["c_cosF"][:, t0:t0 + T], W=[b_cosF], q="pool")
        self.ld(sinF, c["c_sinF"][:, t0:t0 + T], W=[b_sinF], q="pool")
        self.ld(ctm, c["c_costm4"][t0:t0 + T].rearrange("(s p) g c -> p s g c", p=128), W=[b_ctm], q="pool")
        self.ld(stm, c["c_sintm4"][t0:t0 + T].rearrange("(s p) g c -> p s g c", p=128), W=[b_stm], q="pool")
        self.ld(force, c["c_force"][2 * i:2 * i + 2].rearrange("q p j -> p q j"), W=[b_force], q="pool")
        self.ld(cbias, c["c_cbias"][2 * i:2 * i + 2].rearrange("q n p c -> p q n c"), W=[b_cbias], q="pool")
        self.dve(lambda: nc.vector.memset(Btm, 0.0), W=[b_Btm])

        for br in range(3):
            wt, b_wt = self.next_wst()
            self.ld(wt[:], self.wB_kv[br], W=[b_wt])
            for s in range(NSUB):
                pm, b_pm = self.next_pm()
                self.mm8(pm[:, :], b_pm, lambda kc: self.xnT[:, kc, s * 128:(s + 1) * 128], lambda kc: wt[:, kc, :], [self.b_xnT, b_wt])
                self.act(lambda: nc.scalar.copy(out=kvtm[:, s, br * 512:(br + 1) * 512], in_=pm[:, :]), R=[b_pm], W=[b_kvtm])
        if getattr(self, 'stop_after', None) == 'kv':
            return
        for s in range(NSUB):
            for br in (1, 2):
                kview = kvtm[:, s, br * 512:br * 512 + 256].rearrange("p (g d) -> p g d", d=64)
                x1 = kview[:, :, 0:8]
                x2 = kview[:, :, 8:16]
                cs = ctm[:, s, :, :]
                sn = stm[:, s, :, :]
                R_ = [b_kvtm, b_ctm, b_stm]
                self.dve(lambda: nc.vector.tensor_tensor(out=rtmp[:, 0, :, :], in0=x1, in1=cs, op=ALU.mult), R=R_, W=[b_rtmp])
                self.dve(lambda: nc.vector.tensor_tensor(out=rtmp[:, 1, :, :], in0=x2, in1=sn, op=ALU.mult), R=R_, W=[b_rtmp])
                self.dve(lambda: nc.vector.tensor_tensor(out=rtmp[:, 2, :, :], in0=x2, in1=cs, op=ALU.mult), R=R_, W=[b_rtmp])
                self.dve(lambda: nc.vector.tensor_tensor(out=rtmp[:, 3, :, :], in0=x1, in1=sn, op=ALU.mult), R=R_, W=[b_rtmp])
                self.dve(lambda: nc.vector.tensor_tensor(out=x1, in0=rtmp[:, 0, :, :], in1=rtmp[:, 1, :, :], op=ALU.subtract), R=[b_rtmp], W=[b_kvtm])
                self.dve(lambda: nc.vector.tensor_tensor(out=x2, in0=rtmp[:, 2, :, :], in1=rtmp[:, 3, :, :], op=ALU.add), R=[b_rtmp], W=[b_kvtm])
        if getattr(self, 'stop_after', None) == 'rope':
            return
        for s in range(NSUB):
            r0 = t0 + s * 128
            self.st(self.o_cmp_p[r0:r0 + 128, :], kvtm[:, s, 0:512], R=[b_kvtm], q="pool")
            self.st(self.o_sel_p[r0:r0 + 128, :], kvtm[:, s, 512:1024], R=[b_kvtm], q="pool")
            if r0 >= SEQ - 512:
                w0 = r0 - (SEQ - 512)
                self.st(self.o_win_p[w0:w0 + 128, :], kvtm[:, s, 1024:1536], R=[b_kvtm], q="pool")
        if getattr(self, 'stop_after', None) == 'out':
            return
        for s in range(NSUB):
            kt = 2 * i + s
            slot = kt % 6
            self.pool(lambda: nc.gpsimd.tensor_copy(out=ktmp[:, s, 0:768], in_=kvtm[:, s, 0:768]), R=[b_kvtm], W=[b_ktmp])
            self.pool(lambda: nc.gpsimd.tensor_copy(out=ktmp[:, s, 768:1024], in_=kvtm[:, s, 1024:1280]), R=[b_kvtm], W=[b_ktmp])
            self.pool(lambda: nc.gpsimd.tensor_copy(out=self.VS[:, kt, :, 0:64], in_=kvtm[:, s, 768:1024].rearrange("p (g d) -> p g d", d=64)),
                      R=[b_kvtm], W=[self.b_VS])
            self.pool(lambda: nc.gpsimd.tensor_copy(out=self.VW[:, slot, :, 0:64], in_=kvtm[:, s, 1280:1536].rearrange("p (g d) -> p g d", d=64)),
                      R=[b_kvtm], W=[self.b_VW])
            pt, b_pt = self.next_ptr()
            for tl in range(4):
                self.pe(lambda: nc.tensor.transpose(out=pt[:, tl * 128:(tl + 1) * 128], in_=ktmp[:, s, tl * 128:(tl + 1) * 128], identity=self.ident[:]),
                        R=[b_ktmp, self.b_ident], W=[b_pt], signal=(tl == 3))
            self.act(lambda: nc.scalar.copy(out=self.KRAW[:, :, 16 + s * 128:16 + (s + 1) * 128], in_=pt[:, 0:512].rearrange("p (a b) -> p a b", b=128)),
                     R=[b_pt], W=[self.b_KRAW])
            for (c0, dst, b_dst, off) in ((512, self.KA, self.b_KA, kt * 128), (768, self.KW, self.b_KW, slot * 128)):
                pt, b_pt = self.next_ptr()
                for g in range(4):
                    self.pe(lambda: nc.tensor.transpose(out=pt[0:64, g * 128:(g + 1) * 128], in_=ktmp[:, s, c0 + g * 64:c0 + (g + 1) * 64], identity=self.ident[:]),
                            R=[b_ktmp, self.b_ident], W=[b_pt], signal=(g == 3))
                self.dve(lambda: nc.vector.tensor_copy(out=dst[0:64, :, off:off + 128], in_=pt[0:64, 0:512].rearrange("p (a b) -> p a b", b=128)),
                         R=[b_pt], W=[b_dst])
        if getattr(self, 'stop_after', None) == 'tr':
            return
        for sl in range(2):
            wt, b_wt = self.next_wst()
            self.ld(wt[:], self.wB_q[sl], W=[b_wt])
            for m4 in range(4):
                m = sl * 4 + m4
                pm, b_pm = self.next_pm()
                self.mm8(pm[:, 0:T], b_pm, lambda kc: wt[:, kc, m4 * 128:(m4 + 1) * 128], lambda kc: self.xnT[:, kc, :], [self.b_xnT, b_wt])
                self.act(lambda: nc.scalar.copy(out=qrawT[:, m, :], in_=pm[:, 0:T]), R=[b_pm], W=[b_qraw])
                px, b_px = self.px
                self.pe(lambda: nc.tensor.matmul(px[:, 0:T], lhsT=self.pswap[:], rhs=qrawT[:, m, :], start=True, stop=True), R=[self.b_pswap, b_qraw], W=[b_px])
                self.pool(lambda: nc.gpsimd.tensor_tensor(out=qtA, in0=qrawT[:, m, :], in1=cosF, op=ALU.mult), R=[b_qraw, b_cosF], W=[b_qtA])
                self.dve(lambda: nc.vector.tensor_tensor(out=qtB, in0=px[:, 0:T], in1=sinF, op=ALU.mult), R=[b_px, b_sinF], W=[b_qtB])
                self.dve(lambda: nc.vector.tensor_tensor(out=qrot, in0=qtA, in1=qtB, op=ALU.add), R=[b_qtA, b_qtB], W=[b_qrot])
                g = m // 2
                for a in range(2):
                    ro = a * 2 + (m % 2)
                    self.act(lambda: nc.scalar.copy(out=qaug[0:64, g, :, ro * 128:(ro + 1) * 128], in_=qrot[a * 64:(a + 1) * 64, :].rearrange("p (s q) -> p s q", q=128)),
                             R=[b_qrot], W=[b_qaug])
        if getattr(self, 'stop_after', None) == 'q':
            return
        pm, b_pm = self.next_pm()
        self.mm8(pm[0:48, 0:T], b_pm, lambda kc: self.wgl[:, kc, :], lambda kc: self.xnT[:, kc, :], [self.b_xnT, self.b_wgl])
        self.act(lambda: nc.scalar.activation(out=gT[0:48, :], in_=pm[0:48, 0:T], func=AF.Sigmoid, bias=self.gateb[:, 0:1]), R=[b_pm, self.b_gateb], W=[b_gT])

        if getattr(self, 'stop_after', None) == 'gl':
            return
        nblk = 15 if i == 0 else 16
        base = 16 if i == 0 else 0
        m0 = 0 if i == 0 else 16 * i - 1
        pmh, b_pmh = self.next_pm()
        for kv in range(2):
            wt, b_wt = self.next_wst()
            w1v = wt[:].rearrange("p a b -> p (a b)").rearrange("p (j c) -> p j c", c=128)
            self.ld(w1v, self.w1bd[kv], W=[b_wt])
            for hp in range(2):
                tl = kv * 2 + hp
                for j in range(32):
                    self.pe(lambda: nc.tensor.matmul(pmh[:, tl * 16:tl * 16 + nblk], lhsT=w1v[:, j, :],
                                                     rhs=self.KRAW[:, tl, base + j:base + j + 16 * (nblk - 1) + 1:16],
                                                     start=(tl == 0 and j == 0), stop=(j == 31), skip_group_check=True),
                            R=[b_wt, self.b_KRAW], W=[b_pmh], signal=(j == 31))
        for kv in range(2):
            self.act(lambda: nc.scalar.activation(out=hidT[:, 2 * kv:2 * kv + 2, 0:nblk],
                                                  in_=pmh[:, 32 * kv:32 * kv + 32].rearrange("p (a b) -> p a b", b=16)[:, :, 0:nblk],
                                                  func=AF.Silu, bias=self.bias1[:, kv:kv + 1]), R=[b_pmh, self.b_bias1], W=[b_hidT])
        pmk, b_pmk = self.next_pm()
        for g in range(4):
            self.pe(lambda: nc.tensor.matmul(pmk[:, g * 16:g * 16 + nblk], lhsT=self.W2K[:, g % 2, :], rhs=hidT[:, g // 2, 0:nblk],
                                             start=(g == 0), stop=True, skip_group_check=True), R=[self.b_W2K, b_hidT], W=[b_pmk], signal=(g == 3))
        self.act(lambda: nc.scalar.activation(out=self.KCT[:, :, m0:m0 + nblk], in_=pmk[:, 0:64].rearrange("p (a b) -> p a b", b=16)[:, :, 0:nblk],
                                              func=AF.Identity, bias=self.b2K[:, 0:1]), R=[b_pmk, self.b_b2K], W=[self.b_KCT])
        pmv, b_pmv = self.next_pm()
        for hp in range(2):
            self.pe(lambda: nc.tensor.matmul(pmv[0:nblk, hp * 128:(hp + 1) * 128], lhsT=hidT[:, 2 + hp, 0:nblk], rhs=self.W2V[:, :],
                                             start=(hp == 0), stop=False, skip_group_check=True), R=[b_hidT, self.b_W2V], W=[b_pmv], signal=False)
            self.pe(lambda: nc.tensor.matmul(pmv[0:nblk, hp * 128:(hp + 1) * 128], lhsT=self.ones_row[0:1, 0:nblk], rhs=self.b2Vrow[0:1, :],
                                             start=False, stop=True, skip_group_check=True), R=[self.b_ones, self.b_b2V], W=[b_pmv], signal=(hp == 1))
        self.act(lambda: nc.scalar.copy(out=vstage[0:nblk, :, :], in_=pmv[0:nblk, 0:256].rearrange("p (a b) -> p a b", b=128)), R=[b_pmv], W=[b_vst])
        blk = m0
        while blk < m0 + nblk:
            nt = blk // 128
            p0 = blk % 128
            cnt = min(m0 + nblk - blk, 128 - p0)
            o = blk - m0
            self.ld(self.VC[p0:p0 + cnt, nt, :, 0:64], vstage[o:o + cnt, :, :].rearrange("p a (h d) -> p (a h) d", d=64), R=[b_vst], W=[self.b_VC], q="pool")
            blk += cnt
        self.dve(lambda: nc.vector.tensor_copy(out=self.KRAW[:, :, 0:16], in_=self.KRAW[:, :, T:T + 16]), R=[self.b_KRAW], W=[self.b_KRAW])

        if getattr(self, 'stop_after', None) == 'cmpr':
            return
        po0, b_po0 = self.po[0]
        po1, b_po1 = self.po[1]
        px, b_px = self.px

        def bias4(pm, b_pm, tile, b_tile, last=True):
            for ro in range(4):
                self.pe(lambda: nc.tensor.matmul(pm[:, ro * 128:(ro + 1) * 128], lhsT=self.ident[:], rhs=tile, start=False,
                                                 stop=(last and ro == 3), skip_group_check=True), R=[self.b_ident, b_tile], W=[b_pm], signal=(ro == 3))

        def merge(b, g, qs, po, b_po, last):
            self.dve(lambda: nc.vector.tensor_scalar(out=tM[0:64, :], in0=po[64:128, :], scalar1=1e-30, scalar2=None, op0=ALU.max), R=[b_po], W=[b_tM])
            self.act(lambda: nc.scalar.copy(out=tO[0:64, :], in_=po[0:64, :]), R=[b_po], W=[b_tO])
            self.dve(lambda: nc.vector.reciprocal(out=tA[0:64, :], in_=tM[0:64, :]), R=[b_tM], W=[b_tA])
            for ro in range(4):
                r = 2 * (ro % 2) + ro // 2
                h = 4 * g + r
                self.pe(lambda: nc.tensor.matmul(px[0:64, ro * 128:(ro + 1) * 128], lhsT=self.esel[0:48, b * 16 + h, :], rhs=gT[0:48, qs * 128:(qs + 1) * 128],
                                                 start=(ro == 0), stop=True, skip_group_check=True), R=[self.b_esel, b_gT], W=[b_px], signal=(ro == 3))
            self.dve(lambda: nc.vector.tensor_tensor(out=tM[0:64, :], in0=tA[0:64, :], in1=px[0:64, :], op=ALU.mult), R=[b_tA, b_px], W=[b_tM])
            self.dve(lambda: nc.vector.tensor_tensor(out=tA[0:64, :], in0=tM[0:64, :], in1=tO[0:64, :], op=ALU.mult), R=[b_tM, b_tO], W=[b_tA])
            if b == 0:
                self.pool(lambda: nc.gpsimd.tensor_tensor(out=yacc[0:64, :], in0=tA[0:64, :], in1=sz64[0:64, b, qs, :], op=ALU.mult), R=[b_tA, b_sz64], W=[b_yacc])
            else:
                self.pool(lambda: nc.gpsimd.tensor_tensor(out=tB[0:64, :], in0=tA[0:64, :], in1=sz64[0:64, b, qs, :], op=ALU.mult), R=[b_tA, b_sz64], W=[b_tB])
                if not last:
                    self.pool(lambda: nc.gpsimd.tensor_tensor(out=yacc[0:64, :], in0=yacc[0:64, :], in1=tB[0:64, :], op=ALU.add), R=[b_tB, b_yacc], W=[b_yacc])
                else:
                    self.pool(lambda: nc.gpsimd.tensor_tensor(out=yB64[0:64, g, qs, :], in0=yacc[0:64, :], in1=tB[0:64, :], op=ALU.add), R=[b_tB, b_yacc], W=[b_yB])

        for g in range(4):
            for b in range(3):
                wt, b_wt = self.next_wst()
                sl = 2 * b + g // 2
                c0 = (g % 2) * 256
                self.ld(wt[:, :, 0:256], self.wB_z[sl, :, :, c0:c0 + 256], W=[b_wt])
                for mm in range(2):
                    pm, b_pm = self.next_pm()
                    self.mm8(pm[:, 0:T], b_pm, lambda kc: wt[:, kc, mm * 128:(mm + 1) * 128], lambda kc: self.xnT[:, kc, :], [self.b_xnT, b_wt])
                    for a in range(2):
                        ro = a * 2 + mm
                        self.act(lambda: nc.scalar.activation(out=sz64[0:64, b, :, ro * 128:(ro + 1) * 128],
                                                              in_=pm[a * 64:(a + 1) * 64, 0:T].rearrange("p (s q) -> p s q", q=128), func=AF.Silu),
                                 R=[b_pm], W=[b_sz64])
            if getattr(self, 'stop_after', None) == 'z':
                return
            for qs in range(NSUB):
                qt = 2 * i + qs
                nts = [0] if i < 8 else [0, 1]
                for nt in nts:
                    need_bias = not (16 * (128 * nt + 127) + 31 <= 128 * qt)
                    pt_, b_ptile = PT[nt]
                    for a in range(2):
                        pm, b_pm = self.next_pm()
                        self.pe(lambda: nc.tensor.matmul(pm[:, 0:256], lhsT=self.KCT[a * 64:(a + 1) * 64, g, nt * 128:(nt + 1) * 128],
                                                         rhs=qrawT[a * 64:(a + 1) * 64, 2 * g:2 * g + 2, qs * 128:(qs + 1) * 128],
                                                         start=True, stop=(not need_bias), skip_group_check=True),
                                R=[self.b_KCT, b_qraw], W=[b_pm])
                        if need_bias:
                            for mm in range(2):
                                self.pe(lambda: nc.tensor.matmul(pm[:, mm * 128:(mm + 1) * 128], lhsT=self.ident[:], rhs=cbias[:, qs, nt, :], start=False,
                                                                 stop=(mm == 1), skip_group_check=True), R=[self.b_ident, b_cbias], W=[b_pm], signal=(mm == 1))
                        self.act(lambda: nc.scalar.activation(out=pt_[:, a * 256:(a + 1) * 256], in_=pm[:, 0:256], func=AF.Exp, scale=0.125), R=[b_pm], W=[b_ptile])
                    self.pe(lambda: nc.tensor.matmul(po0[:, :], lhsT=self.VC[:, nt, g, :], rhs=pt_, start=(nt == 0), stop=(nt == nts[-1])),
                            R=[self.b_VC, b_ptile], W=[b_po0])
                if getattr(self, 'stop_after', None) == 'cS':
                    return
                for ro in range(4):
                    for nt in nts:
                        self.pe(lambda: nc.tensor.matmul(px[:, ro * 65:(ro + 1) * 65], lhsT=PT[nt][0][:, ro * 128:(ro + 1) * 128], rhs=self.ovaug[:, nt, :],
                                                         start=(ro == 0 and nt == 0), stop=(nt == nts[-1]), skip_group_check=True),
                                R=[PT[nt][1], self.b_ovaug], W=[b_px], signal=(ro == 3 and nt == nts[-1]))
                if getattr(self, 'stop_after', None) == 'cI':
                    return
                pxv = px[:, 0:260].rearrange("p (r c) -> p r c", c=65)
                self.dve(lambda: nc.vector.tensor_scalar(out=rd[:, 0:4], in0=pxv[:, :, 64], scalar1=1e-30, scalar2=None, op0=ALU.max), R=[b_px], W=[b_rd])
                self.dve(lambda: nc.vector.reciprocal(out=rd[:, 4:8], in_=rd[:, 0:4]), R=[b_rd], W=[b_rd])
                self.dve(lambda: nc.vector.tensor_scalar(out=impv, in0=pxv[:, 0, 0:64], scalar1=rd[:, 4:5], scalar2=None, op0=ALU.mult), R=[b_px, b_rd], W=[b_imp])
                for ro in range(1, 4):
                    self.dve(lambda: nc.vector.scalar_tensor_tensor(out=impv, in0=pxv[:, ro, 0:64], scalar=rd[:, 4 + ro:5 + ro], in1=impv, op0=ALU.mult, op1=ALU.add),
                             R=[b_px, b_rd, b_imp], W=[b_imp])
                self.dve(lambda: nc.vector.tensor_tensor(out=score, in0=impv, in1=force[:, qs, :], op=ALU.add), R=[b_imp, b_force], W=[b_score])
                self.dve(lambda: nc.vector.max(out=m8[:, 0:8], in_=score), R=[b_score], W=[b_m8])
                self.dve(lambda: nc.vector.match_replace(out=scr, in_to_replace=m8[:, 0:8], in_values=score, imm_value=-1e30), R=[b_score, b_m8], W=[b_scr])
                self.dve(lambda: nc.vector.max(out=m8[:, 8:16], in_=scr), R=[b_scr], W=[b_m8])
                self.dve(lambda: nc.vector.tensor_scalar(out=Btm[:, 64:128], in0=score, scalar1=m8[:, 15:16], scalar2=1.0, op0=ALU.is_ge, op1=ALU.subtract),
                         R=[b_score, b_m8], W=[b_Btm])
                kts = [kt for kt in range(qt - 4, qt + 1) if kt >= 0]
                pend = None
                for j, kt in enumerate(kts):
                    slot = kt % 6
                    pm, b_pm = self.next_pm()
                    nb_ = (kt == qt) or (kt == qt - 4)
                    self.pe(lambda: nc.tensor.matmul(pm[:, :], lhsT=self.KW[0:64, g, slot * 128:(slot + 1) * 128], rhs=qaug[0:64, g, qs, :],
                                                     start=True, stop=not nb_, skip_group_check=True), R=[self.b_KW, b_qaug], W=[b_pm])
                    if kt == qt:
                        bias4(pm, b_pm, self.tri[:], self.b_tri)
                    elif kt == qt - 4:
                        bias4(pm, b_pm, self.atri[:], self.b_atri)
                    if pend is not None:
                        pend()
                    pt_, b_ptile = PT[j % 2]
                    self.act(lambda: nc.scalar.activation(out=pt_, in_=pm[:, :], func=AF.Exp, scale=0.125), R=[b_pm], W=[b_ptile])

                    def pend(slot=slot, pt_=pt_, b_ptile=b_ptile, first=(j == 0), last=(j == len(kts) - 1)):
                        self.pe(lambda: nc.tensor.matmul(po1[:, :], lhsT=self.VW[:, slot, g, :], rhs=pt_, start=first, stop=last),
                                R=[self.b_VW, b_ptile], W=[b_po1])
                pend()
                ptb, b_ptb = self.next_ptr()
                self.pe(lambda: nc.tensor.transpose(out=ptb[:, 0:128], in_=Btm, identity=self.ident[:]), R=[b_Btm, self.b_ident], W=[b_ptb])
                for ro in range(4):
                    self.act(lambda: nc.scalar.copy(out=qaug[64:128, g, qs, ro * 128:(ro + 1) * 128], in_=ptb[64:128, 0:128]), R=[b_ptb], W=[b_qbias])
                merge(0, g, qs, po0, b_po0, False)
                merge(2, g, qs, po1, b_po1, False)
                pend = None
                for kt in range(qt + 1):
                    pm, b_pm = self.next_pm()
                    self.pe(lambda: nc.tensor.matmul(pm[:, :], lhsT=self.KA[:, g, kt * 128:(kt + 1) * 128], rhs=qaug[:, g, qs, :],
                                                     start=True, stop=(kt != qt), skip_group_check=True), R=[self.b_KA, b_qaug, b_qbias], W=[b_pm])
                    if kt == qt:
                        bias4(pm, b_pm, self.tri[:], self.b_tri)
                    if pend is not None:
                        pend()
                    pt_, b_ptile = PT[kt % 2]
                    self.act(lambda: nc.scalar.activation(out=pt_, in_=pm[:, :], func=AF.Exp, scale=0.125), R=[b_pm], W=[b_ptile])

                    def pend(kt=kt, pt_=pt_, b_ptile=b_ptile):
                        self.pe(lambda: nc.tensor.matmul(po0[:, :], lhsT=self.VS[:, kt, g, :], rhs=pt_, start=(kt == 0), stop=(kt == qt)),
                                R=[self.b_VS, b_ptile], W=[b_po0])
                pend()
                merge(1, g, qs, po0, b_po0, True)

        if getattr(self, 'stop_after', None) == 'attn':
            return
        for cb in range(2):
            pms = [self.next_pm() for _ in range(NSUB)]
            for hh in range(2):
                wt, b_wt = self.next_wst()
                self.ld(wt[0:64, :, :], self.wB_out[cb * 2 + hh], W=[b_wt])
                for s in range(NSUB):
                    for hl in range(8):
                        h = hh * 8 + hl
                        g, r = h // 4, h % 4
                        ro = (r % 2) * 2 + r // 2
                        self.pe(lambda: nc.tensor.matmul(pms[s][0][:, :], lhsT=yB64[0:64, g, s, ro * 128:(ro + 1) * 128], rhs=wt[0:64, hl, :],
                                                         start=(h == 0), stop=(h == 15)), R=[b_yB, b_wt], W=[pms[s][1]], signal=(hl == 7))
            for s in range(NSUB):
                self.dve(lambda: nc.vector.tensor_tensor(out=self.x_tm[:, s, cb * 512:(cb + 1) * 512], in0=self.x_tm[:, s, cb * 512:(cb + 1) * 512],
                                                         in1=pms[s][0][:, :], op=ALU.add), R=[pms[s][1], self.b_x], W=[self.b_x])
        if getattr(self, 'stop_after', None) == 'wout':
            return
        self.rms_T()
        yout = kvtm[:, :, 0:D]
        for s in range(NSUB):
            self.dve(lambda: nc.vector.scalar_tensor_tensor(out=yout[:, s, :], in0=self.x_tm[:, s, :], scalar=self.st4[:, 4 + s:5 + s], in1=self.fgbc[:, :],
                                                            op0=ALU.mult, op1=ALU.mult), R=[self.b_x, self.b_st4, self.b_fgbc], W=[b_kvtm])
            self.st(self.y_p[t0 + s * 128:t0 + (s + 1) * 128, :], yout[:, s, :], R=[b_kvtm], q="pool")

    def sample_phase(self):
        nc = self.nc
        S = self.S
        S.barrier()
        c = self.cst
        px, b_px = self.px
        po0, b_po0 = self.po[0]
        po1, b_po1 = self.po[1]
        KAf = self.KA[:].rearrange("p a b -> p (a b)")
        VSf = self.VS[:].rearrange("p a b c -> p (a b c)")
        b_KAf, b_VSf = self.b_KA, self.b_VS
        KRAWs = KAf[:, 0:4 * 2064].rearrange("p (a b) -> p a b", b=2064)
        KsTs = KAf[:, 8256:8256 + 4096].rearrange("p (a b) -> p a b", b=2048)
        KCTs = KAf[:, 12352:12352 + 4032]
        VSs = VSf[:, 0:8192].rearrange("p (a b c) -> p a b c", b=4, c=128)
        VCs = VSf[:, 8192:12288].rearrange("p (a b c) -> p a b c", b=4, c=128)
        KCs = VSf[:, 12288:16384].rearrange("p (a b) -> p a b", b=1024)
        cv = KB.Carver(self)
        hS, b_hS = cv.F("hS", [128, 4096])
        small, b_small = cv.F("small", [128, 64])
        mask01, b_mask = cv.F("mask01", [128, 256])
        wgt, b_wgt = cv.F("wgt", [128, 260])
        scoreS, b_scoreS = cv.F("scoreS", [128, 256])
        scrS, b_scrS = cv.F("scrS", [128, 256])
        accS, b_accS = cv.F("accS", [128, 16])
        tS, b_tS = cv.F("tS", [128, 16])
        qb2, b_qb2 = cv.H("qb2", [128, 2, 1024])
        knb, b_knb = cv.H("knb", [128, 2, 256])
        szb, b_szb = cv.H("szb", [128, 3072])
        qTS, b_qTS = cv.H("qTS", [128, 2, 32, SPC])
        kTn, b_kTn = cv.H("kTn", [128, 2, 2, SPC])
        szTS, b_szTS = cv.H("szTS", [128, 48, SPC])
        vnewA, b_vnewA = cv.H("vnewA", [128, 2, 4, 128])
        vrow0, b_vrow0 = cv.H("vrow0", [128, 2, 4, 128])
        zrow, b_zrow = cv.H("zrow", [128, 512])
        yTSall, b_yTSall = cv.H("yTSall", [128, 16, SPC])
        ovS, b_ovS = cv.H("ovS", [128, 8, 258])
        pgb = [cv.H("pgb%d" % k, [128, 512]) for k in range(4)]
        hidTs, b_hidTs = cv.H("hidTs", [128, 4, 128])
        vsts2 = [cv.H("vsts%d" % k, [128, 2, 128]) for k in range(2)]
        PTc, b_PTc = cv.H("PTc", [128, 8, 16])
        PTs, b_PTs = cv.H("PTs", [128, 4, 16, 4])
        M4, b_M4 = KAf[:, 12352:12352 + 2048].rearrange("p (a b c) -> p a b c", b=128, c=4), self.nb("M4")
        KwTs, b_KwTs = cv.H("KwTs", [128, 2, 512])
        VWs, b_VWs = cv.H("VWs", [128, 4, 4, 128])
        pnew, b_pnew = cv.H("pnew", [128, 2, 16])
        b_kraw = [self.nb("kraw%d" % k) for k in range(16)]
        b_carry = self.nb("kcarry")
        b_kst = [self.nb("kst%d" % k) for k in range(16)]
        b_vss = [self.nb("vss%d" % k) for k in range(16)]
        b_VCs = self.nb("VCs")
        b_KCs = self.nb("KCs")
        idxf, b_idxf = cv.F("idxf", [128, 128])
        idxi_t, b_idxi = self.sb("idxi", [128, 128], I32)
        ptbi_t, b_ptbi = self.sb("ptbi", [128, 128], I32)
        iop_t, b_iop = self.sb("iop", [128, 1], I32)
        iopf, b_iopf = cv.F("iopf", [128, 1])
        grps, b_grps = cv.F("grps", [128, 4])
        oh4, b_oh4 = cv.F("oh4", [128, 4, 128])
        forceS, b_forceS = cv.F("forceS", [128, 256])
        ropeS, b_ropeS = cv.F("ropeS", [128, 2, 16, 8])
        rtS, b_rtS = cv.F("rtS", [128, 4, 16, 8])
        w00, b_w00 = cv.F("w00", [128, 2, 16])
        gbrow, b_gbrow = cv.F("gbrow", [128, 48])

        self.ld(ovS, c["c_ovaug_s"].rearrange("n p c -> p n c"), W=[b_ovS], q="pool")
        self.ld(grps[0:16, :], c["c_grpsel"][:, :], W=[b_grps], q="pool")
        self.ld(oh4[0:4, :, :], c["c_onehot4"][:, :, :], W=[b_oh4], q="pool")
        self.ld(forceS[0:4, :], c["c_force_s"][:, :], W=[b_forceS], q="pool")
        self.ld(ropeS[0:SPC], c["c_ropes16"].rearrange("t s h c -> s t h c"), W=[b_ropeS], q="pool")
        self.ld(w00[0:SPC, 0, :], self.a_w_s[:, 0, 0].partition_broadcast(SPC), W=[b_w00], q="pool", slow=True)
        self.ld(w00[0:SPC, 1, :], self.a_b_s[:, 0].partition_broadcast(SPC), W=[b_w00], q="pool", slow=True)
        self.ld(gbrow[0:SPC, :], self.b_gate.partition_broadcast(SPC), W=[b_gbrow], q="pool")
        self.pool(lambda: nc.gpsimd.iota(out=iop_t[:], pattern=[[0, 1]], base=0, channel_multiplier=1), W=[b_iop])
        self.dve(lambda: nc.vector.tensor_copy(out=iopf, in_=iop_t[:]), R=[b_iop], W=[b_iopf])
        self.dve(lambda: nc.vector.memset(zrow, 0.0), W=[b_zrow])
        self.dve(lambda: nc.vector.memset(vnewA[:, :, :, :], 1.0), W=[b_vnewA])

        gvS = hS[:, 0:2048]
        lngS = hS[:, 2048:4096]
        guS, b_guS = szb[:, 0:2048], b_szb
        szS_, b_szS = qb2[:, :, :].rearrange("p a b -> p (a b)"), b_qb2
        self.ld(self.x_tm[0:SPC, 0, :], self.x_s[:, :], W=[self.b_x])
        self.rms_T(SPC, SPC)
        self.xhat_T(SPC, SPC)
        self.ld(lngS[0:SPC, :], self.a_ln_g.partition_broadcast(SPC), W=[b_hS], q="pool")
        for cb in range(12):
            wt, b_wt = self.next_wst()
            self.ld(wt[:], self.wA_in[cb], W=[b_wt])
            pm, b_pm = self.next_pm()
            self.mm8(pm[0:SPC, :], b_pm, lambda kc: self.xnT[:, kc, 0:SPC], lambda kc: wt[:, kc, :], [self.b_xnT, b_wt])
            if cb < 4:
                self.act(lambda: nc.scalar.activation(out=gvS[0:SPC, cb * 512:(cb + 1) * 512], in_=pm[0:SPC, :], func=AF.Gelu), R=[b_pm], W=[b_hS])
            elif cb < 8:
                self.act(lambda: nc.scalar.activation(out=guS[0:SPC, (cb - 4) * 512:(cb - 3) * 512], in_=pm[0:SPC, :], func=AF.Gelu), R=[b_pm], W=[b_guS])
            else:
                self.act(lambda: nc.scalar.activation(out=szS_[0:SPC, (cb - 8) * 512:(cb - 7) * 512], in_=pm[0:SPC, :], func=AF.Silu), R=[b_pm], W=[b_szS])
        stt = small[:, 0:24].rearrange("p (a b) -> p a b", b=6)
        for cb in range(4):
            self.dve(lambda: nc.vector.bn_stats(out=stt[0:SPC, cb, :], in_=gvS[0:SPC, cb * 512:(cb + 1) * 512]), R=[b_hS], W=[b_small])
        self.dve(lambda: nc.vector.bn_aggr(out=small[0:SPC, 24:26], in_=stt[0:SPC, :, :]), R=[b_small], W=[b_small])
        self.dve(lambda: nc.vector.tensor_scalar(out=small[0:SPC, 26:27], in0=small[0:SPC, 25:26], scalar1=EPS, scalar2=None, op0=ALU.add), R=[b_small], W=[b_small])
        self.act(lambda: nc.scalar.activation(out=small[0:SPC, 26:27], in_=small[0:SPC, 26:27], func=AF.Sqrt), R=[b_small], W=[b_small])
        self.dve(lambda: nc.vector.reciprocal(out=small[0:SPC, 27:28], in_=small[0:SPC, 26:27]), R=[b_small], W=[b_small])
        self.dve(lambda: nc.vector.tensor_scalar(out=gvS[0:SPC, :], in0=gvS[0:SPC, :], scalar1=small[0:SPC, 24:25], scalar2=small[0:SPC, 27:28],
                                                 op0=ALU.subtract, op1=ALU.mult), R=[b_hS, b_small], W=[b_hS])
        self.dve(lambda: nc.vector.tensor_tensor(out=gvS[0:SPC, :], in0=gvS[0:SPC, :], in1=lngS[0:SPC, :], op=ALU.mult), R=[b_hS], W=[b_hS])
        self.ld(lngS[0:SPC, :], self.a_ln_b.partition_broadcast(SPC), R=[b_hS], W=[b_hS], q="pool")
        self.dve(lambda: nc.vector.tensor_tensor(out=gvS[0:SPC, :], in0=gvS[0:SPC, :], in1=lngS[0:SPC, :], op=ALU.add), R=[b_hS], W=[b_hS])
        self.st(self.o_chv[:, :], gvS[0:SPC, :], R=[b_hS], q="pool")
        mixS = hS[:, 2048:4096]
        for g in range(16):
            self.dve(lambda: nc.vector.tensor_scalar(out=mixS[0:SPC, g * 128:(g + 1) * 128], in0=gvS[0:SPC, g * 128:(g + 1) * 128], scalar1=w00[0:SPC, 0, g:g + 1],
                                                     scalar2=w00[0:SPC, 1, g:g + 1], op0=ALU.mult, op1=ALU.add), R=[b_hS, b_w00], W=[b_hS])
        self.dve(lambda: nc.vector.tensor_tensor(out=mixS[0:SPC, :], in0=mixS[0:SPC, :], in1=guS[0:SPC, :], op=ALU.mult), R=[b_hS, b_guS], W=[b_hS])
        ySb, b_ySb = KAf[:, 0:2048], b_KAf
        self.dve(lambda: nc.vector.tensor_tensor(out=ySb[0:SPC, :], in0=mixS[0:SPC, :], in1=szS_[0:SPC, :], op=ALU.mult), R=[b_hS, b_szS], W=[b_ySb])
        yTS, b_yTS = cv.H("yTS", [128, 16, SPC])
        pt, b_pt = self.next_ptr()
        for fc in range(16):
            self.pe(lambda: nc.tensor.transpose(out=pt[:, fc * SPC:(fc + 1) * SPC], in_=ySb[0:SPC, fc * 128:(fc + 1) * 128], identity=self.ident[0:SPC, 0:SPC]),
                    R=[b_ySb, self.b_ident], W=[b_pt], signal=(fc == 15))
        self.act(lambda: nc.scalar.copy(out=yTS[:, :, :], in_=pt[:, 0:16 * SPC].rearrange("p (a b) -> p a b", b=SPC)), R=[b_pt], W=[b_yTS])
        for cb in range(2):
            pm, b_pm = self.next_pm()
            for fcg in range(4):
                wt, b_wt = self.next_wst()
                self.ld(wt[:, 0:4, :], self.wA_out[fcg * 2 + cb], W=[b_wt])
                for fl in range(4):
                    fc = fcg * 4 + fl
                    self.pe(lambda: nc.tensor.matmul(pm[0:SPC, :], lhsT=yTS[:, fc, :], rhs=wt[:, fl, :], start=(fc == 0), stop=(fc == 15)),
                            R=[b_yTS, b_wt], W=[b_pm], signal=(fl == 3))
            self.dve(lambda: nc.vector.tensor_tensor(out=self.x_tm[0:SPC, 0, cb * 512:(cb + 1) * 512], in0=self.x_tm[0:SPC, 0, cb * 512:(cb + 1) * 512],
                                                     in1=pm[0:SPC, :], op=ALU.add), R=[b_pm, self.b_x], W=[self.b_x])

        self.rms_T(SPC, SPC)
        self.xhat_T(SPC, SPC)
        slabs = [(self.wB_q[0], 0, None), (self.wB_q[1], 512, None)]
        slabs += [(self.wB_kv[k], 1024 + 512 * k, None) for k in range(3)]
        slabs += [(self.wB_z[k], 512 * k, AF.Silu) for k in range(6)]
        for (src, c0, fn) in slabs:
            wt, b_wt = self.next_wst()
            self.ld(wt[:], src, W=[b_wt])
            pm, b_pm = self.next_pm()
            self.mm8(pm[0:SPC, :], b_pm, lambda kc: self.xnT[:, kc, 0:SPC], lambda kc: wt[:, kc, :], [self.b_xnT, b_wt])
            if fn is None:
                self.act(lambda: nc.scalar.copy(out=hS[0:SPC, c0:c0 + 512], in_=pm[0:SPC, :]), R=[b_pm], W=[b_hS])
            else:
                self.act(lambda: nc.scalar.activation(out=szb[0:SPC, c0:c0 + 512], in_=pm[0:SPC, :], func=fn), R=[b_pm], W=[b_szb])
        pm, b_pm = self.next_pm()
        self.mm8(pm[0:SPC, 0:48], b_pm, lambda kc: self.xnT[:, kc, 0:SPC], lambda kc: self.wgl[:, kc, :], [self.b_xnT, self.b_wgl])
        self.dve(lambda: nc.vector.tensor_tensor(out=hS[0:SPC, 2560:2608], in0=pm[0:SPC, 0:48], in1=gbrow[0:SPC, :], op=ALU.add), R=[b_pm, b_gbrow], W=[b_hS])
        self.act(lambda: nc.scalar.activation(out=hS[0:SPC, 2560:2608], in_=hS[0:SPC, 2560:2608], func=AF.Sigmoid), R=[b_hS], W=[b_hS])
        qv = hS[0:SPC, 0:1024].rearrange("s (p a r d) -> s p a r d", p=2, a=2, r=4)
        for p in range(2):
            self.dve(lambda: nc.vector.tensor_copy(out=qb2[0:SPC, 0, p * 512:(p + 1) * 512].rearrange("s (r a d) -> s r a d", r=4, a=2),
                                                   in_=qv[:, p, :, :, :].rearrange("s a r d -> s r a d")), R=[b_hS], W=[b_qb2])
        def rope_tm(view, nh):
            x1 = view[:, :, 0:8]
            x2 = view[:, :, 8:16]
            cs = ropeS[0:SPC, 0, 0:nh, :]
            sn = ropeS[0:SPC, 1, 0:nh, :]
            R_ = [b_hS, b_ropeS]
            self.dve(lambda: nc.vector.tensor_tensor(out=rtS[0:SPC, 0, 0:nh, :], in0=x1, in1=cs, op=ALU.mult), R=R_, W=[b_rtS])
            self.dve(lambda: nc.vector.tensor_tensor(out=rtS[0:SPC, 1, 0:nh, :], in0=x2, in1=sn, op=ALU.mult), R=R_, W=[b_rtS])
            self.dve(lambda: nc.vector.tensor_tensor(out=rtS[0:SPC, 2, 0:nh, :], in0=x2, in1=cs, op=ALU.mult), R=R_, W=[b_rtS])
            self.dve(lambda: nc.vector.tensor_tensor(out=rtS[0:SPC, 3, 0:nh, :], in0=x1, in1=sn, op=ALU.mult), R=R_, W=[b_rtS])
            self.dve(lambda: nc.vector.tensor_tensor(out=x1, in0=rtS[0:SPC, 0, 0:nh, :], in1=rtS[0:SPC, 1, 0:nh, :], op=ALU.subtract), R=[b_rtS], W=[b_hS])
            self.dve(lambda: nc.vector.tensor_tensor(out=x2, in0=rtS[0:SPC, 2, 0:nh, :], in1=rtS[0:SPC, 3, 0:nh, :], op=ALU.add), R=[b_rtS], W=[b_hS])
        rope_tm(hS[0:SPC, 0:1024].rearrange("s (h d) -> s h d", d=64), 16)
        rope_tm(hS[0:SPC, 1536:1792].rearrange("s (h d) -> s h d", d=64), 4)
        rope_tm(hS[0:SPC, 2048:2304].rearrange("s (h d) -> s h d", d=64), 4)
        for p in range(2):
            self.dve(lambda: nc.vector.tensor_copy(out=qb2[0:SPC, 1, p * 512:(p + 1) * 512].rearrange("s (r a d) -> s r a d", r=4, a=2),
                                                   in_=qv[:, p, :, :, :].rearrange("s a r d -> s r a d")), R=[b_hS], W=[b_qb2])
        self.st(self.o_cmp_s[:, :], hS[0:SPC, 1024:1536], R=[b_hS], q="pool")
        self.st(self.o_sel_s[:, :], hS[0:SPC, 1536:2048], R=[b_hS], q="pool")
        self.st(self.o_win_s[:, 511, :], hS[0:SPC, 2048:2560], R=[b_hS], q="pool")
        for s in range(SPC):
            self.dma(lambda: nc.sync.dma_start(out=self.o_win_s[s, 0:511, :], in_=self.state_w[s, 1:512, :]), q="sp", out=True)
        self.dve(lambda: nc.vector.tensor_copy(out=knb[0:SPC, 0, :], in_=hS[0:SPC, 1536:1792]), R=[b_hS], W=[b_knb])
        self.dve(lambda: nc.vector.tensor_copy(out=knb[0:SPC, 1, :], in_=hS[0:SPC, 2048:2304]), R=[b_hS], W=[b_knb])
        self.dve(lambda: nc.vector.tensor_copy(out=vnewA[0:SPC, 0, :, 0:64], in_=hS[0:SPC, 1792:2048].rearrange("s (g d) -> s g d", d=64)), R=[b_hS], W=[b_vnewA])
        self.dve(lambda: nc.vector.tensor_copy(out=vnewA[0:SPC, 1, :, 0:64], in_=hS[0:SPC, 2304:2560].rearrange("s (g d) -> s g d", d=64)), R=[b_hS], W=[b_vnewA])
        idS = self.ident[0:SPC, 0:SPC]
        for v in range(2):
            pt, b_pt = self.next_ptr()
            for k in range(8):
                self.pe(lambda: nc.tensor.transpose(out=pt[:, k * SPC:(k + 1) * SPC], in_=qb2[0:SPC, v, k * 128:(k + 1) * 128], identity=idS),
                        R=[b_qb2, self.b_ident], W=[b_pt], signal=(k == 7))
            self.act(lambda: nc.scalar.copy(out=qTS[:, v, 0:8, :], in_=pt[:, 0:8 * SPC].rearrange("p (a b) -> p a b", b=SPC)), R=[b_pt], W=[b_qTS])
        pt, b_pt = self.next_ptr()
        for v in range(2):
            for p in range(2):
                k = v * 2 + p
                self.pe(lambda: nc.tensor.transpose(out=pt[:, k * SPC:(k + 1) * SPC], in_=knb[0:SPC, v, p * 128:(p + 1) * 128], identity=idS),
                        R=[b_knb, self.b_ident], W=[b_pt], signal=(k == 3))
        self.act(lambda: nc.scalar.copy(out=kTn[:, :, :, :].rearrange("p a b c -> p (a b) c"), in_=pt[:, 0:4 * SPC].rearrange("p (a b) -> p a b", b=SPC)),
                 R=[b_pt], W=[b_kTn])
        pt, b_pt = self.next_ptr()
        for k in range(48):
            self.pe(lambda: nc.tensor.transpose(out=pt[0:64, k * SPC:(k + 1) * SPC], in_=szb[0:SPC, k * 64:(k + 1) * 64], identity=idS),
                    R=[b_szb, self.b_ident], W=[b_pt], signal=(k == 47))
        self.act(lambda: nc.scalar.copy(out=szTS[0:64, :, :], in_=pt[0:64, 0:48 * SPC].rearrange("p (a b) -> p a b", b=SPC)), R=[b_pt], W=[b_szTS])
        S.barrier()

        KCsv = KCs
        for s in range(SPC):
            self.ld(ptbi_t[:], self.page_t[s].partition_broadcast(128), W=[b_ptbi], q="pool")
            self.dve(lambda: nc.vector.tensor_copy(out=idxf, in_=ptbi_t[:]), R=[b_ptbi], W=[b_idxf])
            self.dve(lambda: nc.vector.tensor_scalar(out=idxf, in0=idxf, scalar1=128.0, scalar2=iopf[:, 0:1], op0=ALU.mult, op1=ALU.add), R=[b_idxf, b_iopf], W=[b_idxf])
            self.dve(lambda: nc.vector.tensor_copy(out=idxi_t[:], in_=idxf), R=[b_idxf], W=[b_idxi])
            self.ld(vrow0[0:1, :, :, :], vnewA[s:s + 1, :, :, :], R=[b_vnewA], W=[b_vrow0], q="pool")
            self.pool(lambda: nc.gpsimd.memset(VCs[:, :, :, 0:64], 0.0), W=[b_VCs])
            self.pool(lambda: nc.gpsimd.memset(VCs[:, :, :, 64:128], 1.0), W=[b_VCs])
            self.ld(VCs[127:128, 7, :, :].rearrange("p a b -> p (a b)"), zrow[0:1, :], R=[b_zrow], W=[b_VCs], q="pool")
            self.pool(lambda: nc.gpsimd.memset(KCsv[:, :, :], 0.0), W=[b_KCs])
            if s == 0:
                self.pool(lambda: nc.gpsimd.memset(VSs[:, :, :, 64:128], 1.0), W=b_vss)
            self.pool(lambda: nc.gpsimd.memset(VWs[:, :, :, 64:128], 1.0), W=[b_VWs])
            deferred = []
            for pgp in range(8):
                for pl in range(16):
                    page = pgp * 16 + pl
                    pb, b_pb = pgb[page % 4]
                    self.dma(lambda: nc.gpsimd.indirect_dma_start(out=pb, out_offset=None, in_=self.cache_c[:, :],
                                                                  in_offset=bass.IndirectOffsetOnAxis(ap=idxi_t[:, page:page + 1], axis=0)),
                             R=[b_idxi], W=[b_pb], q="pool")
                    pt, b_pt = self.next_ptr()
                    for tl in range(4):
                        self.pe(lambda: nc.tensor.transpose(out=pt[:, tl * 128:(tl + 1) * 128], in_=pb[:, tl * 128:(tl + 1) * 128], identity=self.ident[:]),
                                R=[b_pb, self.b_ident], W=[b_pt], signal=(tl == 3))
                    if page % 2 == 0:
                        self.act(lambda: nc.scalar.copy(out=KRAWs[:, :, 16 + pl * 128:16 + (pl + 1) * 128], in_=pt[:, 0:512].rearrange("p (a b) -> p a b", b=128)),
                                 R=[b_pt], W=[b_kraw[pl]])
                    else:
                        self.dve(lambda: nc.vector.tensor_copy(out=KRAWs[:, :, 16 + pl * 128:16 + (pl + 1) * 128], in_=pt[:, 0:512].rearrange("p (a b) -> p a b", b=128)),
                                 R=[b_pt], W=[b_kraw[pl]])
                while deferred:
                    deferred.pop(0)()
                nblk = 127 if pgp == 0 else 128
                base = 16 if pgp == 0 else 0
                m0 = 0 if pgp == 0 else 128 * pgp - 1
                pmh, b_pmh = self.next_pm()
                for kv in range(2):
                    wt, b_wt = self.next_wst()
                    w1v = wt[:].rearrange("p a b -> p (a b)").rearrange("p (j c) -> p j c", c=128)
                    self.ld(w1v, self.w1bd[kv], W=[b_wt])
                    for hp in range(2):
                        tl = kv * 2 + hp
                        for j in range(32):
                            self.pe(lambda: nc.tensor.matmul(pmh[:, tl * 128:tl * 128 + nblk], lhsT=w1v[:, j, :],
                                                             rhs=KRAWs[:, tl, base + j:base + j + 16 * (nblk - 1) + 1:16],
                                                             start=(tl == 0 and j == 0), stop=(j == 31), skip_group_check=True),
                                    R=[b_wt, b_carry] + b_kraw, W=[b_pmh], signal=(j == 31))
                for kv in range(2):
                    self.act(lambda: nc.scalar.activation(out=hidTs[:, 2 * kv:2 * kv + 2, 0:nblk],
                                                          in_=pmh[:, 256 * kv:256 * kv + 256].rearrange("p (a b) -> p a b", b=128)[:, :, 0:nblk],
                                                          func=AF.Silu, bias=self.bias1[:, kv:kv + 1]), R=[b_pmh, self.b_bias1], W=[b_hidTs])
                pmk, b_pmk = self.next_pm()
                for g in range(4):
                    self.pe(lambda: nc.tensor.matmul(pmk[:, g * 128:g * 128 + nblk], lhsT=self.W2K[:, g % 2, :], rhs=hidTs[:, g // 2, 0:nblk],
                                                     start=(g == 0), stop=True, skip_group_check=True), R=[self.b_W2K, b_hidTs], W=[b_pmk], signal=(g == 3))
                self.act(lambda: nc.scalar.activation(out=KCsv[:, :, m0:m0 + nblk], in_=pmk[:, :].rearrange("p (a b) -> p a b", b=128)[:, :, 0:nblk],
                                                      func=AF.Identity, bias=self.b2K[:, 0:1]), R=[b_pmk, self.b_b2K], W=[b_KCs])
                pmv, b_pmv = self.next_pm()
                for hp in range(2):
                    self.pe(lambda: nc.tensor.matmul(pmv[0:nblk, hp * 128:(hp + 1) * 128], lhsT=hidTs[:, 2 + hp, 0:nblk], rhs=self.W2V[:, :],
                                                     start=(hp == 0), stop=False, skip_group_check=True), R=[b_hidTs, self.b_W2V], W=[b_pmv], signal=False)
                    self.pe(lambda: nc.tensor.matmul(pmv[0:nblk, hp * 128:(hp + 1) * 128], lhsT=self.ones_row[0:1, 0:nblk], rhs=self.b2Vrow[0:1, :],
                                                     start=False, stop=True, skip_group_check=True), R=[self.b_ones, self.b_b2V], W=[b_pmv], signal=(hp == 1))
                vsts, b_vsts_k = vsts2[pgp % 2]
                self.act(lambda: nc.scalar.copy(out=vsts[0:nblk, :, :], in_=pmv[0:nblk, 0:256].rearrange("p (a b) -> p a b", b=128)), R=[b_pmv], W=[b_vsts_k])
                vv = vsts[:, :, :].rearrange("p a (h d) -> p (a h) d", d=64)
                def place(pgp=pgp, vv=vv, b_src=b_vsts_k):
                    if pgp == 0:
                        self.ld(VCs[0:127, 0, :, 0:64], vv[0:127], R=[b_src], W=[b_VCs], q="pool")
                    else:
                        self.ld(VCs[127:128, pgp - 1, :, 0:64], vv[0:1], R=[b_src], W=[b_VCs], q="pool")
                        self.ld(VCs[0:127, pgp, :, 0:64], vv[1:128], R=[b_src], W=[b_VCs], q="pool")
                deferred.append(place)
                self.dve(lambda: nc.vector.tensor_copy(out=KRAWs[:, :, 0:16], in_=KRAWs[:, :, 2048:2064]), R=[b_kraw[15]], W=[b_carry])
            while deferred:
                deferred.pop(0)()
            pmA, b_pmA = self.next_pm()
            pmB, b_pmB = self.next_pm()
            banks = [(pmA, b_pmA), (pmB, b_pmB)]
            for nt in range(8):
                for g in range(4):
                    p, a = g // 2, g % 2
                    pmx, b_pmx = banks[a]
                    col = (nt * 2 + p) * 4
                    self.pe(lambda: nc.tensor.matmul(pmx[:, col:col + 4], lhsT=KCsv[a * 64:(a + 1) * 64, g, nt * 128:(nt + 1) * 128],
                                                     rhs=qTS[a * 64:(a + 1) * 64, 0, p * 4:(p + 1) * 4, s], start=(nt == 0 and p == 0), stop=True, skip_group_check=True),
                            R=[b_KCs, b_qTS], W=[b_pmx], signal=(nt == 7 and p == 1))
            PTcv = PTc[:, :, :].rearrange("p n (q a r) -> p n q a r", q=2, a=2)
            for a in range(2):
                pmx, b_pmx = banks[a]
                self.act(lambda: nc.scalar.activation(out=PTcv[:, :, :, a, :], in_=pmx[:, 0:64].rearrange("p (n q r) -> p n q r", q=2, r=4), func=AF.Exp, scale=0.125),
                         R=[b_pmx], W=[b_PTc])
            for g in range(4):
                for nt in range(8):
                    self.pe(lambda: nc.tensor.matmul(po0[:, g * 4:(g + 1) * 4], lhsT=VCs[:, nt, g, :], rhs=PTc[:, nt, g * 4:(g + 1) * 4],
                                                     start=(g == 0 and nt == 0), stop=(nt == 7), skip_group_check=True), R=[b_VCs, b_PTc], W=[b_po0], signal=(nt == 7))
            for nt in range(8):
                self.pe(lambda: nc.tensor.matmul(px[0:16, 0:258], lhsT=PTc[:, nt, :], rhs=ovS[:, nt, :], start=(nt == 0), stop=(nt == 7)),
                        R=[b_PTc, b_ovS], W=[b_px], signal=(nt == 7))
            self.dve(lambda: nc.vector.tensor_scalar(out=small[0:16, 32:33], in0=px[0:16, 257:258], scalar1=1e-30, scalar2=None, op0=ALU.max), R=[b_px], W=[b_small])
            self.dve(lambda: nc.vector.reciprocal(out=small[0:16, 33:34], in_=small[0:16, 32:33]), R=[b_small], W=[b_small])
            self.dve(lambda: nc.vector.tensor_scalar(out=wgt[0:16, 0:256], in0=px[0:16, 0:256], scalar1=small[0:16, 33:34], scalar2=None, op0=ALU.mult), R=[b_px, b_small], W=[b_wgt])
            self.pe(lambda: nc.tensor.matmul(px[0:4, 0:256], lhsT=grps[0:16, :], rhs=wgt[0:16, 0:256], start=True, stop=True), R=[b_grps, b_wgt], W=[b_px])
            self.dve(lambda: nc.vector.tensor_tensor(out=scoreS[0:4, :], in0=px[0:4, 0:256], in1=forceS[0:4, :], op=ALU.add), R=[b_px, b_forceS], W=[b_scoreS])
            self.dve(lambda: nc.vector.max(out=small[0:4, 40:48], in_=scoreS[0:4, :]), R=[b_scoreS], W=[b_small])
            self.dve(lambda: nc.vector.match_replace(out=scrS[0:4, :], in_to_replace=small[0:4, 40:48], in_values=scoreS[0:4, :], imm_value=-1e30), R=[b_scoreS, b_small], W=[b_scrS])
            self.dve(lambda: nc.vector.max(out=small[0:4, 48:56], in_=scrS[0:4, :]), R=[b_scrS], W=[b_small])
            self.dve(lambda: nc.vector.tensor_scalar(out=mask01[0:4, :], in0=scoreS[0:4, :], scalar1=small[0:4, 54:55], scalar2=None, op0=ALU.is_ge), R=[b_scoreS, b_small], W=[b_mask])
            for g in range(4):
                self.pe(lambda: nc.tensor.matmul(px[:, 0:256], lhsT=oh4[0:4, g, :], rhs=mask01[0:4, :], start=True, stop=True), R=[b_oh4, b_mask], W=[b_px])
                pxv = px[:, 0:256].rearrange("p (k two) -> p k two", two=2)
                for r in range(4):
                    self.dve(lambda: nc.vector.tensor_copy(out=M4[0:64, g, :, r], in_=pxv[0:64, :, 0]), R=[b_px], W=[b_M4])
                    self.act(lambda: nc.scalar.copy(out=M4[64:128, g, :, r], in_=pxv[64:128, :, 1]), R=[b_px], W=[b_M4])

            first = True
            for pgp in range(8):
                for pl in range(16):
                    page = pgp * 16 + pl
                    pb, b_pb = pgb[page % 4]
                    self.dma(lambda: nc.gpsimd.indirect_dma_start(out=pb, out_offset=None, in_=self.cache_s[:, :],
                                                                  in_offset=bass.IndirectOffsetOnAxis(ap=idxi_t[:, page:page + 1], axis=0)),
                             R=[b_idxi], W=[b_pb], q="pool")
                    self.dve(lambda: nc.vector.tensor_copy(out=VSs[:, pl, :, 0:64], in_=pb[:, 256:512].rearrange("p (g d) -> p g d", d=64)), R=[b_pb], W=[b_vss[pl]])
                    pt, b_pt = self.next_ptr()
                    for p in range(2):
                        self.pe(lambda: nc.tensor.transpose(out=pt[:, p * 128:(p + 1) * 128], in_=pb[:, p * 128:(p + 1) * 128], identity=self.ident[:]),
                                R=[b_pb, self.b_ident], W=[b_pt], signal=(p == 1))
                    self.act(lambda: nc.scalar.copy(out=KsTs[:, :, pl * 128:(pl + 1) * 128], in_=pt[:, 0:256].rearrange("p (a b) -> p a b", b=128)), R=[b_pt], W=[b_kst[pl]])
                pmA, b_pmA = self.next_pm()
                pmB, b_pmB = self.next_pm()
                banks = [(pmA, b_pmA), (pmB, b_pmB)]
                for g in range(4):
                    p, a = g // 2, g % 2
                    pmx, b_pmx = banks[a]
                    for pl in range(16):
                        col = (p * 16 + pl) * 4
                        self.pe(lambda: nc.tensor.matmul(pmx[:, col:col + 4], lhsT=KsTs[a * 64:(a + 1) * 64, p, pl * 128:(pl + 1) * 128],
                                                         rhs=qTS[a * 64:(a + 1) * 64, 1, p * 4:(p + 1) * 4, s], start=(p == 0 and pl == 0), stop=True, skip_group_check=True),
                                R=[b_kst[pl], b_qTS], W=[b_pmx], signal=(pl == 15))
                PTsv = PTs[:, :, :, :].rearrange("p (q a) k r -> p q a k r", a=2)
                for a in range(2):
                    pmx, b_pmx = banks[a]
                    self.act(lambda: nc.scalar.activation(out=PTsv[:, :, a, :, :], in_=pmx[:, 0:128].rearrange("p (q k r) -> p q k r", q=2, r=4), func=AF.Exp, scale=0.125),
                             R=[b_pmx], W=[b_PTs])
                self.dve(lambda: nc.vector.tensor_tensor(out=PTs[:, :, :, :], in0=PTs[:, :, :, :], in1=M4[:, :, pgp * 16:(pgp + 1) * 16, :], op=ALU.mult), R=[b_PTs, b_M4], W=[b_PTs])
                for g in range(4):
                    for pl in range(16):
                        self.pe(lambda: nc.tensor.matmul(po1[:, g * 4:(g + 1) * 4], lhsT=VSs[:, pl, g, :], rhs=PTs[:, g, pl, :],
                                                         start=first, stop=False, skip_group_check=True), R=[b_vss[pl], b_PTs], W=[b_po1], signal=(pl == 15))
                        first = False
            def new_token(v, po, b_po, col0, qidx):
                pmA, b_pmA = self.next_pm()
                pmB, b_pmB = self.next_pm()
                bk = [(pmA, b_pmA), (pmB, b_pmB)]
                for g in range(4):
                    p, a = g // 2, g % 2
                    pmx, b_pmx = bk[a]
                    self.pe(lambda: nc.tensor.matmul(pmx[0:1, p * 4:(p + 1) * 4], lhsT=kTn[a * 64:(a + 1) * 64, v, p, s:s + 1],
                                                     rhs=qTS[a * 64:(a + 1) * 64, qidx, p * 4:(p + 1) * 4, s], start=(p == 0), stop=True, skip_group_check=True),
                            R=[b_kTn, b_qTS], W=[b_pmx])
                pv = pnew[0:1, v, :].rearrange("o (q a r) -> o q a r", q=2, a=2)
                for a in range(2):
                    pmx, b_pmx = bk[a]
                    self.act(lambda: nc.scalar.activation(out=pv[:, :, a, :], in_=pmx[0:1, 0:8].rearrange("o (q r) -> o q r", r=4), func=AF.Exp, scale=0.125), R=[b_pmx], W=[b_pnew])
                for g in range(4):
                    self.pe(lambda: nc.tensor.matmul(po[:, col0 + g * 4:col0 + (g + 1) * 4], lhsT=vrow0[0:1, v, g, :], rhs=pnew[0:1, v, g * 4:(g + 1) * 4],
                                                     start=False, stop=True, skip_group_check=True), R=[b_vrow0, b_pnew], W=[b_po], signal=(g == 3))
            new_token(0, po1, b_po1, 0, 1)

            wst32 = hS[:, 0:2048].rearrange("p (k c) -> p k c", c=512)
            self.ld(wst32, self.state_w[s].rearrange("(k p) c -> p k c", p=128), W=[b_hS])
            wkb = szb[:, 0:1024].rearrange("p (k c) -> p k c", c=256)
            self.dve(lambda: nc.vector.tensor_copy(out=wkb, in_=wst32[:, :, 0:256]), R=[b_hS], W=[b_szb])
            for kt in range(4):
                self.act(lambda: nc.scalar.copy(out=VWs[:, kt, :, 0:64], in_=wst32[:, kt, 256:512].rearrange("p (g d) -> p g d", d=64)), R=[b_hS], W=[b_VWs])
                pt, b_pt = self.next_ptr()
                for p in range(2):
                    self.pe(lambda: nc.tensor.transpose(out=pt[:, p * 128:(p + 1) * 128], in_=wkb[:, kt, p * 128:(p + 1) * 128], identity=self.ident[:]),
                            R=[b_szb, self.b_ident], W=[b_pt], signal=(p == 1))
                self.act(lambda: nc.scalar.copy(out=KwTs[:, :, kt * 128:(kt + 1) * 128], in_=pt[:, 0:256].rearrange("p (a b) -> p a b", b=128)), R=[b_pt], W=[b_KwTs])
            pmA, b_pmA = self.next_pm()
            pmB, b_pmB = self.next_pm()
            banks = [(pmA, b_pmA), (pmB, b_pmB)]
            for g in range(4):
                p, a = g // 2, g % 2
                pmx, b_pmx = banks[a]
                for kt in range(4):
                    col = (p * 4 + kt) * 4
                    self.pe(lambda: nc.tensor.matmul(pmx[:, col:col + 4], lhsT=KwTs[a * 64:(a + 1) * 64, p, kt * 128:(kt + 1) * 128],
                                                     rhs=qTS[a * 64:(a + 1) * 64, 1, p * 4:(p + 1) * 4, s], start=(p == 0 and kt == 0), stop=True, skip_group_check=True),
                            R=[b_KwTs, b_qTS], W=[b_pmx], signal=(kt == 3))
            PTw = PTs[:, :, 0:4, :]
            PTwv = PTw.rearrange("p (q a) k r -> p q a k r", a=2)
            for a in range(2):
                pmx, b_pmx = banks[a]
                self.act(lambda: nc.scalar.activation(out=PTwv[:, :, a, :, :], in_=pmx[:, 0:32].rearrange("p (q k r) -> p q k r", q=2, r=4), func=AF.Exp, scale=0.125),
                         R=[b_pmx], W=[b_PTs])
            self.dve(lambda: nc.vector.memset(PTs[0:1, :, 0, :], 0.0), W=[b_PTs])
            for g in range(4):
                for kt in range(4):
                    self.pe(lambda: nc.tensor.matmul(po0[:, 16 + g * 4:16 + (g + 1) * 4], lhsT=VWs[:, kt, g, :], rhs=PTs[:, g, kt, :],
                                                     start=False, stop=False, skip_group_check=True), R=[b_VWs, b_PTs], W=[b_po0], signal=(kt == 3))
            new_token(1, po0, b_po0, 16, 1)

            self.pe(lambda: nc.tensor.matmul(px[0:64, 0:48], lhsT=oh4[0:4, s, 0:64], rhs=hS[0:SPC, 2560:2608], start=True, stop=True), R=[b_oh4, b_hS], W=[b_px])
            for bi, (b, po, b_po, c0) in enumerate(((0, po0, b_po0, 0), (1, po1, b_po1, 0), (2, po0, b_po0, 16))):
                self.dve(lambda: nc.vector.tensor_scalar(out=tS[64:128, :], in0=po[64:128, c0:c0 + 16], scalar1=1e-30, scalar2=None, op0=ALU.max), R=[b_po], W=[b_tS])
                self.dve(lambda: nc.vector.reciprocal(out=tS[0:64, :], in_=tS[64:128, :]), R=[b_tS], W=[b_tS])
                self.dve(lambda: nc.vector.tensor_tensor(out=tS[0:64, :], in0=tS[0:64, :], in1=po[0:64, c0:c0 + 16], op=ALU.mult), R=[b_tS, b_po], W=[b_tS])
                self.dve(lambda: nc.vector.tensor_tensor(out=tS[0:64, :], in0=tS[0:64, :], in1=px[0:64, b * 16:(b + 1) * 16], op=ALU.mult), R=[b_tS, b_px], W=[b_tS])
                if bi == 0:
                    self.dve(lambda: nc.vector.tensor_tensor(out=accS[0:64, :], in0=tS[0:64, :], in1=szTS[0:64, b * 16:(b + 1) * 16, s], op=ALU.mult), R=[b_tS, b_szTS], W=[b_accS])
                else:
                    self.dve(lambda: nc.vector.tensor_tensor(out=tS[0:64, :], in0=tS[0:64, :], in1=szTS[0:64, b * 16:(b + 1) * 16, s], op=ALU.mult), R=[b_tS, b_szTS], W=[b_tS])
                    self.dve(lambda: nc.vector.tensor_tensor(out=accS[0:64, :], in0=accS[0:64, :], in1=tS[0:64, :], op=ALU.add), R=[b_tS, b_accS], W=[b_accS])
            self.dve(lambda: nc.vector.tensor_copy(out=yTSall[0:64, :, s], in_=accS[0:64, :]), R=[b_accS], W=[b_yTSall])

        for cb in range(2):
            pm, b_pm = self.next_pm()
            for hh in range(2):
                wt, b_wt = self.next_wst()
                self.ld(wt[0:64, :, :], self.wB_out[cb * 2 + hh], W=[b_wt])
                for hl in range(8):
                    h = hh * 8 + hl
                    self.pe(lambda: nc.tensor.matmul(pm[0:SPC, :], lhsT=yTSall[0:64, h, :], rhs=wt[0:64, hl, :], start=(h == 0), stop=(h == 15)),
                            R=[b_yTSall, b_wt], W=[b_pm], signal=(hl == 7))
            self.dve(lambda: nc.vector.tensor_tensor(out=self.x_tm[0:SPC, 0, cb * 512:(cb + 1) * 512], in0=self.x_tm[0:SPC, 0, cb * 512:(cb + 1) * 512],
                                                     in1=pm[0:SPC, :], op=ALU.add), R=[b_pm, self.b_x], W=[self.b_x])
        self.rms_T(SPC, SPC)
        youtS = hS[:, 2048:3072]
        self.dve(lambda: nc.vector.scalar_tensor_tensor(out=youtS[0:SPC, :], in0=self.x_tm[0:SPC, 0, :], scalar=self.st4[0:SPC, 4:5], in1=self.fgbc[0:SPC, :],
                                                        op0=ALU.mult, op1=ALU.mult), R=[self.b_x, self.b_st4, self.b_fgbc], W=[b_hS])
        self.st(self.y_s[:, :], youtS[0:SPC, :], R=[b_hS], q="pool")

    def build(self, debug_stage=None):
        nc = self.nc
        self.alloc()
        self.prologue()
        for i in range(self.ntiles):
            self.layer_a(i)
            if debug_stage == "A":
                for s in range(NSUB):
                    self.st(self.y_p[i * T + s * 128:i * T + (s + 1) * 128, :], self.x_tm[:, s, :], R=[self.b_x])
                self.S.barrier()
                continue
            self.S.barrier()
            self.layer_b(i)
            self.S.barrier()
        if self.do_sample and debug_stage is None:
            self.sample_phase()
        self.S.finish("sp")
        return nc


def core_inputs(inp, c, n_phys, dummy_cache=False):
    m = {}
    m["x_prompt"] = np.ascontiguousarray(inp["x_prompt"][c])
    m["x_sample"] = np.ascontiguousarray(inp["x_sample"][c * SPC:(c + 1) * SPC, 0, :])
    if dummy_cache:
        m["cache_cmp_kv"] = np.zeros((n_phys * 128, 512), np.float32)
        m["cache_sel_kv"] = np.zeros((n_phys * 128, 512), np.float32)
    else:
        m["cache_cmp_kv"] = inp["cache_cmp_kv"].reshape(n_phys * 128, 512)
        m["cache_sel_kv"] = inp["cache_sel_kv"].reshape(n_phys * 128, 512)
    m["state_win_kv"] = np.ascontiguousarray(inp["state_win_kv"][0, c * SPC:(c + 1) * SPC].reshape(SPC, 512, 512))
    m["page_table"] = np.ascontiguousarray(inp["page_table"][c * SPC:(c + 1) * SPC])
    m["norm_g"] = inp["norm_g"]
    m["final_norm_g"] = inp["final_norm_g"]
    m["a_w_in"] = inp["a_w_in"][0]
    m["a_ln_g"] = inp["a_ln_g"][0]
    m["a_ln_b"] = inp["a_ln_b"][0]
    m["a_w_s"] = inp["a_w_s"][0]
    m["a_b_s"] = inp["a_b_s"][0]
    m["a_w_out"] = inp["a_w_out"][0]
    m["b_w_in"] = inp["b_w_in"][0]
    m["b_cmp_pe"] = inp["b_cmp_pe"][0]
    m["b_cmp_w1"] = inp["b_cmp_w1"][0]
    m["b_cmp_b1"] = inp["b_cmp_b1"][0]
    m["b_cmp_w2"] = inp["b_cmp_w2"][0]
    m["b_cmp_b2"] = inp["b_cmp_b2"][0]
    m["b_gate_b"] = inp["b_gate_b"][0].reshape(48)
    m["b_w_out"] = inp["b_w_out"][0]
    m.update(host_consts())
    return m


_NC_CACHE = {}


def kernel(x_prompt, x_sample, cache_cmp_kv, cache_sel_kv, state_win_kv, page_table,
           norm_g, final_norm_g, a_w_in, a_ln_g, a_ln_b, a_w_s, a_b_s, a_w_out,
           b_w_in, b_cmp_pe, b_cmp_w1, b_cmp_b1, b_cmp_w2, b_cmp_b2, b_gate_b, b_w_out):
    inp = dict(x_prompt=x_prompt, x_sample=x_sample, cache_cmp_kv=cache_cmp_kv, cache_sel_kv=cache_sel_kv,
               state_win_kv=state_win_kv, page_table=page_table, norm_g=norm_g, final_norm_g=final_norm_g,
               a_w_in=a_w_in, a_ln_g=a_ln_g, a_ln_b=a_ln_b, a_w_s=a_w_s, a_b_s=a_b_s, a_w_out=a_w_out,
               b_w_in=b_w_in, b_cmp_pe=b_cmp_pe, b_cmp_w1=b_cmp_w1, b_cmp_b1=b_cmp_b1, b_cmp_w2=b_cmp_w2,
               b_cmp_b2=b_cmp_b2, b_gate_b=b_gate_b, b_w_out=b_w_out)
    inp = {k: np.asarray(v) for k, v in inp.items()}
    n_phys = inp["cache_cmp_kv"].shape[1]
    if n_phys not in _NC_CACHE:
        kb = KB(n_phys=n_phys)
        _NC_CACHE[n_phys] = kb.build()
    nc = _NC_CACHE[n_phys]
    in_maps = [core_inputs(inp, c, n_phys) for c in range(NCORES)]
    res = run_bass_kernel_spmd(nc, in_maps, core_ids=list(range(NCORES)))
    r = res.results
    nb = SPC * NCORES
    y_prompt = np.stack([r[c]["y_prompt"] for c in range(NCORES)], 0)
    y_sample = np.concatenate([r[c]["y_sample"] for c in range(NCORES)], 0).reshape(nb, 1, D)
    cmp_p = np.stack([r[c]["cmp_kv_prompt"] for c in range(NCORES)], 0).reshape(1, NCORES, SEQ, 2, NKV, HD)
    cmp_s = np.concatenate([r[c]["cmp_kv_sample"] for c in range(NCORES)], 0).reshape(1, nb, 1, 2, NKV, HD)
    sel_p = np.stack([r[c]["sel_kv_prompt"] for c in range(NCORES)], 0).reshape(1, NCORES, SEQ, 2, NKV, HD)
    sel_s = np.concatenate([r[c]["sel_kv_sample"] for c in range(NCORES)], 0).reshape(1, nb, 1, 2, NKV, HD)
    win_p = np.stack([r[c]["win_kv_prompt"] for c in range(NCORES)], 0).reshape(1, NCORES, 512, 2, NKV, HD)
    win_s = np.concatenate([r[c]["win_kv_sample"] for c in range(NCORES)], 0).reshape(1, nb, 512, 2, NKV, HD)
    chv = np.concatenate([r[c]["chunk_v_sample"] for c in range(NCORES)], 0).reshape(1, nb, 1, AW)
    outs = (y_prompt, y_sample, cmp_p, cmp_s, sel_p, sel_s, win_p, win_s, chv)
    return tuple(np.ascontiguousarray(o.astype(np.float32)) for o in outs)
```

```python
import math
import numpy as np
import ml_dtypes
import concourse.bass as bass
import concourse.mybir as mybir
from concourse.bass_utils import run_bass_kernel_spmd

F32 = mybir.dt.float32
BF16 = mybir.dt.bfloat16
I32 = mybir.dt.int32
AF = mybir.ActivationFunctionType
ALU = mybir.AluOpType
AX = mybir.AxisListType

D = 1024
SEQ = 4096
T = 256
NSUB = 2
NTILES = SEQ // T
AW = 2048
NH = 16
HD = 64
NKV = 4
B_IN = 5680
PAST = 16384
NPAGE = 128
NEG = -30000.0
EPS = 1e-6
NCORES = 8
SPC = 4


class Buf:
    __slots__ = ("name", "wr", "rd", "excl")

    def __init__(self, name, excl=False):
        self.name = name
        self.wr = None
        self.rd = []
        self.excl = excl


class Sync:
    def __init__(self, nc, same_engine=True, n_dma_sems=40):
        self.nc = nc
        self.eng = {"pe": nc.tensor, "act": nc.scalar, "dve": nc.vector, "pool": nc.gpsimd, "sp": nc.sync}
        self.sem = {e: nc.alloc_semaphore("s_" + e) for e in self.eng}
        self.cnt = {e: 0 for e in self.eng}
        self.pend = {e: False for e in self.eng}
        self.waited = {e: {} for e in self.eng}
        self.same_engine = same_engine
        self.dma_sems = [nc.alloc_semaphore("dq%d" % i) for i in range(n_dma_sems)]
        self.dma_val = [0] * n_dma_sems
        self.dma_rr = 0
        self.out_events = []
        self.n_ins = 0
        self.n_wait = 0

    def _wait(self, e, ev):
        sem, val, src = ev
        if src == e and (e == "pe" or not self.same_engine):
            return
        key = id(sem)
        if self.waited[e].get(key, 0) >= val:
            return
        self.eng[e].wait_ge(sem, val)
        self.n_wait += 1
        self.waited[e][key] = val

    def _deps(self, e, R, W):
        for b in R:
            if b.wr is not None:
                self._wait(e, b.wr)
            if b.excl:
                for ev in b.rd:
                    self._wait(e, ev)
        for b in W:
            if b.wr is not None:
                self._wait(e, b.wr)
            for ev in b.rd:
                self._wait(e, ev)

    def _record(self, ev, R, W):
        for b in R:
            if b.excl:
                b.wr = ev
                b.rd = []
            else:
                b.rd.append(ev)
                if len(b.rd) > 64:
                    b.rd = b.rd[-64:]
        for b in W:
            b.wr = ev
            b.rd = []

    def op(self, e, fn, R=(), W=(), signal=True):
        self._deps(e, R, W)
        ins = fn()
        self.n_ins += 1
        if signal:
            self.cnt[e] += 1
            ins.then_inc(self.sem[e], 1)
            ev = (self.sem[e], self.cnt[e], e)
            self.pend[e] = False
        else:
            ev = (self.sem[e], self.cnt[e] + 1, e)
            self.pend[e] = True
        self._record(ev, R, W)
        return ins

    def dma(self, q, fn, R=(), W=(), is_output=False):
        k = self.dma_rr
        self.dma_rr = (self.dma_rr + 1) % len(self.dma_sems)
        sem = self.dma_sems[k]
        if self.dma_val[k] > 0:
            self._wait(q, (sem, self.dma_val[k], "dma"))
        self._deps(q, R, W)
        ins = fn()
        self.n_ins += 1
        self.dma_val[k] += 16
        ins.then_inc(sem, 16)
        ev = (sem, self.dma_val[k], "dma")
        self._record(ev, R, W)
        if is_output:
            self.out_events.append(ev)
        return ins

    def barrier(self):
        evs = []
        for e in self.eng:
            assert not self.pend[e], "pending unsignalled instruction on " + e
            if self.cnt[e] > 0:
                evs.append((self.sem[e], self.cnt[e], e))
        for k, sem in enumerate(self.dma_sems):
            if self.dma_val[k] > 0:
                evs.append((sem, self.dma_val[k], "dma"))
        for e in self.eng:
            for ev in evs:
                if ev[2] != e:
                    self._wait(e, ev)

    def finish(self, e="sp"):
        for ev in self.out_events:
            self._wait(e, ev)


def _rope_tables():
    half = 8
    freqs = np.exp(-math.log(500000.0) * np.arange(half, dtype=np.float32) * np.float32(2.0 / 16)).astype(np.float32)
    pos = np.arange(SEQ, dtype=np.float32)
    ang = (pos[:, None] * freqs[None, :]).astype(np.float32)
    cos = np.cos(ang).astype(np.float32)
    sin = np.sin(ang).astype(np.float32)
    angs = (np.float32(PAST) * freqs).astype(np.float32)
    return cos, sin, np.cos(angs).astype(np.float32), np.sin(angs).astype(np.float32)


_CONST_CACHE = {}


def host_consts():
    if _CONST_CACHE:
        return _CONST_CACHE
    bf = ml_dtypes.bfloat16
    c = {}
    c["c_ident"] = np.eye(128, dtype=np.float32).astype(bf)
    kk = np.arange(128)[:, None]
    qq = np.arange(128)[None, :]
    c["c_tri"] = np.where(kk <= qq, 0.0, NEG).astype(np.float32).astype(bf)
    c["c_atri"] = np.where(kk > qq, 0.0, NEG).astype(np.float32).astype(bf)
    ps = np.zeros((128, 128), np.float32)
    for m in range(128):
        d = m % 64
        if d < 8:
            ps[m + 8, m] = 1.0
        elif d < 16:
            ps[m - 8, m] = 1.0
    c["c_pswap"] = ps.astype(bf)
    cos, sin, cos_s, sin_s = _rope_tables()
    cosF = np.ones((128, SEQ), np.float32)
    sinF = np.zeros((128, SEQ), np.float32)
    for hh in range(2):
        cosF[hh * 64:hh * 64 + 8] = cos.T
        cosF[hh * 64 + 8:hh * 64 + 16] = cos.T
        sinF[hh * 64:hh * 64 + 8] = -sin.T
        sinF[hh * 64 + 8:hh * 64 + 16] = sin.T
    c["c_cosF"] = cosF
    c["c_sinF"] = sinF
    c["c_costm4"] = np.ascontiguousarray(np.tile(cos[:, None, :], (1, 4, 1)))
    c["c_sintm4"] = np.ascontiguousarray(np.tile(sin[:, None, :], (1, 4, 1)))
    c["c_ropes16"] = np.ascontiguousarray(np.stack([np.tile(cos_s[None, None], (SPC, 16, 1)), np.tile(sin_s[None, None], (SPC, 16, 1))], 0).astype(np.float32))
    ind = np.zeros((64, SEQ), np.float32)
    for j in range(64):
        ind[j, j * 64:(j + 1) * 64] = -NEG
    c["c_ind"] = ind.astype(bf)
    fb = np.zeros((32, 128, 64), np.float32)
    for qt in range(32):
        t = qt * 128 + np.arange(128)
        jq = t // 64
        jj = np.arange(64)[None, :]
        forced = (jj == 0) | (jj == jq[:, None]) | (jj == jq[:, None] - 1)
        fb[qt] = np.where(jj > jq[:, None], -10.0, np.where(forced, 10.0, 0.0))
    c["c_force"] = fb
    cb = np.zeros((32, 2, 128, 128), np.float32)
    for qt in range(32):
        for nt in range(2):
            n = nt * 128 + np.arange(128)[:, None]
            t = qt * 128 + np.arange(128)[None, :]
            cb[qt, nt] = np.where(16 * n + 31 <= t, 0.0, NEG)
    c["c_cbias"] = cb.astype(bf)
    ov = np.zeros((2, 128, 65), np.float32)
    for nt in range(2):
        for p in range(128):
            n = nt * 128 + p
            for j in range(64):
                if 16 * n <= 64 * j + 63 and 16 * n + 31 >= 64 * j:
                    ov[nt, p, j] = 1.0
            ov[nt, p, 64] = 1.0
    c["c_ovaug"] = ov.astype(bf)
    E = np.zeros((48, 48, 64), np.float32)
    for k in range(48):
        E[k, k, :] = 1.0
    c["c_esel"] = E.astype(bf)
    ovs = np.zeros((8, 128, 258), np.float32)
    for nt in range(8):
        for p in range(128):
            n = nt * 128 + p
            if n >= 1023:
                continue
            j0 = max(0, (16 * n - 63 + 63) // 64 - 1)
            for j in range(j0, min(257, j0 + 4)):
                if 16 * n <= 64 * j + 63 and 16 * n + 31 >= 64 * j:
                    ovs[nt, p, j] = 1.0
            ovs[nt, p, 257] = 1.0
    c["c_ovaug_s"] = ovs.astype(bf)
    fs = np.zeros((4, 256), np.float32)
    fs[:, 0] = 10.0
    fs[:, 255] = 10.0
    c["c_force_s"] = fs
    gs = np.zeros((16, 4), np.float32)
    for h in range(16):
        gs[h, h // 4] = 1.0
    c["c_grpsel"] = gs
    oh = np.zeros((4, 4, 128), np.float32)
    for g in range(4):
        oh[g, g, :] = 1.0
    c["c_onehot4"] = oh
    _CONST_CACHE.update(c)
    return c


CONST_SPECS = {
    "c_ident": ([128, 128], BF16), "c_tri": ([128, 128], BF16), "c_atri": ([128, 128], BF16),
    "c_pswap": ([128, 128], BF16), "c_cosF": ([128, SEQ], F32), "c_sinF": ([128, SEQ], F32),
    "c_costm4": ([SEQ, 4, 8], F32), "c_sintm4": ([SEQ, 4, 8], F32), "c_ropes16": ([2, SPC, 16, 8], F32),
    "c_ind": ([64, SEQ], BF16), "c_force": ([32, 128, 64], F32), "c_cbias": ([32, 2, 128, 128], BF16),
    "c_ovaug": ([2, 128, 65], BF16), "c_esel": ([48, 48, 64], BF16), "c_ovaug_s": ([8, 128, 258], BF16),
    "c_force_s": ([4, 256], F32), "c_grpsel": ([16, 4], F32), "c_onehot4": ([4, 4, 128], F32),
}


class KB:
    def __init__(self, n_phys, ntiles=NTILES, do_sample=True):
        self.n_phys = n_phys
        self.ntiles = ntiles
        self.do_sample = do_sample
        self.nc = nc = bass.Bass("TRN2", target_bir_lowering=False)
        self.S = Sync(nc)
        self.sb_bytes = 0
        self._uid = 0
        di = lambda n, s, d: nc.dram_tensor(n, list(s), d, kind="ExternalInput").ap()
        do = lambda n, s, d: nc.dram_tensor(n, list(s), d, kind="ExternalOutput").ap()
        self.x_p = di("x_prompt", [SEQ, D], F32)
        self.x_s = di("x_sample", [SPC, D], F32)
        self.cache_c = di("cache_cmp_kv", [n_phys * 128, 512], F32)
        self.cache_s = di("cache_sel_kv", [n_phys * 128, 512], F32)
        self.state_w = di("state_win_kv", [SPC, 512, 512], F32)
        self.page_t = di("page_table", [SPC, NPAGE], I32)
        self.norm_g = di("norm_g", [2, D], F32)
        self.final_g = di("final_norm_g", [D], F32)
        self.a_w_in = di("a_w_in", [D, 3 * AW], F32)
        self.a_ln_g = di("a_ln_g", [AW], F32)
        self.a_ln_b = di("a_ln_b", [AW], F32)
        self.a_w_s = di("a_w_s", [16, 128, 128], F32)
        self.a_b_s = di("a_b_s", [16, 128], F32)
        self.a_w_out = di("a_w_out", [AW, D], F32)
        self.b_w_in = di("b_w_in", [D, B_IN], F32)
        self.b_pe = di("b_cmp_pe", [2, 32, 64], F32)
        self.b_w1 = di("b_cmp_w1", [2, 32, 64, 64], F32)
        self.b_b1 = di("b_cmp_b1", [2, 64], F32)
        self.b_w2 = di("b_cmp_w2", [2, 64, 64], F32)
        self.b_b2 = di("b_cmp_b2", [2, 64], F32)
        self.b_gate = di("b_gate_b", [48], F32)
        self.b_w_out = di("b_w_out", [D, D], F32)
        self.cst = {k: di(k, s, d) for k, (s, d) in CONST_SPECS.items()}
        self.y_p = do("y_prompt", [SEQ, D], F32)
        self.y_s = do("y_sample", [SPC, D], F32)
        self.o_cmp_p = do("cmp_kv_prompt", [SEQ, 512], F32)
        self.o_cmp_s = do("cmp_kv_sample", [SPC, 512], F32)
        self.o_sel_p = do("sel_kv_prompt", [SEQ, 512], F32)
        self.o_sel_s = do("sel_kv_sample", [SPC, 512], F32)
        self.o_win_p = do("win_kv_prompt", [512, 512], F32)
        self.o_win_s = do("win_kv_sample", [SPC, 512, 512], F32)
        self.o_chv = do("chunk_v_sample", [SPC, AW], F32)
        dt = lambda n, s: nc.dram_tensor(n, list(s), BF16).ap()
        self.wA_in = dt("wA_in", [12, 128, 8, 512])
        self.wA_out = dt("wA_out", [8, 128, 4, 512])
        self.wB_q = dt("wB_q", [2, 128, 8, 512])
        self.wB_kv = dt("wB_kv", [3, 128, 8, 512])
        self.wB_z = dt("wB_z", [6, 128, 8, 512])
        self.wB_gl = dt("wB_gl", [128, 8, 48])
        self.wB_out = dt("wB_out", [4, 64, 8, 512])
        self.w1bd = dt("w1bd", [2, 128, 32, 128])
        self.b_wscr = {k: Buf("wscr_" + k) for k in ["A_in", "A_out", "B_q", "B_kv", "B_z", "B_gl", "B_out", "w1bd"]}

    def sb(self, name, shape, dtype, excl=False):
        t = self.nc.alloc_sbuf_tensor(name, list(shape), dtype)
        n = 1
        for s in shape[1:]:
            n *= s
        self.sb_bytes += n * (2 if dtype == BF16 else 4)
        return t, Buf(name, excl)

    def nb(self, name):
        self._uid += 1
        return Buf("%s_%d" % (name, self._uid))

    def pe(self, fn, R=(), W=(), signal=True):
        return self.S.op("pe", fn, R, W, signal)

    def act(self, fn, R=(), W=()):
        return self.S.op("act", fn, R, W)

    def dve(self, fn, R=(), W=()):
        return self.S.op("dve", fn, R, W)

    def pool(self, fn, R=(), W=()):
        return self.S.op("pool", fn, R, W)

    def dma(self, fn, R=(), W=(), q="sp", out=False):
        return self.S.dma(q, fn, R, W, is_output=out)

    def ld(self, out_ap, in_ap, R=(), W=(), q="sp", slow=False):
        nc = self.nc
        eng = nc.sync if q == "sp" else nc.gpsimd
        if slow:
            return self.dma(lambda: eng.dma_start(out=out_ap, in_=in_ap, allow_slow_non_contiguous=True), R, W, q)
        return self.dma(lambda: eng.dma_start(out=out_ap, in_=in_ap), R, W, q)

    def st(self, out_ap, in_ap, R=(), W=(), q="sp"):
        nc = self.nc
        eng = nc.sync if q == "sp" else nc.gpsimd
        return self.dma(lambda: eng.dma_start(out=out_ap, in_=in_ap), R, W, q, out=True)

    def alloc(self):
        nc = self.nc
        sb = self.sb
        self.KA, self.b_KA = sb("KA", [128, 4, SEQ], BF16)
        self.VS, self.b_VS = sb("VS", [128, 32, 4, 128], BF16)
        self.KW, self.b_KW = sb("KW", [128, 4, 6 * 128], BF16)
        self.VW, self.b_VW = sb("VW", [128, 6, 4, 128], BF16)
        self.KCT, self.b_KCT = sb("KCT", [128, 4, 256], BF16)
        self.VC, self.b_VC = sb("VC", [128, 2, 4, 128], BF16)
        self.KRAW, self.b_KRAW = sb("KRAW", [128, 4, 16 + T], BF16)
        self.ident, self.b_ident = sb("ident", [128, 128], BF16)
        self.tri, self.b_tri = sb("tri", [128, 128], BF16)
        self.atri, self.b_atri = sb("atri", [128, 128], BF16)
        self.pswap, self.b_pswap = sb("pswap", [128, 128], BF16)
        self.wmT, self.b_wmT = sb("wmT", [128, 16, 128], BF16)
        self.Bmix, self.b_Bmix = sb("Bmix", [128, 16, 128], F32)
        self.lng, self.b_lng = sb("lng", [128, 16], F32)
        self.W2K, self.b_W2K = sb("W2K", [128, 2, 128], BF16)
        self.W2V, self.b_W2V = sb("W2V", [128, 128], BF16)
        self.b2K, self.b_b2K = sb("b2K", [128, 1], F32)
        self.b2Vrow, self.b_b2V = sb("b2Vrow", [1, 128], BF16)
        self.ones_row, self.b_ones = sb("ones_row", [1, 128], BF16)
        self.bias1, self.b_bias1 = sb("bias1", [128, 2], F32)
        self.esel, self.b_esel = sb("esel", [48, 48, 64], BF16)
        self.ovaug, self.b_ovaug = sb("ovaug", [128, 2, 65], BF16)
        self.fgbc, self.b_fgbc = sb("fgbc", [128, D], F32)
        self.gateb, self.b_gateb = sb("gateb", [48, 1], F32)
        self.gcol, self.b_gcol = sb("gcol", [128, 2, 8], F32)
        self.wgl, self.b_wgl = sb("wgl", [128, 8, 48], BF16)
        self.x_tm, self.b_x = sb("x_tm", [128, NSUB, D], F32)
        self.xhat, self.b_xhat = sb("xhat", [128, NSUB, D], BF16)
        self.xnT, self.b_xnT = sb("xnT", [128, 8, T], BF16)
        self.wst = []
        for i in range(2):
            t, b = sb("wst%d" % i, [128, 8, 512], BF16)
            self.wst.append((t, b))
        self.wst_i = 0
        self.st4, self.b_st4 = sb("st4", [128, 16], F32)
        self.AH_N = 19200
        self.AF_N = 7000
        self.arenaH, _ = sb("arenaH", [128, self.AH_N], BF16)
        self.arenaF, _ = sb("arenaF", [128, self.AF_N], F32)
        self.pm = []
        for i in range(3):
            self.pm.append((nc.alloc_psum_tensor("pm%d" % i, [128, 512], F32), Buf("pm%d" % i, True)))
        self.pm_i = 0
        self.ptr = []
        for i in range(2):
            self.ptr.append((nc.alloc_psum_tensor("ptr%d" % i, [128, 1024], BF16), Buf("ptr%d" % i, True)))
        self.ptr_i = 0
        self.po = []
        for i in range(2):
            self.po.append((nc.alloc_psum_tensor("po%d" % i, [128, 512], F32), Buf("po%d" % i, True)))
        self.px = (nc.alloc_psum_tensor("px", [128, 512], F32), Buf("px", True))

    def next_pm(self):
        r = self.pm[self.pm_i]
        self.pm_i = (self.pm_i + 1) % len(self.pm)
        return r

    def next_ptr(self):
        r = self.ptr[self.ptr_i]
        self.ptr_i = (self.ptr_i + 1) % len(self.ptr)
        return r

    def next_wst(self):
        r = self.wst[self.wst_i]
        self.wst_i = (self.wst_i + 1) % len(self.wst)
        return r

    class Carver:
        def __init__(self, kb):
            self.kb = kb
            self.h = 0
            self.f = 0

        def H(self, name, shape):
            n = 1
            for s in shape[1:]:
                n *= s
            assert self.h + n <= self.kb.AH_N, ("arenaH overflow", name, self.h + n)
            ap = self.kb.arenaH[0:shape[0], self.h:self.h + n]
            self.h += n
            return self._shape(ap, shape), self.kb.nb(name)

        def F(self, name, shape):
            n = 1
            for s in shape[1:]:
                n *= s
            assert self.f + n <= self.kb.AF_N, ("arenaF overflow", name, self.f + n)
            ap = self.kb.arenaF[0:shape[0], self.f:self.f + n]
            self.f += n
            return self._shape(ap, shape), self.kb.nb(name)

        @staticmethod
        def _shape(ap, shape):
            if len(shape) == 2:
                return ap
            if len(shape) == 3:
                return ap.rearrange("p (a b) -> p a b", b=shape[2])
            if len(shape) == 4:
                return ap.rearrange("p (a b c) -> p a b c", b=shape[2], c=shape[3])
            if len(shape) == 5:
                return ap.rearrange("p (a b c d) -> p a b c d", b=shape[2], c=shape[3], d=shape[4])
            raise ValueError(shape)

    def prologue(self):
        nc = self.nc
        cv = KB.Carver(self)
        st32 = [cv.F("st32_0", [128, 2048]), (self.x_tm[:].rearrange("p a b -> p (a b)"), self.b_x)]
        st16 = [cv.H("st16_%d" % i, [128, 2048]) for i in range(2)]
        c = self.cst
        self.ld(self.ident[:], c["c_ident"][:, :], W=[self.b_ident])
        self.ld(self.tri[:], c["c_tri"][:, :], W=[self.b_tri])
        self.ld(self.atri[:], c["c_atri"][:, :], W=[self.b_atri])
        self.ld(self.pswap[:], c["c_pswap"][:, :], W=[self.b_pswap])
        self.ld(self.esel[:], c["c_esel"][:, :, :], W=[self.b_esel])
        self.ld(self.ovaug[:], c["c_ovaug"].rearrange("n p c -> p n c"), W=[self.b_ovaug])
        self.ld(self.fgbc[:], self.final_g.partition_broadcast(128), W=[self.b_fgbc])
        self.ld(self.gateb[:], self.b_gate.rearrange("(p o) -> p o", o=1), W=[self.b_gateb])
        self.ld(self.gcol[:], self.norm_g.rearrange("l (kc p) -> p l kc", p=128), W=[self.b_gcol], slow=True)
        self.ld(self.lng[:], self.a_ln_g.rearrange("(g p) -> p g", p=128), W=[self.b_lng], slow=True)
        for g in range(4):
            self.ld(self.KA[64:128, g, :], c["c_ind"][:, :], W=[self.b_KA])
        self.pool(lambda: nc.gpsimd.memset(self.VS[:], 1.0), W=[self.b_VS])
        self.pool(lambda: nc.gpsimd.memset(self.VW[:], 1.0), W=[self.b_VW])
        self.pool(lambda: nc.gpsimd.memset(self.VC[:], 1.0), W=[self.b_VC])
        self.pool(lambda: nc.gpsimd.memset(self.VC[:, :, :, 0:64], 0.0), W=[self.b_VC])
        self.pool(lambda: nc.gpsimd.memset(self.KCT[:], 0.0), W=[self.b_KCT])
        self.pool(lambda: nc.gpsimd.memset(self.KRAW[:], 0.0), W=[self.b_KRAW])
        self.pool(lambda: nc.gpsimd.memset(self.KW[:], 0.0), W=[self.b_KW])
        self.pool(lambda: nc.gpsimd.memset(self.KA[0:64, :, :], 0.0), W=[self.b_KA])
        self.pool(lambda: nc.gpsimd.memset(self.ones_row[:], 1.0), W=[self.b_ones])

        self._cv_i = 0

        def conv(src, dst, p, a, b, scale=None, wbuf=None, pre=None):
            i = self._cv_i
            self._cv_i += 1
            (s32, b32), (s16, b16) = st32[i % 2], st16[i % 2]
            v32 = s32[0:p, 0:a * b].rearrange("p (a b) -> p a b", b=b)
            v16 = s16[0:p, 0:a * b].rearrange("p (a b) -> p a b", b=b)
            if pre is not None:
                pre(s32, b32)
            else:
                self.ld(v32, src, W=[b32])
            f32 = s32[0:p, 0:a * b]
            f16 = s16[0:p, 0:a * b]
            if scale is not None:
                self.dve(lambda: nc.vector.tensor_scalar(out=f16, in0=f32, scalar1=scale, scalar2=None, op0=ALU.mult),
                         R=[b32, self.b_gcol], W=[b16])
            elif i % 2 == 0:
                self.dve(lambda: nc.vector.tensor_copy(out=f16, in_=f32), R=[b32], W=[b16])
            else:
                self.act(lambda: nc.scalar.copy(out=f16, in_=f32), R=[b32], W=[b16])
            self.ld(dst, v16, R=[b16], q="pool")

        for kc in range(8):
            rows = slice(kc * 128, (kc + 1) * 128)
            sc0 = self.gcol[:, 0, kc:kc + 1]
            sc1 = self.gcol[:, 1, kc:kc + 1]
            for (c0, s0) in ((AW, 0), (0, 4), (2 * AW, 8)):
                conv(self.a_w_in[rows, c0:c0 + 2048].rearrange("p (a b) -> p a b", b=512),
                     self.wA_in[s0:s0 + 4, :, kc, :].rearrange("s p c -> p s c"), 128, 4, 512, sc0, self.b_wscr["A_in"])
            conv(self.b_w_in[rows, 0:1024].rearrange("p (a b) -> p a b", b=512),
                 self.wB_q[0:2, :, kc, :].rearrange("s p c -> p s c"), 128, 2, 512, sc1, self.b_wscr["B_q"])
            conv(self.b_w_in[rows, 1024:2560].rearrange("p (a b) -> p a b", b=512),
                 self.wB_kv[0:3, :, kc, :].rearrange("s p c -> p s c"), 128, 3, 512, sc1, self.b_wscr["B_kv"])
            for hz in range(2):
                conv(self.b_w_in[rows, 2560 + hz * 1536:2560 + (hz + 1) * 1536].rearrange("p (a b) -> p a b", b=512),
                     self.wB_z[hz * 3:hz * 3 + 3, :, kc, :].rearrange("s p c -> p s c"), 128, 3, 512, sc1, self.b_wscr["B_z"])
            conv(self.b_w_in[rows, 5632:5680].rearrange("p (a b) -> p a b", b=48),
                 self.wB_gl[:, kc:kc + 1, :], 128, 1, 48, sc1, self.b_wscr["B_gl"])
        for fc in range(16):
            conv(self.a_w_out[fc * 128:(fc + 1) * 128, :].rearrange("p (a b) -> p a b", b=512),
                 self.wA_out[2 * (fc // 4):2 * (fc // 4) + 2, :, fc % 4, :].rearrange("s p c -> p s c"), 128, 2, 512,
                 None, self.b_wscr["A_out"])
        for cb in range(2):
            for hh in range(2):
                for q in range(2):
                    h0 = hh * 8 + q * 4
                    conv(self.b_w_out[h0 * 64:(h0 + 4) * 64, cb * 512:(cb + 1) * 512].rearrange("(h d) c -> d h c", d=64),
                         self.wB_out[cb * 2 + hh, :, q * 4:q * 4 + 4, :], 64, 4, 512, None, self.b_wscr["B_out"])
        for kv in range(2):
            for jh in range(2):
                def pre(s32, b32, kv=kv, jh=jh):
                    v = s32[:, :].rearrange("p (a b) -> p a b", b=128)
                    self.dve(lambda: nc.vector.memset(s32[:, :], 0.0), W=[b32])
                    src = self.b_w1[kv, jh * 16:(jh + 1) * 16].rearrange("j c h -> c j h")
                    self.ld(v[0:64, :, 0:64], src, W=[b32])
                    self.ld(v[64:128, :, 64:128], src, W=[b32])
                conv(None, self.w1bd[kv, :, jh * 16:(jh + 1) * 16, :], 128, 16, 128, None, self.b_wscr["w1bd"], pre=pre)

        wtmp, b_wtmp = st32[0]
        lnb2, b_lnb2 = cv.F("lnb2", [2, 16, 128])
        rs2, b_rs2 = cv.F("rs2", [2, 16, 128])
        wbf, b_wbf = cv.H("wbf", [128, 128])
        onesc, b_onesc = cv.H("onesc", [128, 1])
        self.pool(lambda: nc.gpsimd.memset(onesc, 1.0), W=[b_onesc])
        self.pool(lambda: nc.gpsimd.memset(lnb2[:, :, :], 1.0), W=[b_lnb2])
        self.ld(lnb2[0:1, :, :], self.a_ln_b.rearrange("(o g c) -> o g c", o=1, c=128), W=[b_lnb2])
        self.ld(rs2[1:2, :, :], self.a_b_s.rearrange("(o g) t -> o g t", o=1), W=[b_rs2])
        for g in range(16):
            wv = wtmp[:, 0:128]
            self.ld(wv, self.a_w_s[g], W=[b_wtmp])
            self.pool(lambda: nc.gpsimd.affine_select(out=wv, in_=wv, pattern=[[-1, 128]], compare_op=ALU.is_ge,
                                                      fill=0.0, base=0, channel_multiplier=1), R=[b_wtmp], W=[b_wtmp])
            self.dve(lambda: nc.vector.tensor_copy(out=wbf, in_=wv), R=[b_wtmp], W=[b_wbf])
            pt, b_pt = self.next_ptr()
            self.pe(lambda: nc.tensor.transpose(out=pt[:, 0:128], in_=wbf, identity=self.ident[:]),
                    R=[b_wbf, self.b_ident], W=[b_pt])
            self.act(lambda: nc.scalar.copy(out=self.wmT[:, g, :], in_=pt[:, 0:128]), R=[b_pt], W=[self.b_wmT])
            pm, b_pm = self.next_pm()
            self.pe(lambda: nc.tensor.matmul(pm[0:1, 0:128], lhsT=onesc, rhs=self.wmT[:, g, :], start=True, stop=True),
                    R=[b_onesc, self.b_wmT], W=[b_pm])
            self.act(lambda: nc.scalar.copy(out=rs2[0:1, g, :], in_=pm[0:1, 0:128]), R=[b_pm], W=[b_rs2])
        for g in range(16):
            pm, b_pm = self.next_pm()
            self.pe(lambda: nc.tensor.matmul(pm[:, 0:128], lhsT=lnb2[:, g, :], rhs=rs2[:, g, :], start=True, stop=True),
                    R=[b_lnb2, b_rs2], W=[b_pm])
            self.act(lambda: nc.scalar.copy(out=self.Bmix[:, g, :], in_=pm[:, 0:128]), R=[b_pm], W=[self.b_Bmix])

        w2f, b_w2f = cv.F("w2f", [128, 2, 128])
        self.dve(lambda: nc.vector.memset(w2f[:, :, :], 0.0), W=[b_w2f])
        for a in range(2):
            for hcol in range(2):
                self.ld(w2f[a * 64:(a + 1) * 64, a, hcol * 64:(hcol + 1) * 64], self.b_w2[0], W=[b_w2f])
        self.dve(lambda: nc.vector.tensor_copy(out=self.W2K[:], in_=w2f[:, :, :]), R=[b_w2f], W=[self.b_W2K])
        w2v, b_w2v = cv.F("w2v", [128, 128])
        self.dve(lambda: nc.vector.memset(w2v, 0.0), W=[b_w2v])
        for a in range(2):
            self.ld(w2v[a * 64:(a + 1) * 64, a * 64:(a + 1) * 64], self.b_w2[1], W=[b_w2v])
        self.dve(lambda: nc.vector.tensor_copy(out=self.W2V[:], in_=w2v), R=[b_w2v], W=[self.b_W2V])
        for a in range(2):
            self.ld(self.b2K[a * 64:(a + 1) * 64, :], self.b_b2[0].rearrange("(p o) -> p o", o=1), W=[self.b_b2K])
        b2vf, b_b2vf = cv.F("b2vf", [1, 128])
        for a in range(2):
            self.ld(b2vf[0:1, a * 64:(a + 1) * 64], self.b_b2[1].rearrange("(o c) -> o c", o=1), W=[b_b2vf])
        self.dve(lambda: nc.vector.tensor_copy(out=self.b2Vrow[:], in_=b2vf), R=[b_b2vf], W=[self.b_b2V])
        pef, b_pef = cv.F("pef", [128, 2, 32])
        peb, b_peb = cv.H("peb", [128, 2, 32])
        b1f, b_b1f = cv.F("b1f", [128, 2])
        for a in range(2):
            self.ld(pef[a * 64:(a + 1) * 64, :, :], self.b_pe.rearrange("k j c -> c k j"), W=[b_pef], slow=True)
            self.ld(b1f[a * 64:(a + 1) * 64, :], self.b_b1.rearrange("k h -> h k"), W=[b_b1f], slow=True)
        self.dve(lambda: nc.vector.tensor_copy(out=peb[:, :, :], in_=pef[:, :, :]), R=[b_pef], W=[b_peb])
        self.S.barrier()
        self.ld(self.wgl[:], self.wB_gl[:, :, :], W=[self.b_wgl])
        for kv in range(2):
            wt, b_wt = self.next_wst()
            w1v = wt[:].rearrange("p a b -> p (a b)").rearrange("p (j c) -> p j c", c=128)
            self.ld(w1v, self.w1bd[kv], R=[self.b_wscr["w1bd"]], W=[b_wt])
            pm, b_pm = self.next_pm()
            for j in range(32):
                self.pe(lambda: nc.tensor.matmul(pm[:, 0:1], lhsT=w1v[:, j, :], rhs=peb[:, kv, j:j + 1],
                                                 start=(j == 0), stop=(j == 31)), R=[b_wt, b_peb], W=[b_pm], signal=(j == 31))
            self.dve(lambda: nc.vector.tensor_tensor(out=self.bias1[:, kv:kv + 1], in0=pm[:, 0:1], in1=b1f[:, kv:kv + 1], op=ALU.add),
                     R=[b_pm, b_b1f], W=[self.b_bias1])
        self.S.barrier()

    def rms_T(self, npart=128, ncol=T):
        nc = self.nc
        nsub = (ncol + 127) // 128
        st4, b_st4 = self.st4, self.b_st4
        for s in range(nsub):
            self.act(lambda: nc.scalar.activation(out=self.xhat[0:npart, s, :], in_=self.x_tm[0:npart, s, :], func=AF.Square,
                                                  accum_out=st4[0:npart, s:s + 1]), R=[self.b_x], W=[self.b_xhat, b_st4])
        self.dve(lambda: nc.vector.tensor_scalar(out=st4[0:npart, 2:2 + nsub], in0=st4[0:npart, 0:nsub], scalar1=1.0 / D, scalar2=EPS,
                                                 op0=ALU.mult, op1=ALU.add), R=[b_st4], W=[b_st4])
        self.act(lambda: nc.scalar.activation(out=st4[0:npart, 2:2 + nsub], in_=st4[0:npart, 2:2 + nsub], func=AF.Sqrt), R=[b_st4], W=[b_st4])
        self.dve(lambda: nc.vector.reciprocal(out=st4[0:npart, 4:4 + nsub], in_=st4[0:npart, 2:2 + nsub]), R=[b_st4], W=[b_st4])

    def xhat_T(self, npart=128, ncol=T):
        nc = self.nc
        nsub = (ncol + 127) // 128
        st4, b_st4 = self.st4, self.b_st4
        for s in range(nsub):
            self.act(lambda: nc.scalar.mul(out=self.xhat[0:npart, s, :], in_=self.x_tm[0:npart, s, :], mul=st4[0:npart, 4 + s:5 + s]),
                     R=[self.b_x, b_st4], W=[self.b_xhat])
        w = min(npart, 128)
        for kc in range(8):
            pt, b_pt = self.next_ptr()
            for s in range(nsub):
                self.pe(lambda: nc.tensor.transpose(out=pt[:, s * 128:s * 128 + w], in_=self.xhat[0:npart, s, kc * 128:(kc + 1) * 128],
                                                    identity=self.ident[0:npart, 0:npart]), R=[self.b_xhat, self.b_ident], W=[b_pt], signal=(s == nsub - 1))
            if kc % 2 == 0:
                self.act(lambda: nc.scalar.copy(out=self.xnT[:, kc, 0:ncol], in_=pt[:, 0:ncol]), R=[b_pt], W=[self.b_xnT])
            else:
                self.dve(lambda: nc.vector.tensor_copy(out=self.xnT[:, kc, 0:ncol], in_=pt[:, 0:ncol]), R=[b_pt], W=[self.b_xnT])

    def layer_a(self, i):
        nc = self.nc
        cv = KB.Carver(self)
        gv, b_gv = cv.H("gv", [128, NSUB, AW])
        gu = [cv.H("gu%d" % k, [128, 4, T]) for k in range(2)]
        sz = [cv.H("sz%d" % k, [128, 4, T]) for k in range(2)]
        gs = [cv.H("gs%d" % k, [128, 4, T]) for k in range(2)]
        yT, b_yT = cv.H("yT", [128, 16, T])
        tmpf = [cv.F("tmpf%d" % k, [128, 128]) for k in range(2)]
        stats, b_stats = cv.F("stats", [128, NSUB, 4, 6])
        mv, b_mv = cv.F("mv", [128, NSUB, 2])
        rv, b_rv = cv.F("rv", [128, 4])
        for s in range(NSUB):
            self.ld(self.x_tm[:, s, :], self.x_p[i * T + s * 128:i * T + (s + 1) * 128, :], W=[self.b_x])
        self.rms_T()
        self.xhat_T()
        for cb in range(4):
            wt, b_wt = self.next_wst()
            self.ld(wt[:], self.wA_in[cb], W=[b_wt])
            for s in range(NSUB):
                pm, b_pm = self.next_pm()
                for kc in range(8):
                    self.pe(lambda: nc.tensor.matmul(pm[:, :], lhsT=self.xnT[:, kc, s * 128:(s + 1) * 128], rhs=wt[:, kc, :],
                                                     start=(kc == 0), stop=(kc == 7)), R=[self.b_xnT, b_wt], W=[b_pm], signal=(kc == 7))
                self.act(lambda: nc.scalar.activation(out=gv[:, s, cb * 512:(cb + 1) * 512], in_=pm[:, :], func=AF.Gelu), R=[b_pm], W=[b_gv])
                self.dve(lambda: nc.vector.bn_stats(out=stats[:, s, cb, :], in_=gv[:, s, cb * 512:(cb + 1) * 512]), R=[b_gv], W=[b_stats])
        for s in range(NSUB):
            self.dve(lambda: nc.vector.bn_aggr(out=mv[:, s, :], in_=stats[:, s, :, :]), R=[b_stats], W=[b_mv])
        self.dve(lambda: nc.vector.tensor_scalar(out=rv[:, 0:NSUB], in0=mv[:, :, 1], scalar1=EPS, scalar2=None, op0=ALU.add), R=[b_mv], W=[b_rv])
        self.act(lambda: nc.scalar.activation(out=rv[:, 0:NSUB], in_=rv[:, 0:NSUB], func=AF.Sqrt), R=[b_rv], W=[b_rv])
        self.dve(lambda: nc.vector.reciprocal(out=rv[:, 2:2 + NSUB], in_=rv[:, 0:NSUB]), R=[b_rv], W=[b_rv])
        for s in range(NSUB):
            self.dve(lambda: nc.vector.tensor_scalar(out=gv[:, s, :], in0=gv[:, s, :], scalar1=mv[:, s, 0:1], scalar2=rv[:, 2 + s:3 + s],
                                                     op0=ALU.subtract, op1=ALU.mult), R=[b_gv, b_mv, b_rv], W=[b_gv])
        for q in range(4):
            (guq, b_gu), (szq, b_sz), (gsq, b_gs) = gu[q % 2], sz[q % 2], gs[q % 2]
            for (slab, dst, b_dst, fn) in ((4 + q, guq, b_gu, AF.Gelu), (8 + q, szq, b_sz, AF.Silu)):
                wt, b_wt = self.next_wst()
                self.ld(wt[:], self.wA_in[slab], W=[b_wt])
                for m in range(4):
                    pm, b_pm = self.next_pm()
                    for kc in range(8):
                        self.pe(lambda: nc.tensor.matmul(pm[:, 0:T], lhsT=wt[:, kc, m * 128:(m + 1) * 128], rhs=self.xnT[:, kc, :],
                                                         start=(kc == 0), stop=(kc == 7)), R=[self.b_xnT, b_wt], W=[b_pm], signal=(kc == 7))
                    self.act(lambda: nc.scalar.activation(out=dst[:, m, :], in_=pm[:, 0:T], func=fn), R=[b_pm], W=[b_dst])
            self.pool(lambda: nc.gpsimd.tensor_tensor(out=gsq[:, :, :], in0=guq[:, :, :], in1=szq[:, :, :], op=ALU.mult), R=[b_gu, b_sz], W=[b_gs])
            for s in range(NSUB):
                pm, b_pm = self.next_pm()
                for m in range(4):
                    g = 4 * q + m
                    self.pe(lambda: nc.tensor.matmul(pm[:, m * 128:(m + 1) * 128], lhsT=gv[:, s, g * 128:(g + 1) * 128], rhs=self.wmT[:, g, :],
                                                     start=(m == 0), stop=True, skip_group_check=True), R=[b_gv, self.b_wmT], W=[b_pm], signal=(m == 3))
                for m in range(4):
                    g = 4 * q + m
                    tf, b_tf = tmpf[m % 2]
                    self.dve(lambda: nc.vector.scalar_tensor_tensor(out=tf, in0=pm[:, m * 128:(m + 1) * 128], scalar=self.lng[:, g:g + 1],
                                                                    in1=self.Bmix[:, g, :], op0=ALU.mult, op1=ALU.add),
                             R=[b_pm, self.b_lng, self.b_Bmix], W=[b_tf])
                    self.dve(lambda: nc.vector.tensor_tensor(out=yT[:, g, s * 128:(s + 1) * 128], in0=tf, in1=gsq[:, m, s * 128:(s + 1) * 128],
                                                             op=ALU.mult), R=[b_tf, b_gs], W=[b_yT])
        for cb in range(2):
            pms = [self.next_pm() for _ in range(NSUB)]
            for fcg in range(4):
                wt, b_wt = self.next_wst()
                self.ld(wt[:, 0:4, :], self.wA_out[fcg * 2 + cb], W=[b_wt])
                for s in range(NSUB):
                    for fl in range(4):
                        fc = fcg * 4 + fl
                        self.pe(lambda: nc.tensor.matmul(pms[s][0][:, :], lhsT=yT[:, fc, s * 128:(s + 1) * 128], rhs=wt[:, fl, :],
                                                         start=(fc == 0), stop=(fc == 15)), R=[b_yT, b_wt], W=[pms[s][1]], signal=(fl == 3))
            for s in range(NSUB):
                self.dve(lambda: nc.vector.tensor_tensor(out=self.x_tm[:, s, cb * 512:(cb + 1) * 512], in0=self.x_tm[:, s, cb * 512:(cb + 1) * 512],
                                                         in1=pms[s][0][:, :], op=ALU.add), R=[pms[s][1], self.b_x], W=[self.b_x])

    def mm8(self, out_ap, b_out, lhs_fn, rhs_fn, R):
        nc = self.nc
        for kc in range(8):
            self.pe(lambda: nc.tensor.matmul(out_ap, lhsT=lhs_fn(kc), rhs=rhs_fn(kc), start=(kc == 0), stop=(kc == 7)),
                    R=R, W=[b_out], signal=(kc == 7))

    def layer_b(self, i):
        nc = self.nc
        cv = KB.Carver(self)
        kvtm, b_kvtm = cv.F("kvtm", [128, NSUB, 1536])
        cosF, b_cosF = cv.F("cosF", [128, T])
        sinF, b_sinF = cv.F("sinF", [128, T])
        ctm, b_ctm = cv.F("ctm", [128, NSUB, 4, 8])
        stm, b_stm = cv.F("stm", [128, NSUB, 4, 8])
        rtmp, b_rtmp = cv.F("rtmp", [128, 4, 4, 8])
        force, b_force = cv.F("force", [128, NSUB, 64])
        tM, b_tM = cv.F("tM", [128, 512])
        tA, b_tA = cv.F("tA", [128, 512])
        tB, b_tB = cv.F("tB", [128, 512])
        yacc, b_yacc = cv.F("yacc", [128, 512])
        impv, b_imp = cv.F("imp", [128, 64])
        score, b_score = cv.F("score", [128, 64])
        scr, b_scr = cv.F("scr", [128, 64])
        m8, b_m8 = cv.F("m8", [128, 16])
        rd, b_rd = cv.F("rd", [128, 8])
        tO_off = cv.f
        qtA, b_qtA = cv.F("qtA", [128, T])
        qtB, b_qtB = cv.F("qtB", [128, T])
        tO, b_tO = self.arenaF[:, tO_off:tO_off + 512], self.nb("tO")
        ktmp, b_ktmp = cv.H("ktmp", [128, NSUB, 1024])
        qrawT, b_qraw = cv.H("qrawT", [128, 8, T])
        qaug, b_qaug = cv.H("qaug", [128, 4, NSUB, 512])
        b_qbias = self.nb("qbias")
        sz64, b_sz64 = cv.H("sz64", [128, 3, NSUB, 512])
        gT, b_gT = cv.H("gT", [128, T])
        PT = [cv.H("PT%d" % k, [128, 512]) for k in range(2)]
        yB64, b_yB = cv.H("yB64", [128, 4, NSUB, 512])
        Btm, b_Btm = cv.H("Btm", [128, 128])
        cbias, b_cbias = cv.H("cbias", [128, NSUB, 2, 128])
        hidT, b_hidT = cv.H("hidT", [128, 4, 16])
        vstage, b_vst = cv.H("vstage", [128, 2, 128])
        qrot, b_qrot = cv.H("qrot", [128, T])
        c = self.cst
        t0 = i * T

        self.rms_T()
        self.xhat_T()
        self.ld(cosF, c["c_cosF"][:, t0:t0 + T], W=[b_cosF], q="pool")
        self.ld(sinF, c["c_sinF"][:, t0:t0 + T], W=[b_sinF], q="pool")
        self.ld(ctm, c["c_costm4"][t0:t0 + T].rearrange("(s p) g c -> p s g c", p=128), W=[b_ctm], q="pool")
        self.ld(stm, c["c_sintm4"][t0:t0 + T].rearrange("(s p) g c -> p s g c", p=128), W=[b_stm], q="pool")
        self.ld(force, c["c_force"][2 * i:2 * i + 2].rearrange("q p j -> p q j"), W=[b_force], q="pool")
        self.ld(cbias, c["c_cbias"][2 * i:2 * i + 2].rearrange("q n p c -> p q n c"), W=[b_cbias], q="pool")
        self.dve(lambda: nc.vector.memset(Btm, 0.0), W=[b_Btm])

        for br in range(3):
            wt, b_wt = self.next_wst()
            self.ld(wt[:], self.wB_kv[br], W=[b_wt])
            for s in range(NSUB):
                pm, b_pm = self.next_pm()
                self.mm8(pm[:, :], b_pm, lambda kc: self.xnT[:, kc, s * 128:(s + 1) * 128], lambda kc: wt[:, kc, :], [self.b_xnT, b_wt])
                self.act(lambda: nc.scalar.copy(out=kvtm[:, s, br * 512:(br + 1) * 512], in_=pm[:, :]), R=[b_pm], W=[b_kvtm])
        if getattr(self, 'stop_after', None) == 'kv':
            return
        for s in range(NSUB):
            for br in (1, 2):
                kview = kvtm[:, s, br * 512:br * 512 + 256].rearrange("p (g d) -> p g d", d=64)
                x1 = kview[:, :, 0:8]
                x2 = kview[:, :, 8:16]
                cs = ctm[:, s, :, :]
                sn = stm[:, s, :, :]
                R_ = [b_kvtm, b_ctm, b_stm]
                self.dve(lambda: nc.vector.tensor_tensor(out=rtmp[:, 0, :, :], in0=x1, in1=cs, op=ALU.mult), R=R_, W=[b_rtmp])
                self.dve(lambda: nc.vector.tensor_tensor(out=rtmp[:, 1, :, :], in0=x2, in1=sn, op=ALU.mult), R=R_, W=[b_rtmp])
                self.dve(lambda: nc.vector.tensor_tensor(out=rtmp[:, 2, :, :], in0=x2, in1=cs, op=ALU.mult), R=R_, W=[b_rtmp])
                self.dve(lambda: nc.vector.tensor_tensor(out=rtmp[:, 3, :, :], in0=x1, in1=sn, op=ALU.mult), R=R_, W=[b_rtmp])
                self.dve(lambda: nc.vector.tensor_tensor(out=x1, in0=rtmp[:, 0, :, :], in1=rtmp[:, 1, :, :], op=ALU.subtract), R=[b_rtmp], W=[b_kvtm])
                self.dve(lambda: nc.vector.tensor_tensor(out=x2, in0=rtmp[:, 2, :, :], in1=rtmp[:, 3, :, :], op=ALU.add), R=[b_rtmp], W=[b_kvtm])
        if getattr(self, 'stop_after', None) == 'rope':
            return
        for s in range(NSUB):
            r0 = t0 + s * 128
            self.st(self.o_cmp_p[r0:r0 + 128, :], kvtm[:, s, 0:512], R=[b_kvtm], q="pool")
            self.st(self.o_sel_p[r0:r0 + 128, :], kvtm[:, s, 512:1024], R=[b_kvtm], q="pool")
            if r0 >= SEQ - 512:
                w0 = r0 - (SEQ - 512)
                self.st(self.o_win_p[w0:w0 + 128, :], kvtm[:, s, 1024:1536], R=[b_kvtm], q="pool")
        if getattr(self, 'stop_after', None) == 'out':
            return
        for s in range(NSUB):
            kt = 2 * i + s
            slot = kt % 6
            self.pool(lambda: nc.gpsimd.tensor_copy(out=ktmp[:, s, 0:768], in_=kvtm[:, s, 0:768]), R=[b_kvtm], W=[b_ktmp])
            self.pool(lambda: nc.gpsimd.tensor_copy(out=ktmp[:, s, 768:1024], in_=kvtm[:, s, 1024:1280]), R=[b_kvtm], W=[b_ktmp])
            self.pool(lambda: nc.gpsimd.tensor_copy(out=self.VS[:, kt, :, 0:64], in_=kvtm[:, s, 768:1024].rearrange("p (g d) -> p g d", d=64)),
                      R=[b_kvtm], W=[self.b_VS])
            self.pool(lambda: nc.gpsimd.tensor_copy(out=self.VW[:, slot, :, 0:64], in_=kvtm[:, s, 1280:1536].rearrange("p (g d) -> p g d", d=64)),
                      R=[b_kvtm], W=[self.b_VW])
            pt, b_pt = self.next_ptr()
            for tl in range(4):
                self.pe(lambda: nc.tensor.transpose(out=pt[:, tl * 128:(tl + 1) * 128], in_=ktmp[:, s, tl * 128:(tl + 1) * 128], identity=self.ident[:]),
                        R=[b_ktmp, self.b_ident], W=[b_pt], signal=(tl == 3))
            self.act(lambda: nc.scalar.copy(out=self.KRAW[:, :, 16 + s * 128:16 + (s + 1) * 128], in_=pt[:, 0:512].rearrange("p (a b) -> p a b", b=128)),
                     R=[b_pt], W=[self.b_KRAW])
            for (c0, dst, b_dst, off) in ((512, self.KA, self.b_KA, kt * 128), (768, self.KW, self.b_KW, slot * 128)):
                pt, b_pt = self.next_ptr()
                for g in range(4):
                    self.pe(lambda: nc.tensor.transpose(out=pt[0:64, g * 128:(g + 1) * 128], in_=ktmp[:, s, c0 + g * 64:c0 + (g + 1) * 64], identity=self.ident[:]),
                            R=[b_ktmp, self.b_ident], W=[b_pt], signal=(g == 3))
                self.dve(lambda: nc.vector.tensor_copy(out=dst[0:64, :, off:off + 128], in_=pt[0:64, 0:512].rearrange("p (a b) -> p a b", b=128)),
                         R=[b_pt], W=[b_dst])
        if getattr(self, 'stop_after', None) == 'tr':
            return
        for sl in range(2):
            wt, b_wt = self.next_wst()
            self.ld(wt[:], self.wB_q[sl], W=[b_wt])
            for m4 in range(4):
                m = sl * 4 + m4
                pm, b_pm = self.next_pm()
                self.mm8(pm[:, 0:T], b_pm, lambda kc: wt[:, kc, m4 * 128:(m4 + 1) * 128], lambda kc: self.xnT[:, kc, :], [self.b_xnT, b_wt])
                self.act(lambda: nc.scalar.copy(out=qrawT[:, m, :], in_=pm[:, 0:T]), R=[b_pm], W=[b_qraw])
                px, b_px = self.px
                self.pe(lambda: nc.tensor.matmul(px[:, 0:T], lhsT=self.pswap[:], rhs=qrawT[:, m, :], start=True, stop=True), R=[self.b_pswap, b_qraw], W=[b_px])
                self.pool(lambda: nc.gpsimd.tensor_tensor(out=qtA, in0=qrawT[:, m, :], in1=cosF, op=ALU.mult), R=[b_qraw, b_cosF], W=[b_qtA])
                self.dve(lambda: nc.vector.tensor_tensor(out=qtB, in0=px[:, 0:T], in1=sinF, op=ALU.mult), R=[b_px, b_sinF], W=[b_qtB])
                self.dve(lambda: nc.vector.tensor_tensor(out=qrot, in0=qtA, in1=qtB, op=ALU.add), R=[b_qtA, b_qtB], W=[b_qrot])
                g = m // 2
                for a in range(2):
                    ro = a * 2 + (m % 2)
                    self.act(lambda: nc.scalar.copy(out=qaug[0:64, g, :, ro * 128:(ro + 1) * 128], in_=qrot[a * 64:(a + 1) * 64, :].rearrange("p (s q) -> p s q", q=128)),
                             R=[b_qrot], W=[b_qaug])
        if getattr(self, 'stop_after', None) == 'q':
            return
        pm, b_pm = self.next_pm()
        self.mm8(pm[0:48, 0:T], b_pm, lambda kc: self.wgl[:, kc, :], lambda kc: self.xnT[:, kc, :], [self.b_xnT, self.b_wgl])
        self.act(lambda: nc.scalar.activation(out=gT[0:48, :], in_=pm[0:48, 0:T], func=AF.Sigmoid, bias=self.gateb[:, 0:1]), R=[b_pm, self.b_gateb], W=[b_gT])

        if getattr(self, 'stop_after', None) == 'gl':
            return
        nblk = 15 if i == 0 else 16
        base = 16 if i == 0 else 0
        m0 = 0 if i == 0 else 16 * i - 1
        pmh, b_pmh = self.next_pm()
        for kv in range(2):
            wt, b_wt = self.next_wst()
            w1v = wt[:].rearrange("p a b -> p (a b)").rearrange("p (j c) -> p j c", c=128)
            self.ld(w1v, self.w1bd[kv], W=[b_wt])
            for hp in range(2):
                tl = kv * 2 + hp
                for j in range(32):
                    self.pe(lambda: nc.tensor.matmul(pmh[:, tl * 16:tl * 16 + nblk], lhsT=w1v[:, j, :],
                                                     rhs=self.KRAW[:, tl, base + j:base + j + 16 * (nblk - 1) + 1:16],
                                                     start=(tl == 0 and j == 0), stop=(j == 31), skip_group_check=True),
                            R=[b_wt, self.b_KRAW], W=[b_pmh], signal=(j == 31))
        for kv in range(2):
            self.act(lambda: nc.scalar.activation(out=hidT[:, 2 * kv:2 * kv + 2, 0:nblk],
                                                  in_=pmh[:, 32 * kv:32 * kv + 32].rearrange("p (a b) -> p a b", b=16)[:, :, 0:nblk],
                                                  func=AF.Silu, bias=self.bias1[:, kv:kv + 1]), R=[b_pmh, self.b_bias1], W=[b_hidT])
        pmk, b_pmk = self.next_pm()
        for g in range(4):
            self.pe(lambda: nc.tensor.matmul(pmk[:, g * 16:g * 16 + nblk], lhsT=self.W2K[:, g % 2, :], rhs=hidT[:, g // 2, 0:nblk],
                                             start=(g == 0), stop=True, skip_group_check=True), R=[self.b_W2K, b_hidT], W=[b_pmk], signal=(g == 3))
        self.act(lambda: nc.scalar.activation(out=self.KCT[:, :, m0:m0 + nblk], in_=pmk[:, 0:64].rearrange("p (a b) -> p a b", b=16)[:, :, 0:nblk],
                                              func=AF.Identity, bias=self.b2K[:, 0:1]), R=[b_pmk, self.b_b2K], W=[self.b_KCT])
        pmv, b_pmv = self.next_pm()
        for hp in range(2):
            self.pe(lambda: nc.tensor.matmul(pmv[0:nblk, hp * 128:(hp + 1) * 128], lhsT=hidT[:, 2 + hp, 0:nblk], rhs=self.W2V[:, :],
                                             start=(hp == 0), stop=False, skip_group_check=True), R=[b_hidT, self.b_W2V], W=[b_pmv], signal=False)
            self.pe(lambda: nc.tensor.matmul(pmv[0:nblk, hp * 128:(hp + 1) * 128], lhsT=self.ones_row[0:1, 0:nblk], rhs=self.b2Vrow[0:1, :],
                                             start=False, stop=True, skip_group_check=True), R=[self.b_ones, self.b_b2V], W=[b_pmv], signal=(hp == 1))
        self.act(lambda: nc.scalar.copy(out=vstage[0:nblk, :, :], in_=pmv[0:nblk, 0:256].rearrange("p (a b) -> p a b", b=128)), R=[b_pmv], W=[b_vst])
        blk = m0
        while blk < m0 + nblk:
            nt = blk // 128
            p0 = blk % 128
            cnt = min(m0 + nblk - blk, 128 - p0)
            o = blk - m0
            self.ld(self.VC[p0:p0 + cnt, nt, :, 0:64], vstage[o:o + cnt, :, :].rearrange("p a (h d) -> p (a h) d", d=64), R=[b_vst], W=[self.b_VC], q="pool")
            blk += cnt
        self.dve(lambda: nc.vector.tensor_copy(out=self.KRAW[:, :, 0:16], in_=self.KRAW[:, :, T:T + 16]), R=[self.b_KRAW], W=[self.b_KRAW])

        if getattr(self, 'stop_after', None) == 'cmpr':
            return
        po0, b_po0 = self.po[0]
        po1, b_po1 = self.po[1]
        px, b_px = self.px

        def bias4(pm, b_pm, tile, b_tile, last=True):
            for ro in range(4):
                self.pe(lambda: nc.tensor.matmul(pm[:, ro * 128:(ro + 1) * 128], lhsT=self.ident[:], rhs=tile, start=False,
                                                 stop=(last and ro == 3), skip_group_check=True), R=[self.b_ident, b_tile], W=[b_pm], signal=(ro == 3))

        def merge(b, g, qs, po, b_po, last):
            self.dve(lambda: nc.vector.tensor_scalar(out=tM[0:64, :], in0=po[64:128, :], scalar1=1e-30, scalar2=None, op0=ALU.max), R=[b_po], W=[b_tM])
            self.act(lambda: nc.scalar.copy(out=tO[0:64, :], in_=po[0:64, :]), R=[b_po], W=[b_tO])
            self.dve(lambda: nc.vector.reciprocal(out=tA[0:64, :], in_=tM[0:64, :]), R=[b_tM], W=[b_tA])
            for ro in range(4):
                r = 2 * (ro % 2) + ro // 2
                h = 4 * g + r
                self.pe(lambda: nc.tensor.matmul(px[0:64, ro * 128:(ro + 1) * 128], lhsT=self.esel[0:48, b * 16 + h, :], rhs=gT[0:48, qs * 128:(qs + 1) * 128],
                                                 start=(ro == 0), stop=True, skip_group_check=True), R=[self.b_esel, b_gT], W=[b_px], signal=(ro == 3))
            self.dve(lambda: nc.vector.tensor_tensor(out=tM[0:64, :], in0=tA[0:64, :], in1=px[0:64, :], op=ALU.mult), R=[b_tA, b_px], W=[b_tM])
            self.dve(lambda: nc.vector.tensor_tensor(out=tA[0:64, :], in0=tM[0:64, :], in1=tO[0:64, :], op=ALU.mult), R=[b_tM, b_tO], W=[b_tA])
            if b == 0:
                self.pool(lambda: nc.gpsimd.tensor_tensor(out=yacc[0:64, :], in0=tA[0:64, :], in1=sz64[0:64, b, qs, :], op=ALU.mult), R=[b_tA, b_sz64], W=[b_yacc])
            else:
                self.pool(lambda: nc.gpsimd.tensor_tensor(out=tB[0:64, :], in0=tA[0:64, :], in1=sz64[0:64, b, qs, :], op=ALU.mult), R=[b_tA, b_sz64], W=[b_tB])
                if not last:
                    self.pool(lambda: nc.gpsimd.tensor_tensor(out=yacc[0:64, :], in0=yacc[0:64, :], in1=tB[0:64, :], op=ALU.add), R=[b_tB, b_yacc], W=[b_yacc])
                else:
                    self.pool(lambda: nc.gpsimd.tensor_tensor(out=yB64[0:64, g, qs, :], in0=yacc[0:64, :], in1=tB[0:64, :], op=ALU.add), R=[b_tB, b_yacc], W=[b_yB])

        for g in range(4):
            for b in range(3):
                wt, b_wt = self.next_wst()
                sl = 2 * b + g // 2
                c0 = (g % 2) * 256
                self.ld(wt[:, :, 0:256], self.wB_z[sl, :, :, c0:c0 + 256], W=[b_wt])
                for mm in range(2):
                    pm, b_pm = self.next_pm()
                    self.mm8(pm[:, 0:T], b_pm, lambda kc: wt[:, kc, mm * 128:(mm + 1) * 128], lambda kc: self.xnT[:, kc, :], [self.b_xnT, b_wt])
                    for a in range(2):
                        ro = a * 2 + mm
                        self.act(lambda: nc.scalar.activation(out=sz64[0:64, b, :, ro * 128:(ro + 1) * 128],
                                                              in_=pm[a * 64:(a + 1) * 64, 0:T].rearrange("p (s q) -> p s q", q=128), func=AF.Silu),
                                 R=[b_pm], W=[b_sz64])
            if getattr(self, 'stop_after', None) == 'z':
                return
            for qs in range(NSUB):
                qt = 2 * i + qs
                nts = [0] if i < 8 else [0, 1]
                for nt in nts:
                    need_bias = not (16 * (128 * nt + 127) + 31 <= 128 * qt)
                    pt_, b_ptile = PT[nt]
                    for a in range(2):
                        pm, b_pm = self.next_pm()
                        self.pe(lambda: nc.tensor.matmul(pm[:, 0:256], lhsT=self.KCT[a * 64:(a + 1) * 64, g, nt * 128:(nt + 1) * 128],
                                                         rhs=qrawT[a * 64:(a + 1) * 64, 2 * g:2 * g + 2, qs * 128:(qs + 1) * 128],
                                                         start=True, stop=(not need_bias), skip_group_check=True),
                                R=[self.b_KCT, b_qraw], W=[b_pm])
                        if need_bias:
                            for mm in range(2):
                                self.pe(lambda: nc.tensor.matmul(pm[:, mm * 128:(mm + 1) * 128], lhsT=self.ident[:], rhs=cbias[:, qs, nt, :], start=False,
                                                                 stop=(mm == 1), skip_group_check=True), R=[self.b_ident, b_cbias], W=[b_pm], signal=(mm == 1))
                        self.act(lambda: nc.scalar.activation(out=pt_[:, a * 256:(a + 1) * 256], in_=pm[:, 0:256], func=AF.Exp, scale=0.125), R=[b_pm], W=[b_ptile])
                    self.pe(lambda: nc.tensor.matmul(po0[:, :], lhsT=self.VC[:, nt, g, :], rhs=pt_, start=(nt == 0), stop=(nt == nts[-1])),
                            R=[self.b_VC, b_ptile], W=[b_po0])
                if getattr(self, 'stop_after', None) == 'cS':
                    return
                for ro in range(4):
                    for nt in nts:
                        self.pe(lambda: nc.tensor.matmul(px[:, ro * 65:(ro + 1) * 65], lhsT=PT[nt][0][:, ro * 128:(ro + 1) * 128], rhs=self.ovaug[:, nt, :],
                                                         start=(ro == 0 and nt == 0), stop=(nt == nts[-1]), skip_group_check=True),
                                R=[PT[nt][1], self.b_ovaug], W=[b_px], signal=(ro == 3 and nt == nts[-1]))
                if getattr(self, 'stop_after', None) == 'cI':
                    return
                pxv = px[:, 0:260].rearrange("p (r c) -> p r c", c=65)
                self.dve(lambda: nc.vector.tensor_scalar(out=rd[:, 0:4], in0=pxv[:, :, 64], scalar1=1e-30, scalar2=None, op0=ALU.max), R=[b_px], W=[b_rd])
                self.dve(lambda: nc.vector.reciprocal(out=rd[:, 4:8], in_=rd[:, 0:4]), R=[b_rd], W=[b_rd])
                self.dve(lambda: nc.vector.tensor_scalar(out=impv, in0=pxv[:, 0, 0:64], scalar1=rd[:, 4:5], scalar2=None, op0=ALU.mult), R=[b_px, b_rd], W=[b_imp])
                for ro in range(1, 4):
                    self.dve(lambda: nc.vector.scalar_tensor_tensor(out=impv, in0=pxv[:, ro, 0:64], scalar=rd[:, 4 + ro:5 + ro], in1=impv, op0=ALU.mult, op1=ALU.add),
                             R=[b_px, b_rd, b_imp], W=[b_imp])
                self.dve(lambda: nc.vector.tensor_tensor(out=score, in0=impv, in1=force[:, qs, :], op=ALU.add), R=[b_imp, b_force], W=[b_score])
                self.dve(lambda: nc.vector.max(out=m8[:, 0:8], in_=score), R=[b_score], W=[b_m8])
                self.dve(lambda: nc.vector.match_replace(out=scr, in_to_replace=m8[:, 0:8], in_values=score, imm_value=-1e30), R=[b_score, b_m8], W=[b_scr])
                self.dve(lambda: nc.vector.max(out=m8[:, 8:16], in_=scr), R=[b_scr], W=[b_m8])
                self.dve(lambda: nc.vector.tensor_scalar(out=Btm[:, 64:128], in0=score, scalar1=m8[:, 15:16], scalar2=1.0, op0=ALU.is_ge, op1=ALU.subtract),
                         R=[b_score, b_m8], W=[b_Btm])
                kts = [kt for kt in range(qt - 4, qt + 1) if kt >= 0]
                pend = None
                for j, kt in enumerate(kts):
                    slot = kt % 6
                    pm, b_pm = self.next_pm()
                    nb_ = (kt == qt) or (kt == qt - 4)
                    self.pe(lambda: nc.tensor.matmul(pm[:, :], lhsT=self.KW[0:64, g, slot * 128:(slot + 1) * 128], rhs=qaug[0:64, g, qs, :],
                                                     start=True, stop=not nb_, skip_group_check=True), R=[self.b_KW, b_qaug], W=[b_pm])
                    if kt == qt:
                        bias4(pm, b_pm, self.tri[:], self.b_tri)
                    elif kt == qt - 4:
                        bias4(pm, b_pm, self.atri[:], self.b_atri)
                    if pend is not None:
                        pend()
                    pt_, b_ptile = PT[j % 2]
                    self.act(lambda: nc.scalar.activation(out=pt_, in_=pm[:, :], func=AF.Exp, scale=0.125), R=[b_pm], W=[b_ptile])

                    def pend(slot=slot, pt_=pt_, b_ptile=b_ptile, first=(j == 0), last=(j == len(kts) - 1)):
                        self.pe(lambda: nc.tensor.matmul(po1[:, :], lhsT=self.VW[:, slot, g, :], rhs=pt_, start=first, stop=last),
                                R=[self.b_VW, b_ptile], W=[b_po1])
                pend()
                ptb, b_ptb = self.next_ptr()
                self.pe(lambda: nc.tensor.transpose(out=ptb[:, 0:128], in_=Btm, identity=self.ident[:]), R=[b_Btm, self.b_ident], W=[b_ptb])
                for ro in range(4):
                    self.act(lambda: nc.scalar.copy(out=qaug[64:128, g, qs, ro * 128:(ro + 1) * 128], in_=ptb[64:128, 0:128]), R=[b_ptb], W=[b_qbias])
                merge(0, g, qs, po0, b_po0, False)
                merge(2, g, qs, po1, b_po1, False)
                pend = None
                for kt in range(qt + 1):
                    pm, b_pm = self.next_pm()
                    self.pe(lambda: nc.tensor.matmul(pm[:, :], lhsT=self.KA[:, g, kt * 128:(kt + 1) * 128], rhs=qaug[:, g, qs, :],
                                                     start=True, stop=(kt != qt), skip_group_check=True), R=[self.b_KA, b_qaug, b_qbias], W=[b_pm])
                    if kt == qt:
                        bias4(pm, b_pm, self.tri[:], self.b_tri)
                    if pend is not None:
                        pend()
                    pt_, b_ptile = PT[kt % 2]
                    self.act(lambda: nc.scalar.activation(out=pt_, in_=pm[:, :], func=AF.Exp, scale=0.125), R=[b_pm], W=[b_ptile])

                    def pend(kt=kt, pt_=pt_, b_ptile=b_ptile):
                        self.pe(lambda: nc.tensor.matmul(po0[:, :], lhsT=self.VS[:, kt, g, :], rhs=pt_, start=(kt == 0), stop=(kt == qt)),
                                R=[self.b_VS, b_ptile], W=[b_po0])
                pend()
                merge(1, g, qs, po0, b_po0, True)

        if getattr(self, 'stop_after', None) == 'attn':
            return
        for cb in range(2):
            pms = [self.next_pm() for _ in range(NSUB)]
            for hh in range(2):
                wt, b_wt = self.next_wst()
                self.ld(wt[0:64, :, :], self.wB_out[cb * 2 + hh], W=[b_wt])
                for s in range(NSUB):
                    for hl in range(8):
                        h = hh * 8 + hl
                        g, r = h // 4, h % 4
                        ro = (r % 2) * 2 + r // 2
                        self.pe(lambda: nc.tensor.matmul(pms[s][0][:, :], lhsT=yB64[0:64, g, s, ro * 128:(ro + 1) * 128], rhs=wt[0:64, hl, :],
                                                         start=(h == 0), stop=(h == 15)), R=[b_yB, b_wt], W=[pms[s][1]], signal=(hl == 7))
            for s in range(NSUB):
                self.dve(lambda: nc.vector.tensor_tensor(out=self.x_tm[:, s, cb * 512:(cb + 1) * 512], in0=self.x_tm[:, s, cb * 512:(cb + 1) * 512],
                                                         in1=pms[s][0][:, :], op=ALU.add), R=[pms[s][1], self.b_x], W=[self.b_x])
        if getattr(self, 'stop_after', None) == 'wout':
            return
        self.rms_T()
        yout = kvtm[:, :, 0:D]
        for s in range(NSUB):
            self.dve(lambda: nc.vector.scalar_tensor_tensor(out=yout[:, s, :], in0=self.x_tm[:, s, :], scalar=self.st4[:, 4 + s:5 + s], in1=self.fgbc[:, :],
                                                            op0=ALU.mult, op1=ALU.mult), R=[self.b_x, self.b_st4, self.b_fgbc], W=[b_kvtm])
            self.st(self.y_p[t0 + s * 128:t0 + (s + 1) * 128, :], yout[:, s, :], R=[b_kvtm], q="pool")

    def sample_phase(self):
        nc = self.nc
        S = self.S
        S.barrier()
        c = self.cst
        px, b_px = self.px
        po0, b_po0 = self.po[0]
        po1, b_po1 = self.po[1]
        KAf = self.KA[:].rearrange("p a b -> p (a b)")
        VSf = self.VS[:].rearrange("p a b c -> p (a b c)")
        b_KAf, b_VSf = self.b_KA, self.b_VS
        KR2 = [KAf[:, k * 4160:(k + 1) * 4160].rearrange("p (a b) -> p a b", b=1040) for k in range(2)]
        KsTs = KAf[:, 8320:8320 + 4096].rearrange("p (a b) -> p a b", b=2048)
        KCTs = KAf[:, 12352:12352 + 4032]
        VSs = VSf[:, 0:8192].rearrange("p (a b c) -> p a b c", b=4, c=128)
        VCs = VSf[:, 8192:12288].rearrange("p (a b c) -> p a b c", b=4, c=128)
        KCs = VSf[:, 12288:16384].rearrange("p (a b) -> p a b", b=1024)
        cv = KB.Carver(self)
        hS, b_hS = cv.F("hS", [128, 4096])
        small, b_small = cv.F("small", [128, 64])
        mask01, b_mask = cv.F("mask01", [128, 256])
        wgt, b_wgt = cv.F("wgt", [128, 260])
        scoreS, b_scoreS = cv.F("scoreS", [128, 256])
        scrS, b_scrS = cv.F("scrS", [128, 256])
        accS, b_accS = cv.F("accS", [128, 16])
        tS, b_tS = cv.F("tS", [128, 16])
        qb2, b_qb2 = cv.H("qb2", [128, 2, 1024])
        knb, b_knb = cv.H("knb", [128, 2, 256])
        szb, b_szb = cv.H("szb", [128, 3072])
        qTS, b_qTS = cv.H("qTS", [128, 2, 32, SPC])
        kTn, b_kTn = cv.H("kTn", [128, 2, 2, SPC])
        szTS, b_szTS = cv.H("szTS", [128, 48, SPC])
        vnewA, b_vnewA = cv.H("vnewA", [128, 2, 4, 128])
        vrow0, b_vrow0 = cv.H("vrow0", [128, 2, 4, 128])
        zrow, b_zrow = cv.H("zrow", [128, 512])
        yTSall, b_yTSall = cv.H("yTSall", [128, 16, SPC])
        ovS, b_ovS = cv.H("ovS", [128, 8, 258])
        pgb = [cv.H("pgb%d" % k, [128, 512]) for k in range(4)]
        hidTs, b_hidTs = cv.H("hidTs", [128, 4, 128])
        vsts2 = [cv.H("vsts%d" % k, [128, 2, 128]) for k in range(2)]
        PTc, b_PTc = cv.H("PTc", [128, 8, 16])
        PTs, b_PTs = cv.H("PTs", [128, 4, 16, 4])
        M4, b_M4 = KAf[:, 12416:12416 + 2048].rearrange("p (a b c) -> p a b c", b=128, c=4), self.nb("M4")
        KwTs, b_KwTs = cv.H("KwTs", [128, 2, 512])
        VWs, b_VWs = cv.H("VWs", [128, 4, 4, 128])
        pnew, b_pnew = cv.H("pnew", [128, 2, 16])
        b_kr2 = [[self.nb("kr%d_%d" % (b_, k)) for k in range(8)] for b_ in range(2)]
        b_carry2 = [self.nb("kcarry0"), self.nb("kcarry1")]
        b_kst = [self.nb("kst%d" % k) for k in range(16)]
        b_vss = [self.nb("vss%d" % k) for k in range(16)]
        b_VCs = self.nb("VCs")
        b_KCs = self.nb("KCs")
        idxf, b_idxf = cv.F("idxf", [128, 128])
        idxi_t, b_idxi = self.sb("idxi", [128, 128], I32)
        ptbi_t, b_ptbi = self.sb("ptbi", [128, 128], I32)
        iop_t, b_iop = self.sb("iop", [128, 1], I32)
        iopf, b_iopf = cv.F("iopf", [128, 1])
        grps, b_grps = cv.F("grps", [128, 4])
        oh4, b_oh4 = cv.F("oh4", [128, 4, 128])
        forceS, b_forceS = cv.F("forceS", [128, 256])
        ropeS, b_ropeS = cv.F("ropeS", [128, 2, 16, 8])
        rtS, b_rtS = cv.F("rtS", [128, 4, 16, 8])
        w00, b_w00 = cv.F("w00", [128, 2, 16])
        gbrow, b_gbrow = cv.F("gbrow", [128, 48])

        self.ld(ovS, c["c_ovaug_s"].rearrange("n p c -> p n c"), W=[b_ovS], q="pool")
        self.ld(grps[0:16, :], c["c_grpsel"][:, :], W=[b_grps], q="pool")
        self.ld(oh4[0:4, :, :], c["c_onehot4"][:, :, :], W=[b_oh4], q="pool")
        self.ld(forceS[0:4, :], c["c_force_s"][:, :], W=[b_forceS], q="pool")
        self.ld(ropeS[0:SPC], c["c_ropes16"].rearrange("t s h c -> s t h c"), W=[b_ropeS], q="pool")
        self.ld(w00[0:SPC, 0, :], self.a_w_s[:, 0, 0].partition_broadcast(SPC), W=[b_w00], q="pool", slow=True)
        self.ld(w00[0:SPC, 1, :], self.a_b_s[:, 0].partition_broadcast(SPC), W=[b_w00], q="pool", slow=True)
        self.ld(gbrow[0:SPC, :], self.b_gate.partition_broadcast(SPC), W=[b_gbrow], q="pool")
        self.pool(lambda: nc.gpsimd.iota(out=iop_t[:], pattern=[[0, 1]], base=0, channel_multiplier=1), W=[b_iop])
        self.dve(lambda: nc.vector.tensor_copy(out=iopf, in_=iop_t[:]), R=[b_iop], W=[b_iopf])
        self.dve(lambda: nc.vector.memset(zrow, 0.0), W=[b_zrow])
        self.dve(lambda: nc.vector.memset(vnewA[:, :, :, :], 1.0), W=[b_vnewA])

        gvS = hS[:, 0:2048]
        lngS = hS[:, 2048:4096]
        guS, b_guS = szb[:, 0:2048], b_szb
        szS_, b_szS = qb2[:, :, :].rearrange("p a b -> p (a b)"), b_qb2
        self.ld(self.x_tm[0:SPC, 0, :], self.x_s[:, :], W=[self.b_x])
        self.rms_T(SPC, SPC)
        self.xhat_T(SPC, SPC)
        self.ld(lngS[0:SPC, :], self.a_ln_g.partition_broadcast(SPC), W=[b_hS], q="pool")
        for cb in range(12):
            wt, b_wt = self.next_wst()
            self.ld(wt[:], self.wA_in[cb], W=[b_wt])
            pm, b_pm = self.next_pm()
            self.mm8(pm[0:SPC, :], b_pm, lambda kc: self.xnT[:, kc, 0:SPC], lambda kc: wt[:, kc, :], [self.b_xnT, b_wt])
            if cb < 4:
                self.act(lambda: nc.scalar.activation(out=gvS[0:SPC, cb * 512:(cb + 1) * 512], in_=pm[0:SPC, :], func=AF.Gelu), R=[b_pm], W=[b_hS])
            elif cb < 8:
                self.act(lambda: nc.scalar.activation(out=guS[0:SPC, (cb - 4) * 512:(cb - 3) * 512], in_=pm[0:SPC, :], func=AF.Gelu), R=[b_pm], W=[b_guS])
            else:
                self.act(lambda: nc.scalar.activation(out=szS_[0:SPC, (cb - 8) * 512:(cb - 7) * 512], in_=pm[0:SPC, :], func=AF.Silu), R=[b_pm], W=[b_szS])
        stt = small[:, 0:24].rearrange("p (a b) -> p a b", b=6)
        for cb in range(4):
            self.dve(lambda: nc.vector.bn_stats(out=stt[0:SPC, cb, :], in_=gvS[0:SPC, cb * 512:(cb + 1) * 512]), R=[b_hS], W=[b_small])
        self.dve(lambda: nc.vector.bn_aggr(out=small[0:SPC, 24:26], in_=stt[0:SPC, :, :]), R=[b_small], W=[b_small])
        self.dve(lambda: nc.vector.tensor_scalar(out=small[0:SPC, 26:27], in0=small[0:SPC, 25:26], scalar1=EPS, scalar2=None, op0=ALU.add), R=[b_small], W=[b_small])
        self.act(lambda: nc.scalar.activation(out=small[0:SPC, 26:27], in_=small[0:SPC, 26:27], func=AF.Sqrt), R=[b_small], W=[b_small])
        self.dve(lambda: nc.vector.reciprocal(out=small[0:SPC, 27:28], in_=small[0:SPC, 26:27]), R=[b_small], W=[b_small])
        self.dve(lambda: nc.vector.tensor_scalar(out=gvS[0:SPC, :], in0=gvS[0:SPC, :], scalar1=small[0:SPC, 24:25], scalar2=small[0:SPC, 27:28],
                                                 op0=ALU.subtract, op1=ALU.mult), R=[b_hS, b_small], W=[b_hS])
        self.dve(lambda: nc.vector.tensor_tensor(out=gvS[0:SPC, :], in0=gvS[0:SPC, :], in1=lngS[0:SPC, :], op=ALU.mult), R=[b_hS], W=[b_hS])
        self.ld(lngS[0:SPC, :], self.a_ln_b.partition_broadcast(SPC), R=[b_hS], W=[b_hS], q="pool")
        self.dve(lambda: nc.vector.tensor_tensor(out=gvS[0:SPC, :], in0=gvS[0:SPC, :], in1=lngS[0:SPC, :], op=ALU.add), R=[b_hS], W=[b_hS])
        self.st(self.o_chv[:, :], gvS[0:SPC, :], R=[b_hS], q="pool")
        mixS = hS[:, 2048:4096]
        for g in range(16):
            self.dve(lambda: nc.vector.tensor_scalar(out=mixS[0:SPC, g * 128:(g + 1) * 128], in0=gvS[0:SPC, g * 128:(g + 1) * 128], scalar1=w00[0:SPC, 0, g:g + 1],
                                                     scalar2=w00[0:SPC, 1, g:g + 1], op0=ALU.mult, op1=ALU.add), R=[b_hS, b_w00], W=[b_hS])
        self.dve(lambda: nc.vector.tensor_tensor(out=mixS[0:SPC, :], in0=mixS[0:SPC, :], in1=guS[0:SPC, :], op=ALU.mult), R=[b_hS, b_guS], W=[b_hS])
        ySb, b_ySb = KAf[:, 0:2048], b_KAf
        self.dve(lambda: nc.vector.tensor_tensor(out=ySb[0:SPC, :], in0=mixS[0:SPC, :], in1=szS_[0:SPC, :], op=ALU.mult), R=[b_hS, b_szS], W=[b_ySb])
        yTS, b_yTS = cv.H("yTS", [128, 16, SPC])
        pt, b_pt = self.next_ptr()
        for fc in range(16):
            self.pe(lambda: nc.tensor.transpose(out=pt[:, fc * SPC:(fc + 1) * SPC], in_=ySb[0:SPC, fc * 128:(fc + 1) * 128], identity=self.ident[0:SPC, 0:SPC]),
                    R=[b_ySb, self.b_ident], W=[b_pt], signal=(fc == 15))
        self.act(lambda: nc.scalar.copy(out=yTS[:, :, :], in_=pt[:, 0:16 * SPC].rearrange("p (a b) -> p a b", b=SPC)), R=[b_pt], W=[b_yTS])
        for cb in range(2):
            pm, b_pm = self.next_pm()
            for fcg in range(4):
                wt, b_wt = self.next_wst()
                self.ld(wt[:, 0:4, :], self.wA_out[fcg * 2 + cb], W=[b_wt])
                for fl in range(4):
                    fc = fcg * 4 + fl
                    self.pe(lambda: nc.tensor.matmul(pm[0:SPC, :], lhsT=yTS[:, fc, :], rhs=wt[:, fl, :], start=(fc == 0), stop=(fc == 15)),
                            R=[b_yTS, b_wt], W=[b_pm], signal=(fl == 3))
            self.dve(lambda: nc.vector.tensor_tensor(out=self.x_tm[0:SPC, 0, cb * 512:(cb + 1) * 512], in0=self.x_tm[0:SPC, 0, cb * 512:(cb + 1) * 512],
                                                     in1=pm[0:SPC, :], op=ALU.add), R=[b_pm, self.b_x], W=[self.b_x])

        self.rms_T(SPC, SPC)
        self.xhat_T(SPC, SPC)
        slabs = [(self.wB_q[0], 0, None), (self.wB_q[1], 512, None)]
        slabs += [(self.wB_kv[k], 1024 + 512 * k, None) for k in range(3)]
        slabs += [(self.wB_z[k], 512 * k, AF.Silu) for k in range(6)]
        for (src, c0, fn) in slabs:
            wt, b_wt = self.next_wst()
            self.ld(wt[:], src, W=[b_wt])
            pm, b_pm = self.next_pm()
            self.mm8(pm[0:SPC, :], b_pm, lambda kc: self.xnT[:, kc, 0:SPC], lambda kc: wt[:, kc, :], [self.b_xnT, b_wt])
            if fn is None:
                self.act(lambda: nc.scalar.copy(out=hS[0:SPC, c0:c0 + 512], in_=pm[0:SPC, :]), R=[b_pm], W=[b_hS])
            else:
                self.act(lambda: nc.scalar.activation(out=szb[0:SPC, c0:c0 + 512], in_=pm[0:SPC, :], func=fn), R=[b_pm], W=[b_szb])
        pm, b_pm = self.next_pm()
        self.mm8(pm[0:SPC, 0:48], b_pm, lambda kc: self.xnT[:, kc, 0:SPC], lambda kc: self.wgl[:, kc, :], [self.b_xnT, self.b_wgl])
        self.dve(lambda: nc.vector.tensor_tensor(out=hS[0:SPC, 2560:2608], in0=pm[0:SPC, 0:48], in1=gbrow[0:SPC, :], op=ALU.add), R=[b_pm, b_gbrow], W=[b_hS])
        self.act(lambda: nc.scalar.activation(out=hS[0:SPC, 2560:2608], in_=hS[0:SPC, 2560:2608], func=AF.Sigmoid), R=[b_hS], W=[b_hS])
        qv = hS[0:SPC, 0:1024].rearrange("s (p a r d) -> s p a r d", p=2, a=2, r=4)
        for p in range(2):
            self.dve(lambda: nc.vector.tensor_copy(out=qb2[0:SPC, 0, p * 512:(p + 1) * 512].rearrange("s (r a d) -> s r a d", r=4, a=2),
                                                   in_=qv[:, p, :, :, :].rearrange("s a r d -> s r a d")), R=[b_hS], W=[b_qb2])
        def rope_tm(view, nh):
            x1 = view[:, :, 0:8]
            x2 = view[:, :, 8:16]
            cs = ropeS[0:SPC, 0, 0:nh, :]
            sn = ropeS[0:SPC, 1, 0:nh, :]
            R_ = [b_hS, b_ropeS]
            self.dve(lambda: nc.vector.tensor_tensor(out=rtS[0:SPC, 0, 0:nh, :], in0=x1, in1=cs, op=ALU.mult), R=R_, W=[b_rtS])
            self.dve(lambda: nc.vector.tensor_tensor(out=rtS[0:SPC, 1, 0:nh, :], in0=x2, in1=sn, op=ALU.mult), R=R_, W=[b_rtS])
            self.dve(lambda: nc.vector.tensor_tensor(out=rtS[0:SPC, 2, 0:nh, :], in0=x2, in1=cs, op=ALU.mult), R=R_, W=[b_rtS])
            self.dve(lambda: nc.vector.tensor_tensor(out=rtS[0:SPC, 3, 0:nh, :], in0=x1, in1=sn, op=ALU.mult), R=R_, W=[b_rtS])
            self.dve(lambda: nc.vector.tensor_tensor(out=x1, in0=rtS[0:SPC, 0, 0:nh, :], in1=rtS[0:SPC, 1, 0:nh, :], op=ALU.subtract), R=[b_rtS], W=[b_hS])
            self.dve(lambda: nc.vector.tensor_tensor(out=x2, in0=rtS[0:SPC, 2, 0:nh, :], in1=rtS[0:SPC, 3, 0:nh, :], op=ALU.add), R=[b_rtS], W=[b_hS])
        rope_tm(hS[0:SPC, 0:1024].rearrange("s (h d) -> s h d", d=64), 16)
        rope_tm(hS[0:SPC, 1536:1792].rearrange("s (h d) -> s h d", d=64), 4)
        rope_tm(hS[0:SPC, 2048:2304].rearrange("s (h d) -> s h d", d=64), 4)
        for p in range(2):
            self.dve(lambda: nc.vector.tensor_copy(out=qb2[0:SPC, 1, p * 512:(p + 1) * 512].rearrange("s (r a d) -> s r a d", r=4, a=2),
                                                   in_=qv[:, p, :, :, :].rearrange("s a r d -> s r a d")), R=[b_hS], W=[b_qb2])
        self.st(self.o_cmp_s[:, :], hS[0:SPC, 1024:1536], R=[b_hS], q="pool")
        self.st(self.o_sel_s[:, :], hS[0:SPC, 1536:2048], R=[b_hS], q="pool")
        self.st(self.o_win_s[:, 511, :], hS[0:SPC, 2048:2560], R=[b_hS], q="pool")
        for s in range(SPC):
            self.dma(lambda: nc.sync.dma_start(out=self.o_win_s[s, 0:511, :], in_=self.state_w[s, 1:512, :]), q="sp", out=True)
        self.dve(lambda: nc.vector.tensor_copy(out=knb[0:SPC, 0, :], in_=hS[0:SPC, 1536:1792]), R=[b_hS], W=[b_knb])
        self.dve(lambda: nc.vector.tensor_copy(out=knb[0:SPC, 1, :], in_=hS[0:SPC, 2048:2304]), R=[b_hS], W=[b_knb])
        self.dve(lambda: nc.vector.tensor_copy(out=vnewA[0:SPC, 0, :, 0:64], in_=hS[0:SPC, 1792:2048].rearrange("s (g d) -> s g d", d=64)), R=[b_hS], W=[b_vnewA])
        self.dve(lambda: nc.vector.tensor_copy(out=vnewA[0:SPC, 1, :, 0:64], in_=hS[0:SPC, 2304:2560].rearrange("s (g d) -> s g d", d=64)), R=[b_hS], W=[b_vnewA])
        idS = self.ident[0:SPC, 0:SPC]
        for v in range(2):
            pt, b_pt = self.next_ptr()
            for k in range(8):
                self.pe(lambda: nc.tensor.transpose(out=pt[:, k * SPC:(k + 1) * SPC], in_=qb2[0:SPC, v, k * 128:(k + 1) * 128], identity=idS),
                        R=[b_qb2, self.b_ident], W=[b_pt], signal=(k == 7))
            self.act(lambda: nc.scalar.copy(out=qTS[:, v, 0:8, :], in_=pt[:, 0:8 * SPC].rearrange("p (a b) -> p a b", b=SPC)), R=[b_pt], W=[b_qTS])
        pt, b_pt = self.next_ptr()
        for v in range(2):
            for p in range(2):
                k = v * 2 + p
                self.pe(lambda: nc.tensor.transpose(out=pt[:, k * SPC:(k + 1) * SPC], in_=knb[0:SPC, v, p * 128:(p + 1) * 128], identity=idS),
                        R=[b_knb, self.b_ident], W=[b_pt], signal=(k == 3))
        self.act(lambda: nc.scalar.copy(out=kTn[:, :, :, :].rearrange("p a b c -> p (a b) c"), in_=pt[:, 0:4 * SPC].rearrange("p (a b) -> p a b", b=SPC)),
                 R=[b_pt], W=[b_kTn])
        pt, b_pt = self.next_ptr()
        for k in range(48):
            self.pe(lambda: nc.tensor.transpose(out=pt[0:64, k * SPC:(k + 1) * SPC], in_=szb[0:SPC, k * 64:(k + 1) * 64], identity=idS),
                    R=[b_szb, self.b_ident], W=[b_pt], signal=(k == 47))
        self.act(lambda: nc.scalar.copy(out=szTS[0:64, :, :], in_=pt[0:64, 0:48 * SPC].rearrange("p (a b) -> p a b", b=SPC)), R=[b_pt], W=[b_szTS])
        S.barrier()

        KCsv = KCs
        for s in range(SPC):
            self.ld(ptbi_t[:], self.page_t[s].partition_broadcast(128), W=[b_ptbi], q="pool")
            self.dve(lambda: nc.vector.tensor_copy(out=idxf, in_=ptbi_t[:]), R=[b_ptbi], W=[b_idxf])
            self.dve(lambda: nc.vector.tensor_scalar(out=idxf, in0=idxf, scalar1=128.0, scalar2=iopf[:, 0:1], op0=ALU.mult, op1=ALU.add), R=[b_idxf, b_iopf], W=[b_idxf])
            self.dve(lambda: nc.vector.tensor_copy(out=idxi_t[:], in_=idxf), R=[b_idxf], W=[b_idxi])
            self.ld(vrow0[0:1, :, :, :], vnewA[s:s + 1, :, :, :], R=[b_vnewA], W=[b_vrow0], q="pool")
            self.pool(lambda: nc.gpsimd.memset(VCs[:, :, :, 0:64], 0.0), W=[b_VCs])
            self.pool(lambda: nc.gpsimd.memset(VCs[:, :, :, 64:128], 1.0), W=[b_VCs])
            self.ld(VCs[127:128, 7, :, :].rearrange("p a b -> p (a b)"), zrow[0:1, :], R=[b_zrow], W=[b_VCs], q="pool")
            self.pool(lambda: nc.gpsimd.memset(KCsv[:, :, :], 0.0), W=[b_KCs])
            if s == 0:
                self.pool(lambda: nc.gpsimd.memset(VSs[:, :, :, 64:128], 1.0), W=b_vss)
            self.pool(lambda: nc.gpsimd.memset(VWs[:, :, :, 64:128], 1.0), W=[b_VWs])
            w1vs = []
            for kv in range(2):
                wt, b_wt = self.wst[kv]
                w1v = wt[:].rearrange("p a b -> p (a b)").rearrange("p (j c) -> p j c", c=128)
                self.ld(w1v, self.w1bd[kv], W=[b_wt])
                w1vs.append((w1v, b_wt))
            deferred = []

            def compress_steps(gi):
                buf = gi % 2
                KRb = KR2[buf]
                nblk = 63 if gi == 0 else 64
                base = 16 if gi == 0 else 0
                m0 = 0 if gi == 0 else 64 * gi - 1
                pmh, b_pmh = self.next_pm()
                steps = []
                for kv in range(2):
                    w1v, b_wt = w1vs[kv]
                    for hp in range(2):
                        tl = kv * 2 + hp
                        for j0 in range(0, 32, 8):
                            def st_(tl=tl, j0=j0, w1v=w1v, b_wt=b_wt):
                                for j in range(j0, j0 + 8):
                                    self.pe(lambda: nc.tensor.matmul(pmh[:, tl * 64:tl * 64 + nblk], lhsT=w1v[:, j, :],
                                                                     rhs=KRb[:, tl, base + j:base + j + 16 * (nblk - 1) + 1:16],
                                                                     start=(tl == 0 and j == 0), stop=(j == 31), skip_group_check=True),
                                            R=[b_wt, b_carry2[buf]] + b_kr2[buf], W=[b_pmh], signal=(j == j0 + 7))
                            steps.append(st_)

                def tail():
                    for kv in range(2):
                        self.act(lambda: nc.scalar.activation(out=hidTs[:, 2 * kv:2 * kv + 2, 0:nblk],
                                                              in_=pmh[:, 128 * kv:128 * kv + 128].rearrange("p (a b) -> p a b", b=64)[:, :, 0:nblk],
                                                              func=AF.Silu, bias=self.bias1[:, kv:kv + 1]), R=[b_pmh, self.b_bias1], W=[b_hidTs])
                    pmk, b_pmk = self.next_pm()
                    for g in range(4):
                        self.pe(lambda: nc.tensor.matmul(pmk[:, g * 64:g * 64 + nblk], lhsT=self.W2K[:, g % 2, :], rhs=hidTs[:, g // 2, 0:nblk],
                                                         start=(g == 0), stop=True, skip_group_check=True), R=[self.b_W2K, b_hidTs], W=[b_pmk], signal=(g == 3))
                    self.act(lambda: nc.scalar.activation(out=KCsv[:, :, m0:m0 + nblk], in_=pmk[:, 0:256].rearrange("p (a b) -> p a b", b=64)[:, :, 0:nblk],
                                                          func=AF.Identity, bias=self.b2K[:, 0:1]), R=[b_pmk, self.b_b2K], W=[b_KCs])
                    pmv, b_pmv = self.next_pm()
                    for hp in range(2):
                        self.pe(lambda: nc.tensor.matmul(pmv[0:nblk, hp * 128:(hp + 1) * 128], lhsT=hidTs[:, 2 + hp, 0:nblk], rhs=self.W2V[:, :],
                                                         start=(hp == 0), stop=False, skip_group_check=True), R=[b_hidTs, self.b_W2V], W=[b_pmv], signal=False)
                        self.pe(lambda: nc.tensor.matmul(pmv[0:nblk, hp * 128:(hp + 1) * 128], lhsT=self.ones_row[0:1, 0:nblk], rhs=self.b2Vrow[0:1, :],
                                                         start=False, stop=True, skip_group_check=True), R=[self.b_ones, self.b_b2V], W=[b_pmv], signal=(hp == 1))
                    vsts, b_vsts_k = vsts2[gi % 2]
                    self.act(lambda: nc.scalar.copy(out=vsts[0:nblk, :, :], in_=pmv[0:nblk, 0:256].rearrange("p (a b) -> p a b", b=128)), R=[b_pmv], W=[b_vsts_k])
                    vv = vsts[:, :, :].rearrange("p a (h d) -> p (a h) d", d=64)

                    def place():
                        blk = m0
                        while blk < m0 + nblk:
                            nt_, p0 = blk // 128, blk % 128
                            cnt = min(m0 + nblk - blk, 128 - p0)
                            o = blk - m0
                            self.ld(VCs[p0:p0 + cnt, nt_, :, 0:64], vv[o:o + cnt], R=[b_vsts_k], W=[b_VCs], q="pool")
                            blk += cnt
                    deferred.append(place)
                steps.append(tail)
                return steps

            pending = []
            for gi in range(16):
                buf = gi % 2
                for pl in range(8):
                    page = gi * 8 + pl
                    pb, b_pb = pgb[page % 4]
                    self.dma(lambda: nc.gpsimd.indirect_dma_start(out=pb, out_offset=None, in_=self.cache_c[:, :],
                                                                  in_offset=bass.IndirectOffsetOnAxis(ap=idxi_t[:, page:page + 1], axis=0)),
                             R=[b_idxi], W=[b_pb], q="pool")
                    if pl == 4:
                        while deferred:
                            deferred.pop(0)()
                    pt, b_pt = self.next_ptr()
                    for tl in range(4):
                        self.pe(lambda: nc.tensor.transpose(out=pt[:, tl * 128:(tl + 1) * 128], in_=pb[:, tl * 128:(tl + 1) * 128], identity=self.ident[:]),
                                R=[b_pb, self.b_ident], W=[b_pt], signal=(tl == 3))
                    dst = KR2[buf][:, :, 16 + pl * 128:16 + (pl + 1) * 128]
                    if page % 2 == 0:
                        self.act(lambda: nc.scalar.copy(out=dst, in_=pt[:, 0:512].rearrange("p (a b) -> p a b", b=128)), R=[b_pt], W=[b_kr2[buf][pl]])
                    else:
                        self.dve(lambda: nc.vector.tensor_copy(out=dst, in_=pt[:, 0:512].rearrange("p (a b) -> p a b", b=128)), R=[b_pt], W=[b_kr2[buf][pl]])
                    share = -(-len(pending) // (8 - pl)) if pending else 0
                    for _ in range(share):
                        pending.pop(0)()
                self.dve(lambda: nc.vector.tensor_copy(out=KR2[1 - buf][:, :, 0:16], in_=KR2[buf][:, :, 1024:1040]), R=[b_kr2[buf][7]], W=[b_carry2[1 - buf]])
                assert not pending
                pending = compress_steps(gi)
            while pending:
                pending.pop(0)()
            while deferred:
                deferred.pop(0)()
            pmA, b_pmA = self.next_pm()
            pmB, b_pmB = self.next_pm()
            banks = [(pmA, b_pmA), (pmB, b_pmB)]
            for nt in range(8):
                for g in range(4):
                    p, a = g // 2, g % 2
                    pmx, b_pmx = banks[a]
                    col = (nt * 2 + p) * 4
                    self.pe(lambda: nc.tensor.matmul(pmx[:, col:col + 4], lhsT=KCsv[a * 64:(a + 1) * 64, g, nt * 128:(nt + 1) * 128],
                                                     rhs=qTS[a * 64:(a + 1) * 64, 0, p * 4:(p + 1) * 4, s], start=(nt == 0 and p == 0), stop=True, skip_group_check=True),
                            R=[b_KCs, b_qTS], W=[b_pmx], signal=(nt == 7 and p == 1))
            PTcv = PTc[:, :, :].rearrange("p n (q a r) -> p n q a r", q=2, a=2)
            for a in range(2):
                pmx, b_pmx = banks[a]
                self.act(lambda: nc.scalar.activation(out=PTcv[:, :, :, a, :], in_=pmx[:, 0:64].rearrange("p (n q r) -> p n q r", q=2, r=4), func=AF.Exp, scale=0.125),
                         R=[b_pmx], W=[b_PTc])
            for g in range(4):
                for nt in range(8):
                    self.pe(lambda: nc.tensor.matmul(po0[:, g * 4:(g + 1) * 4], lhsT=VCs[:, nt, g, :], rhs=PTc[:, nt, g * 4:(g + 1) * 4],
                                                     start=(g == 0 and nt == 0), stop=(nt == 7), skip_group_check=True), R=[b_VCs, b_PTc], W=[b_po0], signal=(nt == 7))
            for nt in range(8):
                self.pe(lambda: nc.tensor.matmul(px[0:16, 0:258], lhsT=PTc[:, nt, :], rhs=ovS[:, nt, :], start=(nt == 0), stop=(nt == 7)),
                        R=[b_PTc, b_ovS], W=[b_px], signal=(nt == 7))
            self.dve(lambda: nc.vector.tensor_scalar(out=small[0:16, 32:33], in0=px[0:16, 257:258], scalar1=1e-30, scalar2=None, op0=ALU.max), R=[b_px], W=[b_small])
            self.dve(lambda: nc.vector.reciprocal(out=small[0:16, 33:34], in_=small[0:16, 32:33]), R=[b_small], W=[b_small])
            self.dve(lambda: nc.vector.tensor_scalar(out=wgt[0:16, 0:256], in0=px[0:16, 0:256], scalar1=small[0:16, 33:34], scalar2=None, op0=ALU.mult), R=[b_px, b_small], W=[b_wgt])
            self.pe(lambda: nc.tensor.matmul(px[0:4, 0:256], lhsT=grps[0:16, :], rhs=wgt[0:16, 0:256], start=True, stop=True), R=[b_grps, b_wgt], W=[b_px])
            self.dve(lambda: nc.vector.tensor_tensor(out=scoreS[0:4, :], in0=px[0:4, 0:256], in1=forceS[0:4, :], op=ALU.add), R=[b_px, b_forceS], W=[b_scoreS])
            self.dve(lambda: nc.vector.max(out=small[0:4, 40:48], in_=scoreS[0:4, :]), R=[b_scoreS], W=[b_small])
            self.dve(lambda: nc.vector.match_replace(out=scrS[0:4, :], in_to_replace=small[0:4, 40:48], in_values=scoreS[0:4, :], imm_value=-1e30), R=[b_scoreS, b_small], W=[b_scrS])
            self.dve(lambda: nc.vector.max(out=small[0:4, 48:56], in_=scrS[0:4, :]), R=[b_scrS], W=[b_small])
            self.dve(lambda: nc.vector.tensor_scalar(out=mask01[0:4, :], in0=scoreS[0:4, :], scalar1=small[0:4, 54:55], scalar2=None, op0=ALU.is_ge), R=[b_scoreS, b_small], W=[b_mask])
            for g in range(4):
                self.pe(lambda: nc.tensor.matmul(px[:, 0:256], lhsT=oh4[0:4, g, :], rhs=mask01[0:4, :], start=True, stop=True), R=[b_oh4, b_mask], W=[b_px])
                pxv = px[:, 0:256].rearrange("p (k two) -> p k two", two=2)
                for r in range(4):
                    self.dve(lambda: nc.vector.tensor_copy(out=M4[0:64, g, :, r], in_=pxv[0:64, :, 0]), R=[b_px], W=[b_M4])
                    self.act(lambda: nc.scalar.copy(out=M4[64:128, g, :, r], in_=pxv[64:128, :, 1]), R=[b_px], W=[b_M4])

            first = True
            for pgp in range(8):
                for pl in range(16):
                    page = pgp * 16 + pl
                    pb, b_pb = pgb[page % 4]
                    self.dma(lambda: nc.gpsimd.indirect_dma_start(out=pb, out_offset=None, in_=self.cache_s[:, :],
                                                                  in_offset=bass.IndirectOffsetOnAxis(ap=idxi_t[:, page:page + 1], axis=0)),
                             R=[b_idxi], W=[b_pb], q="pool")
                    self.dve(lambda: nc.vector.tensor_copy(out=VSs[:, pl, :, 0:64], in_=pb[:, 256:512].rearrange("p (g d) -> p g d", d=64)), R=[b_pb], W=[b_vss[pl]])
                    pt, b_pt = self.next_ptr()
                    for p in range(2):
                        self.pe(lambda: nc.tensor.transpose(out=pt[:, p * 128:(p + 1) * 128], in_=pb[:, p * 128:(p + 1) * 128], identity=self.ident[:]),
                                R=[b_pb, self.b_ident], W=[b_pt], signal=(p == 1))
                    self.act(lambda: nc.scalar.copy(out=KsTs[:, :, pl * 128:(pl + 1) * 128], in_=pt[:, 0:256].rearrange("p (a b) -> p a b", b=128)), R=[b_pt], W=[b_kst[pl]])
                pmA, b_pmA = self.next_pm()
                pmB, b_pmB = self.next_pm()
                banks = [(pmA, b_pmA), (pmB, b_pmB)]
                for g in range(4):
                    p, a = g // 2, g % 2
                    pmx, b_pmx = banks[a]
                    for pl in range(16):
                        col = (p * 16 + pl) * 4
                        self.pe(lambda: nc.tensor.matmul(pmx[:, col:col + 4], lhsT=KsTs[a * 64:(a + 1) * 64, p, pl * 128:(pl + 1) * 128],
                                                         rhs=qTS[a * 64:(a + 1) * 64, 1, p * 4:(p + 1) * 4, s], start=(p == 0 and pl == 0), stop=True, skip_group_check=True),
                                R=[b_kst[pl], b_qTS], W=[b_pmx], signal=(pl == 15))
                PTsv = PTs[:, :, :, :].rearrange("p (q a) k r -> p q a k r", a=2)
                for a in range(2):
                    pmx, b_pmx = banks[a]
                    self.act(lambda: nc.scalar.activation(out=PTsv[:, :, a, :, :], in_=pmx[:, 0:128].rearrange("p (q k r) -> p q k r", q=2, r=4), func=AF.Exp, scale=0.125),
                             R=[b_pmx], W=[b_PTs])
                self.dve(lambda: nc.vector.tensor_tensor(out=PTs[:, :, :, :], in0=PTs[:, :, :, :], in1=M4[:, :, pgp * 16:(pgp + 1) * 16, :], op=ALU.mult), R=[b_PTs, b_M4], W=[b_PTs])
                for g in range(4):
                    for pl in range(16):
                        self.pe(lambda: nc.tensor.matmul(po1[:, g * 4:(g + 1) * 4], lhsT=VSs[:, pl, g, :], rhs=PTs[:, g, pl, :],
                                                         start=first, stop=False, skip_group_check=True), R=[b_vss[pl], b_PTs], W=[b_po1], signal=(pl == 15))
                        first = False
            def new_token(v, po, b_po, col0, qidx):
                pmA, b_pmA = self.next_pm()
                pmB, b_pmB = self.next_pm()
                bk = [(pmA, b_pmA), (pmB, b_pmB)]
                for g in range(4):
                    p, a = g // 2, g % 2
                    pmx, b_pmx = bk[a]
                    self.pe(lambda: nc.tensor.matmul(pmx[0:1, p * 4:(p + 1) * 4], lhsT=kTn[a * 64:(a + 1) * 64, v, p, s:s + 1],
                                                     rhs=qTS[a * 64:(a + 1) * 64, qidx, p * 4:(p + 1) * 4, s], start=(p == 0), stop=True, skip_group_check=True),
                            R=[b_kTn, b_qTS], W=[b_pmx])
                pv = pnew[0:1, v, :].rearrange("o (q a r) -> o q a r", q=2, a=2)
                for a in range(2):
                    pmx, b_pmx = bk[a]
                    self.act(lambda: nc.scalar.activation(out=pv[:, :, a, :], in_=pmx[0:1, 0:8].rearrange("o (q r) -> o q r", r=4), func=AF.Exp, scale=0.125), R=[b_pmx], W=[b_pnew])
                for g in range(4):
                    self.pe(lambda: nc.tensor.matmul(po[:, col0 + g * 4:col0 + (g + 1) * 4], lhsT=vrow0[0:1, v, g, :], rhs=pnew[0:1, v, g * 4:(g + 1) * 4],
                                                     start=False, stop=True, skip_group_check=True), R=[b_vrow0, b_pnew], W=[b_po], signal=(g == 3))
            new_token(0, po1, b_po1, 0, 1)

            wst32 = hS[:, 0:2048].rearrange("p (k c) -> p k c", c=512)
            self.ld(wst32, self.state_w[s].rearrange("(k p) c -> p k c", p=128), W=[b_hS])
            wkb = szb[:, 0:1024].rearrange("p (k c) -> p k c", c=256)
            self.dve(lambda: nc.vector.tensor_copy(out=wkb, in_=wst32[:, :, 0:256]), R=[b_hS], W=[b_szb])
            for kt in range(4):
                self.act(lambda: nc.scalar.copy(out=VWs[:, kt, :, 0:64], in_=wst32[:, kt, 256:512].rearrange("p (g d) -> p g d", d=64)), R=[b_hS], W=[b_VWs])
                pt, b_pt = self.next_ptr()
                for p in range(2):
                    self.pe(lambda: nc.tensor.transpose(out=pt[:, p * 128:(p + 1) * 128], in_=wkb[:, kt, p * 128:(p + 1) * 128], identity=self.ident[:]),
                            R=[b_szb, self.b_ident], W=[b_pt], signal=(p == 1))
                self.act(lambda: nc.scalar.copy(out=KwTs[:, :, kt * 128:(kt + 1) * 128], in_=pt[:, 0:256].rearrange("p (a b) -> p a b", b=128)), R=[b_pt], W=[b_KwTs])
            pmA, b_pmA = self.next_pm()
            pmB, b_pmB = self.next_pm()
            banks = [(pmA, b_pmA), (pmB, b_pmB)]
            for g in range(4):
                p, a = g // 2, g % 2
                pmx, b_pmx = banks[a]
                for kt in range(4):
                    col = (p * 4 + kt) * 4
                    self.pe(lambda: nc.tensor.matmul(pmx[:, col:col + 4], lhsT=KwTs[a * 64:(a + 1) * 64, p, kt * 128:(kt + 1) * 128],
                                                     rhs=qTS[a * 64:(a + 1) * 64, 1, p * 4:(p + 1) * 4, s], start=(p == 0 and kt == 0), stop=True, skip_group_check=True),
                            R=[b_KwTs, b_qTS], W=[b_pmx], signal=(kt == 3))
            PTw = PTs[:, :, 0:4, :]
            PTwv = PTw.rearrange("p (q a) k r -> p q a k r", a=2)
            for a in range(2):
                pmx, b_pmx = banks[a]
                self.act(lambda: nc.scalar.activation(out=PTwv[:, :, a, :, :], in_=pmx[:, 0:32].rearrange("p (q k r) -> p q k r", q=2, r=4), func=AF.Exp, scale=0.125),
                         R=[b_pmx], W=[b_PTs])
            self.dve(lambda: nc.vector.memset(PTs[0:1, :, 0, :], 0.0), W=[b_PTs])
            for g in range(4):
                for kt in range(4):
                    self.pe(lambda: nc.tensor.matmul(po0[:, 16 + g * 4:16 + (g + 1) * 4], lhsT=VWs[:, kt, g, :], rhs=PTs[:, g, kt, :],
                                                     start=False, stop=False, skip_group_check=True), R=[b_VWs, b_PTs], W=[b_po0], signal=(kt == 3))
            new_token(1, po0, b_po0, 16, 1)

            self.pe(lambda: nc.tensor.matmul(px[0:64, 0:48], lhsT=oh4[0:4, s, 0:64], rhs=hS[0:SPC, 2560:2608], start=True, stop=True), R=[b_oh4, b_hS], W=[b_px])
            for bi, (b, po, b_po, c0) in enumerate(((0, po0, b_po0, 0), (1, po1, b_po1, 0), (2, po0, b_po0, 16))):
                self.dve(lambda: nc.vector.tensor_scalar(out=tS[64:128, :], in0=po[64:128, c0:c0 + 16], scalar1=1e-30, scalar2=None, op0=ALU.max), R=[b_po], W=[b_tS])
                self.dve(lambda: nc.vector.reciprocal(out=tS[0:64, :], in_=tS[64:128, :]), R=[b_tS], W=[b_tS])
                self.dve(lambda: nc.vector.tensor_tensor(out=tS[0:64, :], in0=tS[0:64, :], in1=po[0:64, c0:c0 + 16], op=ALU.mult), R=[b_tS, b_po], W=[b_tS])
                self.dve(lambda: nc.vector.tensor_tensor(out=tS[0:64, :], in0=tS[0:64, :], in1=px[0:64, b * 16:(b + 1) * 16], op=ALU.mult), R=[b_tS, b_px], W=[b_tS])
                if bi == 0:
                    self.dve(lambda: nc.vector.tensor_tensor(out=accS[0:64, :], in0=tS[0:64, :], in1=szTS[0:64, b * 16:(b + 1) * 16, s], op=ALU.mult), R=[b_tS, b_szTS], W=[b_accS])
                else:
                    self.dve(lambda: nc.vector.tensor_tensor(out=tS[0:64, :], in0=tS[0:64, :], in1=szTS[0:64, b * 16:(b + 1) * 16, s], op=ALU.mult), R=[b_tS, b_szTS], W=[b_tS])
                    self.dve(lambda: nc.vector.tensor_tensor(out=accS[0:64, :], in0=accS[0:64, :], in1=tS[0:64, :], op=ALU.add), R=[b_tS, b_accS], W=[b_accS])
            self.dve(lambda: nc.vector.tensor_copy(out=yTSall[0:64, :, s], in_=accS[0:64, :]), R=[b_accS], W=[b_yTSall])

        for cb in range(2):
            pm, b_pm = self.next_pm()
            for hh in range(2):
                wt, b_wt = self.next_wst()
                self.ld(wt[0:64, :, :], self.wB_out[cb * 2 + hh], W=[b_wt])
                for hl in range(8):
                    h = hh * 8 + hl
                    self.pe(lambda: nc.tensor.matmul(pm[0:SPC, :], lhsT=yTSall[0:64, h, :], rhs=wt[0:64, hl, :], start=(h == 0), stop=(h == 15)),
                            R=[b_yTSall, b_wt], W=[b_pm], signal=(hl == 7))
            self.dve(lambda: nc.vector.tensor_tensor(out=self.x_tm[0:SPC, 0, cb * 512:(cb + 1) * 512], in0=self.x_tm[0:SPC, 0, cb * 512:(cb + 1) * 512],
                                                     in1=pm[0:SPC, :], op=ALU.add), R=[b_pm, self.b_x], W=[self.b_x])
        self.rms_T(SPC, SPC)
        youtS = hS[:, 2048:3072]
        self.dve(lambda: nc.vector.scalar_tensor_tensor(out=youtS[0:SPC, :], in0=self.x_tm[0:SPC, 0, :], scalar=self.st4[0:SPC, 4:5], in1=self.fgbc[0:SPC, :],
                                                        op0=ALU.mult, op1=ALU.mult), R=[self.b_x, self.b_st4, self.b_fgbc], W=[b_hS])
        self.st(self.y_s[:, :], youtS[0:SPC, :], R=[b_hS], q="pool")

    def build(self, debug_stage=None):
        nc = self.nc
        self.alloc()
        self.prologue()
        for i in range(self.ntiles):
            self.layer_a(i)
            if debug_stage == "A":
                for s in range(NSUB):
                    self.st(self.y_p[i * T + s * 128:i * T + (s + 1) * 128, :], self.x_tm[:, s, :], R=[self.b_x])
                self.S.barrier()
                continue
            self.S.barrier()
            self.layer_b(i)
            self.S.barrier()
        if self.do_sample and debug_stage is None:
            self.sample_phase()
        self.S.finish("sp")
        return nc


def core_inputs(inp, c, n_phys, dummy_cache=False):
    m = {}
    m["x_prompt"] = np.ascontiguousarray(inp["x_prompt"][c])
    m["x_sample"] = np.ascontiguousarray(inp["x_sample"][c * SPC:(c + 1) * SPC, 0, :])
    if dummy_cache:
        m["cache_cmp_kv"] = np.zeros((n_phys * 128, 512), np.float32)
        m["cache_sel_kv"] = np.zeros((n_phys * 128, 512), np.float32)
    else:
        m["cache_cmp_kv"] = inp["cache_cmp_kv"].reshape(n_phys * 128, 512)
        m["cache_sel_kv"] = inp["cache_sel_kv"].reshape(n_phys * 128, 512)
    m["state_win_kv"] = np.ascontiguousarray(inp["state_win_kv"][0, c * SPC:(c + 1) * SPC].reshape(SPC, 512, 512))
    m["page_table"] = np.ascontiguousarray(inp["page_table"][c * SPC:(c + 1) * SPC])
    m["norm_g"] = inp["norm_g"]
    m["final_norm_g"] = inp["final_norm_g"]
    m["a_w_in"] = inp["a_w_in"][0]
    m["a_ln_g"] = inp["a_ln_g"][0]
    m["a_ln_b"] = inp["a_ln_b"][0]
    m["a_w_s"] = inp["a_w_s"][0]
    m["a_b_s"] = inp["a_b_s"][0]
    m["a_w_out"] = inp["a_w_out"][0]
    m["b_w_in"] = inp["b_w_in"][0]
    m["b_cmp_pe"] = inp["b_cmp_pe"][0]
    m["b_cmp_w1"] = inp["b_cmp_w1"][0]
    m["b_cmp_b1"] = inp["b_cmp_b1"][0]
    m["b_cmp_w2"] = inp["b_cmp_w2"][0]
    m["b_cmp_b2"] = inp["b_cmp_b2"][0]
    m["b_gate_b"] = inp["b_gate_b"][0].reshape(48)
    m["b_w_out"] = inp["b_w_out"][0]
    m.update(host_consts())
    return m


_NC_CACHE = {}


def kernel(x_prompt, x_sample, cache_cmp_kv, cache_sel_kv, state_win_kv, page_table,
           norm_g, final_norm_g, a_w_in, a_ln_g, a_ln_b, a_w_s, a_b_s, a_w_out,
           b_w_in, b_cmp_pe, b_cmp_w1, b_cmp_b1, b_cmp_w2, b_cmp_b2, b_gate_b, b_w_out):
    inp = dict(x_prompt=x_prompt, x_sample=x_sample, cache_cmp_kv=cache_cmp_kv, cache_sel_kv=cache_sel_kv,
               state_win_kv=state_win_kv, page_table=page_table, norm_g=norm_g, final_norm_g=final_norm_g,
               a_w_in=a_w_in, a_ln_g=a_ln_g, a_ln_b=a_ln_b, a_w_s=a_w_s, a_b_s=a_b_s, a_w_out=a_w_out,
               b_w_in=b_w_in, b_cmp_pe=b_cmp_pe, b_cmp_w1=b_cmp_w1, b_cmp_b1=b_cmp_b1, b_cmp_w2=b_cmp_w2,
               b_cmp_b2=b_cmp_b2, b_gate_b=b_gate_b, b_w_out=b_w_out)
    inp = {k: np.asarray(v) for k, v in inp.items()}
    n_phys = inp["cache_cmp_kv"].shape[1]
    if n_phys not in _NC_CACHE:
        kb = KB(n_phys=n_phys)
        _NC_CACHE[n_phys] = kb.build()
    nc = _NC_CACHE[n_phys]
    in_maps = [core_inputs(inp, c, n_phys) for c in range(NCORES)]
    res = run_bass_kernel_spmd(nc, in_maps, core_ids=list(range(NCORES)))
    r = res.results
    nb = SPC * NCORES
    y_prompt = np.stack([r[c]["y_prompt"] for c in range(NCORES)], 0)
    y_sample = np.concatenate([r[c]["y_sample"] for c in range(NCORES)], 0).reshape(nb, 1, D)
    cmp_p = np.stack([r[c]["cmp_kv_prompt"] for c in range(NCORES)], 0).reshape(1, NCORES, SEQ, 2, NKV, HD)
    cmp_s = np.concatenate([r[c]["cmp_kv_sample"] for c in range(NCORES)], 0).reshape(1, nb, 1, 2, NKV, HD)
    sel_p = np.stack([r[c]["sel_kv_prompt"] for c in range(NCORES)], 0).reshape(1, NCORES, SEQ, 2, NKV, HD)
    sel_s = np.concatenate([r[c]["sel_kv_sample"] for c in range(NCORES)], 0).reshape(1, nb, 1, 2, NKV, HD)
    win_p = np.stack([r[c]["win_kv_prompt"] for c in range(NCORES)], 0).reshape(1, NCORES, 512, 2, NKV, HD)
    win_s = np.concatenate([r[c]["win_kv_sample"] for c in range(NCORES)], 0).reshape(1, nb, 512, 2, NKV, HD)
    chv = np.concatenate([r[c]["chunk_v_sample"] for c in range(NCORES)], 0).reshape(1, nb, 1, AW)
    outs = (y_prompt, y_sample, cmp_p, cmp_s, sel_p, sel_s, win_p, win_s, chv)
    return tuple(np.ascontiguousarray(o.astype(np.float32)) for o in outs)
```

```python
import math
import numpy as np
import ml_dtypes
import concourse.bass as bass
import concourse.mybir as mybir
from concourse.bass_utils import run_bass_kernel_spmd

F32 = mybir.dt.float32
BF16 = mybir.dt.bfloat16
I32 = mybir.dt.int32
AF = mybir.ActivationFunctionType
ALU = mybir.AluOpType
AX = mybir.AxisListType

D = 1024
SEQ = 4096
T = 256
NSUB = 2
NTILES = SEQ // T
AW = 2048
NH = 16
HD = 64
NKV = 4
B_IN = 5680
PAST = 16384
NPAGE = 128
NEG = -30000.0
EPS = 1e-6
NCORES = 8
SPC = 4


class Buf:
    __slots__ = ("name", "wr", "rd", "excl")

    def __init__(self, name, excl=False):
        self.name = name
        self.wr = None
        self.rd = []
        self.excl = excl


class Sync:
    def __init__(self, nc, same_engine=True, n_dma_sems=40):
        self.nc = nc
        self.eng = {"pe": nc.tensor, "act": nc.scalar, "dve": nc.vector, "pool": nc.gpsimd, "sp": nc.sync}
        self.sem = {e: nc.alloc_semaphore("s_" + e) for e in self.eng}
        self.cnt = {e: 0 for e in self.eng}
        self.pend = {e: False for e in self.eng}
        self.waited = {e: {} for e in self.eng}
        self.same_engine = same_engine
        self.dma_sems = [nc.alloc_semaphore("dq%d" % i) for i in range(n_dma_sems)]
        self.dma_val = [0] * n_dma_sems
        self.dma_rr = 0
        self.out_events = []
        self.n_ins = 0
        self.n_wait = 0

    def _wait(self, e, ev):
        sem, val, src = ev
        if src == e and (e == "pe" or not self.same_engine):
            return
        key = id(sem)
        if self.waited[e].get(key, 0) >= val:
            return
        self.eng[e].wait_ge(sem, val)
        self.n_wait += 1
        self.waited[e][key] = val

    def _deps(self, e, R, W):
        for b in R:
            if b.wr is not None:
                self._wait(e, b.wr)
            if b.excl:
                for ev in b.rd:
                    self._wait(e, ev)
        for b in W:
            if b.wr is not None:
                self._wait(e, b.wr)
            for ev in b.rd:
                self._wait(e, ev)

    def _record(self, ev, R, W):
        for b in R:
            if b.excl:
                b.wr = ev
                b.rd = []
            else:
                b.rd.append(ev)
                if len(b.rd) > 64:
                    b.rd = b.rd[-64:]
        for b in W:
            b.wr = ev
            b.rd = []

    def op(self, e, fn, R=(), W=(), signal=True):
        self._deps(e, R, W)
        ins = fn()
        self.n_ins += 1
        if signal:
            self.cnt[e] += 1
            ins.then_inc(self.sem[e], 1)
            ev = (self.sem[e], self.cnt[e], e)
            self.pend[e] = False
        else:
            ev = (self.sem[e], self.cnt[e] + 1, e)
            self.pend[e] = True
        self._record(ev, R, W)
        return ins

    def dma(self, q, fn, R=(), W=(), is_output=False):
        k = self.dma_rr
        self.dma_rr = (self.dma_rr + 1) % len(self.dma_sems)
        sem = self.dma_sems[k]
        if self.dma_val[k] > 0:
            self._wait(q, (sem, self.dma_val[k], "dma"))
        self._deps(q, R, W)
        ins = fn()
        self.n_ins += 1
        self.dma_val[k] += 16
        ins.then_inc(sem, 16)
        ev = (sem, self.dma_val[k], "dma")
        self._record(ev, R, W)
        if is_output:
            self.out_events.append(ev)
        return ins

    def barrier(self):
        evs = []
        for e in self.eng:
            assert not self.pend[e], "pending unsignalled instruction on " + e
            if self.cnt[e] > 0:
                evs.append((self.sem[e], self.cnt[e], e))
        for k, sem in enumerate(self.dma_sems):
            if self.dma_val[k] > 0:
                evs.append((sem, self.dma_val[k], "dma"))
        for e in self.eng:
            for ev in evs:
                if ev[2] != e:
                    self._wait(e, ev)

    def finish(self, e="sp"):
        for ev in self.out_events:
            self._wait(e, ev)


def _rope_tables():
    half = 8
    freqs = np.exp(-math.log(500000.0) * np.arange(half, dtype=np.float32) * np.float32(2.0 / 16)).astype(np.float32)
    pos = np.arange(SEQ, dtype=np.float32)
    ang = (pos[:, None] * freqs[None, :]).astype(np.float32)
    cos = np.cos(ang).astype(np.float32)
    sin = np.sin(ang).astype(np.float32)
    angs = (np.float32(PAST) * freqs).astype(np.float32)
    return cos, sin, np.cos(angs).astype(np.float32), np.sin(angs).astype(np.float32)


_CONST_CACHE = {}


def host_consts():
    if _CONST_CACHE:
        return _CONST_CACHE
    bf = ml_dtypes.bfloat16
    c = {}
    c["c_ident"] = np.eye(128, dtype=np.float32).astype(bf)
    kk = np.arange(128)[:, None]
    qq = np.arange(128)[None, :]
    c["c_tri"] = np.where(kk <= qq, 0.0, NEG).astype(np.float32).astype(bf)
    c["c_atri"] = np.where(kk > qq, 0.0, NEG).astype(np.float32).astype(bf)
    ps = np.zeros((128, 128), np.float32)
    for m in range(128):
        d = m % 64
        if d < 8:
            ps[m + 8, m] = 1.0
        elif d < 16:
            ps[m - 8, m] = 1.0
    c["c_pswap"] = ps.astype(bf)
    cos, sin, cos_s, sin_s = _rope_tables()
    cosF = np.ones((128, SEQ), np.float32)
    sinF = np.zeros((128, SEQ), np.float32)
    for hh in range(2):
        cosF[hh * 64:hh * 64 + 8] = cos.T
        cosF[hh * 64 + 8:hh * 64 + 16] = cos.T
        sinF[hh * 64:hh * 64 + 8] = -sin.T
        sinF[hh * 64 + 8:hh * 64 + 16] = sin.T
    c["c_cosF"] = cosF
    c["c_sinF"] = sinF
    c["c_costm4"] = np.ascontiguousarray(np.tile(cos[:, None, :], (1, 4, 1)))
    c["c_sintm4"] = np.ascontiguousarray(np.tile(sin[:, None, :], (1, 4, 1)))
    c["c_ropes16"] = np.ascontiguousarray(np.stack([np.tile(cos_s[None, None], (SPC, 16, 1)), np.tile(sin_s[None, None], (SPC, 16, 1))], 0).astype(np.float32))
    ind = np.zeros((64, SEQ), np.float32)
    for j in range(64):
        ind[j, j * 64:(j + 1) * 64] = -NEG
    c["c_ind"] = ind.astype(bf)
    fb = np.zeros((32, 128, 64), np.float32)
    for qt in range(32):
        t = qt * 128 + np.arange(128)
        jq = t // 64
        jj = np.arange(64)[None, :]
        forced = (jj == 0) | (jj == jq[:, None]) | (jj == jq[:, None] - 1)
        fb[qt] = np.where(jj > jq[:, None], -10.0, np.where(forced, 10.0, 0.0))
    c["c_force"] = fb
    cb = np.zeros((32, 2, 128, 128), np.float32)
    for qt in range(32):
        for nt in range(2):
            n = nt * 128 + np.arange(128)[:, None]
            t = qt * 128 + np.arange(128)[None, :]
            cb[qt, nt] = np.where(16 * n + 31 <= t, 0.0, NEG)
    c["c_cbias"] = cb.astype(bf)
    ov = np.zeros((2, 128, 65), np.float32)
    for nt in range(2):
        for p in range(128):
            n = nt * 128 + p
            for j in range(64):
                if 16 * n <= 64 * j + 63 and 16 * n + 31 >= 64 * j:
                    ov[nt, p, j] = 1.0
            ov[nt, p, 64] = 1.0
    c["c_ovaug"] = ov.astype(bf)
    E = np.zeros((48, 48, 64), np.float32)
    for k in range(48):
        E[k, k, :] = 1.0
    c["c_esel"] = E.astype(bf)
    ovs = np.zeros((8, 128, 258), np.float32)
    for nt in range(8):
        for p in range(128):
            n = nt * 128 + p
            if n >= 1023:
                continue
            j0 = max(0, (16 * n - 63 + 63) // 64 - 1)
            for j in range(j0, min(257, j0 + 4)):
                if 16 * n <= 64 * j + 63 and 16 * n + 31 >= 64 * j:
                    ovs[nt, p, j] = 1.0
            ovs[nt, p, 257] = 1.0
    c["c_ovaug_s"] = ovs.astype(bf)
    fs = np.zeros((4, 256), np.float32)
    fs[:, 0] = 10.0
    fs[:, 255] = 10.0
    c["c_force_s"] = fs
    gs = np.zeros((16, 4), np.float32)
    for h in range(16):
        gs[h, h // 4] = 1.0
    c["c_grpsel"] = gs
    oh = np.zeros((4, 4, 128), np.float32)
    for g in range(4):
        oh[g, g, :] = 1.0
    c["c_onehot4"] = oh
    _CONST_CACHE.update(c)
    return c


CONST_SPECS = {
    "c_ident": ([128, 128], BF16), "c_tri": ([128, 128], BF16), "c_atri": ([128, 128], BF16),
    "c_pswap": ([128, 128], BF16), "c_cosF": ([128, SEQ], F32), "c_sinF": ([128, SEQ], F32),
    "c_costm4": ([SEQ, 4, 8], F32), "c_sintm4": ([SEQ, 4, 8], F32), "c_ropes16": ([2, SPC, 16, 8], F32),
    "c_ind": ([64, SEQ], BF16), "c_force": ([32, 128, 64], F32), "c_cbias": ([32, 2, 128, 128], BF16),
    "c_ovaug": ([2, 128, 65], BF16), "c_esel": ([48, 48, 64], BF16), "c_ovaug_s": ([8, 128, 258], BF16),
    "c_force_s": ([4, 256], F32), "c_grpsel": ([16, 4], F32), "c_onehot4": ([4, 4, 128], F32),
}


class KB:
    def __init__(self, n_phys, ntiles=NTILES, do_sample=True):
        self.n_phys = n_phys
        self.ntiles = ntiles
        self.do_sample = do_sample
        self.nc = nc = bass.Bass("TRN2", target_bir_lowering=False)
        self.S = Sync(nc)
        self.sb_bytes = 0
        self._uid = 0
        di = lambda n, s, d: nc.dram_tensor(n, list(s), d, kind="ExternalInput").ap()
        do = lambda n, s, d: nc.dram_tensor(n, list(s), d, kind="ExternalOutput").ap()
        self.x_p = di("x_prompt", [SEQ, D], F32)
        self.x_s = di("x_sample", [SPC, D], F32)
        self.cache_c = di("cache_cmp_kv", [n_phys * 128, 512], F32)
        self.cache_s = di("cache_sel_kv", [n_phys * 128, 512], F32)
        self.state_w = di("state_win_kv", [SPC, 512, 512], F32)
        self.page_t = di("page_table", [SPC, NPAGE], I32)
        self.norm_g = di("norm_g", [2, D], F32)
        self.final_g = di("final_norm_g", [D], F32)
        self.a_w_in = di("a_w_in", [D, 3 * AW], F32)
        self.a_ln_g = di("a_ln_g", [AW], F32)
        self.a_ln_b = di("a_ln_b", [AW], F32)
        self.a_w_s = di("a_w_s", [16, 128, 128], F32)
        self.a_b_s = di("a_b_s", [16, 128], F32)
        self.a_w_out = di("a_w_out", [AW, D], F32)
        self.b_w_in = di("b_w_in", [D, B_IN], F32)
        self.b_pe = di("b_cmp_pe", [2, 32, 64], F32)
        self.b_w1 = di("b_cmp_w1", [2, 32, 64, 64], F32)
        self.b_b1 = di("b_cmp_b1", [2, 64], F32)
        self.b_w2 = di("b_cmp_w2", [2, 64, 64], F32)
        self.b_b2 = di("b_cmp_b2", [2, 64], F32)
        self.b_gate = di("b_gate_b", [48], F32)
        self.b_w_out = di("b_w_out", [D, D], F32)
        self.cst = {k: di(k, s, d) for k, (s, d) in CONST_SPECS.items()}
        self.y_p = do("y_prompt", [SEQ, D], F32)
        self.y_s = do("y_sample", [SPC, D], F32)
        self.o_cmp_p = do("cmp_kv_prompt", [SEQ, 512], F32)
        self.o_cmp_s = do("cmp_kv_sample", [SPC, 512], F32)
        self.o_sel_p = do("sel_kv_prompt", [SEQ, 512], F32)
        self.o_sel_s = do("sel_kv_sample", [SPC, 512], F32)
        self.o_win_p = do("win_kv_prompt", [512, 512], F32)
        self.o_win_s = do("win_kv_sample", [SPC, 512, 512], F32)
        self.o_chv = do("chunk_v_sample", [SPC, AW], F32)
        dt = lambda n, s: nc.dram_tensor(n, list(s), BF16).ap()
        self.wA_in = dt("wA_in", [12, 128, 8, 512])
        self.wA_out = dt("wA_out", [8, 128, 4, 512])
        self.wB_q = dt("wB_q", [2, 128, 8, 512])
        self.wB_kv = dt("wB_kv", [3, 128, 8, 512])
        self.wB_z = dt("wB_z", [6, 128, 8, 512])
        self.wB_gl = dt("wB_gl", [128, 8, 48])
        self.wB_out = dt("wB_out", [4, 64, 8, 512])
        self.w1bd = dt("w1bd", [2, 128, 32, 128])
        self.b_wscr = {k: Buf("wscr_" + k) for k in ["A_in", "A_out", "B_q", "B_kv", "B_z", "B_gl", "B_out", "w1bd"]}

    def sb(self, name, shape, dtype, excl=False):
        t = self.nc.alloc_sbuf_tensor(name, list(shape), dtype)
        n = 1
        for s in shape[1:]:
            n *= s
        self.sb_bytes += n * (2 if dtype == BF16 else 4)
        return t, Buf(name, excl)

    def nb(self, name):
        self._uid += 1
        return Buf("%s_%d" % (name, self._uid))

    def pe(self, fn, R=(), W=(), signal=True):
        return self.S.op("pe", fn, R, W, signal)

    def act(self, fn, R=(), W=()):
        return self.S.op("act", fn, R, W)

    def dve(self, fn, R=(), W=()):
        return self.S.op("dve", fn, R, W)

    def pool(self, fn, R=(), W=()):
        return self.S.op("pool", fn, R, W)

    def dma(self, fn, R=(), W=(), q="sp", out=False):
        return self.S.dma(q, fn, R, W, is_output=out)

    def ld(self, out_ap, in_ap, R=(), W=(), q="sp", slow=False):
        nc = self.nc
        eng = nc.sync if q == "sp" else nc.gpsimd
        if slow:
            return self.dma(lambda: eng.dma_start(out=out_ap, in_=in_ap, allow_slow_non_contiguous=True), R, W, q)
        return self.dma(lambda: eng.dma_start(out=out_ap, in_=in_ap), R, W, q)

    def st(self, out_ap, in_ap, R=(), W=(), q="sp"):
        nc = self.nc
        eng = nc.sync if q == "sp" else nc.gpsimd
        return self.dma(lambda: eng.dma_start(out=out_ap, in_=in_ap), R, W, q, out=True)

    def alloc(self):
        nc = self.nc
        sb = self.sb
        self.KA, self.b_KA = sb("KA", [128, 4, SEQ], BF16)
        self.VS, self.b_VS = sb("VS", [128, 32, 4, 128], BF16)
        self.KW, self.b_KW = sb("KW", [128, 4, 6 * 128], BF16)
        self.VW, self.b_VW = sb("VW", [128, 6, 4, 128], BF16)
        self.KCT, self.b_KCT = sb("KCT", [128, 4, 256], BF16)
        self.VC, self.b_VC = sb("VC", [128, 2, 4, 128], BF16)
        self.KRAW, self.b_KRAW = sb("KRAW", [128, 4, 16 + T], BF16)
        self.ident, self.b_ident = sb("ident", [128, 128], BF16)
        self.tri, self.b_tri = sb("tri", [128, 128], BF16)
        self.atri, self.b_atri = sb("atri", [128, 128], BF16)
        self.pswap, self.b_pswap = sb("pswap", [128, 128], BF16)
        self.wmT, self.b_wmT = sb("wmT", [128, 16, 128], BF16)
        self.Bmix, self.b_Bmix = sb("Bmix", [128, 16, 128], F32)
        self.lng, self.b_lng = sb("lng", [128, 16], F32)
        self.W2K, self.b_W2K = sb("W2K", [128, 2, 128], BF16)
        self.W2V, self.b_W2V = sb("W2V", [128, 128], BF16)
        self.b2K, self.b_b2K = sb("b2K", [128, 1], F32)
        self.b2Vrow, self.b_b2V = sb("b2Vrow", [1, 128], BF16)
        self.ones_row, self.b_ones = sb("ones_row", [1, 128], BF16)
        self.bias1, self.b_bias1 = sb("bias1", [128, 2], F32)
        self.esel, self.b_esel = sb("esel", [48, 48, 64], BF16)
        self.ovaug, self.b_ovaug = sb("ovaug", [128, 2, 65], BF16)
        self.fgbc, self.b_fgbc = sb("fgbc", [128, D], F32)
        self.gateb, self.b_gateb = sb("gateb", [48, 1], F32)
        self.gcol, self.b_gcol = sb("gcol", [128, 2, 8], F32)
        self.wgl, self.b_wgl = sb("wgl", [128, 8, 48], BF16)
        self.x_tm, self.b_x = sb("x_tm", [128, NSUB, D], F32)
        self.xhat, self.b_xhat = sb("xhat", [128, NSUB, D], BF16)
        self.xnT, self.b_xnT = sb("xnT", [128, 8, T], BF16)
        self.wst = []
        for i in range(2):
            t, b = sb("wst%d" % i, [128, 8, 512], BF16)
            self.wst.append((t, b))
        self.wst_i = 0
        self.st4, self.b_st4 = sb("st4", [128, 16], F32)
        self.AH_N = 19200
        self.AF_N = 7000
        self.arenaH, _ = sb("arenaH", [128, self.AH_N], BF16)
        self.arenaF, _ = sb("arenaF", [128, self.AF_N], F32)
        self.pm = []
        for i in range(3):
            self.pm.append((nc.alloc_psum_tensor("pm%d" % i, [128, 512], F32), Buf("pm%d" % i, True)))
        self.pm_i = 0
        self.ptr = []
        for i in range(2):
            self.ptr.append((nc.alloc_psum_tensor("ptr%d" % i, [128, 1024], BF16), Buf("ptr%d" % i, True)))
        self.ptr_i = 0
        self.po = []
        for i in range(2):
            self.po.append((nc.alloc_psum_tensor("po%d" % i, [128, 512], F32), Buf("po%d" % i, True)))
        self.px = (nc.alloc_psum_tensor("px", [128, 512], F32), Buf("px", True))

    def next_pm(self):
        r = self.pm[self.pm_i]
        self.pm_i = (self.pm_i + 1) % len(self.pm)
        return r

    def next_ptr(self):
        r = self.ptr[self.ptr_i]
        self.ptr_i = (self.ptr_i + 1) % len(self.ptr)
        return r

    def next_wst(self):
        r = self.wst[self.wst_i]
        self.wst_i = (self.wst_i + 1) % len(self.wst)
        return r

    class Carver:
        def __init__(self, kb):
            self.kb = kb
            self.h = 0
            self.f = 0

        def H(self, name, shape):
            n = 1
            for s in shape[1:]:
                n *= s
            assert self.h + n <= self.kb.AH_N, ("arenaH overflow", name, self.h + n)
            ap = self.kb.arenaH[0:shape[0], self.h:self.h + n]
            self.h += n
            return self._shape(ap, shape), self.kb.nb(name)

        def F(self, name, shape):
            n = 1
            for s in shape[1:]:
                n *= s
            assert self.f + n <= self.kb.AF_N, ("arenaF overflow", name, self.f + n)
            ap = self.kb.arenaF[0:shape[0], self.f:self.f + n]
            self.f += n
            return self._shape(ap, shape), self.kb.nb(name)

        @staticmethod
        def _shape(ap, shape):
            if len(shape) == 2:
                return ap
            if len(shape) == 3:
                return ap.rearrange("p (a b) -> p a b", b=shape[2])
            if len(shape) == 4:
                return ap.rearrange("p (a b c) -> p a b c", b=shape[2], c=shape[3])
            if len(shape) == 5:
                return ap.rearrange("p (a b c d) -> p a b c d", b=shape[2], c=shape[3], d=shape[4])
            raise ValueError(shape)

    def prologue(self):
        nc = self.nc
        cv = KB.Carver(self)
        st32 = [cv.F("st32_0", [128, 2048]), (self.x_tm[:].rearrange("p a b -> p (a b)"), self.b_x)]
        st16 = [cv.H("st16_%d" % i, [128, 2048]) for i in range(2)]
        c = self.cst
        self.ld(self.ident[:], c["c_ident"][:, :], W=[self.b_ident])
        self.ld(self.tri[:], c["c_tri"][:, :], W=[self.b_tri])
        self.ld(self.atri[:], c["c_atri"][:, :], W=[self.b_atri])
        self.ld(self.pswap[:], c["c_pswap"][:, :], W=[self.b_pswap])
        self.ld(self.esel[:], c["c_esel"][:, :, :], W=[self.b_esel])
        self.ld(self.ovaug[:], c["c_ovaug"].rearrange("n p c -> p n c"), W=[self.b_ovaug])
        self.ld(self.fgbc[:], self.final_g.partition_broadcast(128), W=[self.b_fgbc])
        self.ld(self.gateb[:], self.b_gate.rearrange("(p o) -> p o", o=1), W=[self.b_gateb])
        self.ld(self.gcol[:], self.norm_g.rearrange("l (kc p) -> p l kc", p=128), W=[self.b_gcol], slow=True)
        self.ld(self.lng[:], self.a_ln_g.rearrange("(g p) -> p g", p=128), W=[self.b_lng], slow=True)
        for g in range(4):
            self.ld(self.KA[64:128, g, :], c["c_ind"][:, :], W=[self.b_KA])
        self.pool(lambda: nc.gpsimd.memset(self.VS[:], 1.0), W=[self.b_VS])
        self.pool(lambda: nc.gpsimd.memset(self.VW[:], 1.0), W=[self.b_VW])
        self.pool(lambda: nc.gpsimd.memset(self.VC[:], 1.0), W=[self.b_VC])
        self.pool(lambda: nc.gpsimd.memset(self.VC[:, :, :, 0:64], 0.0), W=[self.b_VC])
        self.pool(lambda: nc.gpsimd.memset(self.KCT[:], 0.0), W=[self.b_KCT])
        self.pool(lambda: nc.gpsimd.memset(self.KRAW[:], 0.0), W=[self.b_KRAW])
        self.pool(lambda: nc.gpsimd.memset(self.KW[:], 0.0), W=[self.b_KW])
        self.pool(lambda: nc.gpsimd.memset(self.KA[0:64, :, :], 0.0), W=[self.b_KA])
        self.pool(lambda: nc.gpsimd.memset(self.ones_row[:], 1.0), W=[self.b_ones])

        self._cv_i = 0

        def conv(src, dst, p, a, b, scale=None, wbuf=None, pre=None):
            i = self._cv_i
            self._cv_i += 1
            (s32, b32), (s16, b16) = st32[i % 2], st16[i % 2]
            v32 = s32[0:p, 0:a * b].rearrange("p (a b) -> p a b", b=b)
            v16 = s16[0:p, 0:a * b].rearrange("p (a b) -> p a b", b=b)
            if pre is not None:
                pre(s32, b32)
            else:
                self.ld(v32, src, W=[b32])
            f32 = s32[0:p, 0:a * b]
            f16 = s16[0:p, 0:a * b]
            if scale is not None:
                self.dve(lambda: nc.vector.tensor_scalar(out=f16, in0=f32, scalar1=scale, scalar2=None, op0=ALU.mult),
                         R=[b32, self.b_gcol], W=[b16])
            elif i % 2 == 0:
                self.dve(lambda: nc.vector.tensor_copy(out=f16, in_=f32), R=[b32], W=[b16])
            else:
                self.act(lambda: nc.scalar.copy(out=f16, in_=f32), R=[b32], W=[b16])
            self.ld(dst, v16, R=[b16], q="pool")

        for kc in range(8):
            rows = slice(kc * 128, (kc + 1) * 128)
            sc0 = self.gcol[:, 0, kc:kc + 1]
            sc1 = self.gcol[:, 1, kc:kc + 1]
            for (c0, s0) in ((AW, 0), (0, 4), (2 * AW, 8)):
                conv(self.a_w_in[rows, c0:c0 + 2048].rearrange("p (a b) -> p a b", b=512),
                     self.wA_in[s0:s0 + 4, :, kc, :].rearrange("s p c -> p s c"), 128, 4, 512, sc0, self.b_wscr["A_in"])
            conv(self.b_w_in[rows, 0:1024].rearrange("p (a b) -> p a b", b=512),
                 self.wB_q[0:2, :, kc, :].rearrange("s p c -> p s c"), 128, 2, 512, sc1, self.b_wscr["B_q"])
            conv(self.b_w_in[rows, 1024:2560].rearrange("p (a b) -> p a b", b=512),
                 self.wB_kv[0:3, :, kc, :].rearrange("s p c -> p s c"), 128, 3, 512, sc1, self.b_wscr["B_kv"])
            for hz in range(2):
                conv(self.b_w_in[rows, 2560 + hz * 1536:2560 + (hz + 1) * 1536].rearrange("p (a b) -> p a b", b=512),
                     self.wB_z[hz * 3:hz * 3 + 3, :, kc, :].rearrange("s p c -> p s c"), 128, 3, 512, sc1, self.b_wscr["B_z"])
            conv(self.b_w_in[rows, 5632:5680].rearrange("p (a b) -> p a b", b=48),
                 self.wB_gl[:, kc:kc + 1, :], 128, 1, 48, sc1, self.b_wscr["B_gl"])
        for fc in range(16):
            conv(self.a_w_out[fc * 128:(fc + 1) * 128, :].rearrange("p (a b) -> p a b", b=512),
                 self.wA_out[2 * (fc // 4):2 * (fc // 4) + 2, :, fc % 4, :].rearrange("s p c -> p s c"), 128, 2, 512,
                 None, self.b_wscr["A_out"])
        for cb in range(2):
            for hh in range(2):
                for q in range(2):
                    h0 = hh * 8 + q * 4
                    conv(self.b_w_out[h0 * 64:(h0 + 4) * 64, cb * 512:(cb + 1) * 512].rearrange("(h d) c -> d h c", d=64),
                         self.wB_out[cb * 2 + hh, :, q * 4:q * 4 + 4, :], 64, 4, 512, None, self.b_wscr["B_out"])
        for kv in range(2):
            for jh in range(2):
                def pre(s32, b32, kv=kv, jh=jh):
                    v = s32[:, :].rearrange("p (a b) -> p a b", b=128)
                    self.dve(lambda: nc.vector.memset(s32[:, :], 0.0), W=[b32])
                    src = self.b_w1[kv, jh * 16:(jh + 1) * 16].rearrange("j c h -> c j h")
                    self.ld(v[0:64, :, 0:64], src, W=[b32])
                    self.ld(v[64:128, :, 64:128], src, W=[b32])
                conv(None, self.w1bd[kv, :, jh * 16:(jh + 1) * 16, :], 128, 16, 128, None, self.b_wscr["w1bd"], pre=pre)

        wtmp, b_wtmp = st32[0]
        lnb2, b_lnb2 = cv.F("lnb2", [2, 16, 128])
        rs2, b_rs2 = cv.F("rs2", [2, 16, 128])
        wbf, b_wbf = cv.H("wbf", [128, 128])
        onesc, b_onesc = cv.H("onesc", [128, 1])
        self.pool(lambda: nc.gpsimd.memset(onesc, 1.0), W=[b_onesc])
        self.pool(lambda: nc.gpsimd.memset(lnb2[:, :, :], 1.0), W=[b_lnb2])
        self.ld(lnb2[0:1, :, :], self.a_ln_b.rearrange("(o g c) -> o g c", o=1, c=128), W=[b_lnb2])
        self.ld(rs2[1:2, :, :], self.a_b_s.rearrange("(o g) t -> o g t", o=1), W=[b_rs2])
        for g in range(16):
            wv = wtmp[:, 0:128]
            self.ld(wv, self.a_w_s[g], W=[b_wtmp])
            self.pool(lambda: nc.gpsimd.affine_select(out=wv, in_=wv, pattern=[[-1, 128]], compare_op=ALU.is_ge,
                                                      fill=0.0, base=0, channel_multiplier=1), R=[b_wtmp], W=[b_wtmp])
            self.dve(lambda: nc.vector.tensor_copy(out=wbf, in_=wv), R=[b_wtmp], W=[b_wbf])
            pt, b_pt = self.next_ptr()
            self.pe(lambda: nc.tensor.transpose(out=pt[:, 0:128], in_=wbf, identity=self.ident[:]),
                    R=[b_wbf, self.b_ident], W=[b_pt])
            self.act(lambda: nc.scalar.copy(out=self.wmT[:, g, :], in_=pt[:, 0:128]), R=[b_pt], W=[self.b_wmT])
            pm, b_pm = self.next_pm()
            self.pe(lambda: nc.tensor.matmul(pm[0:1, 0:128], lhsT=onesc, rhs=self.wmT[:, g, :], start=True, stop=True),
                    R=[b_onesc, self.b_wmT], W=[b_pm])
            self.act(lambda: nc.scalar.copy(out=rs2[0:1, g, :], in_=pm[0:1, 0:128]), R=[b_pm], W=[b_rs2])
        for g in range(16):
            pm, b_pm = self.next_pm()
            self.pe(lambda: nc.tensor.matmul(pm[:, 0:128], lhsT=lnb2[:, g, :], rhs=rs2[:, g, :], start=True, stop=True),
                    R=[b_lnb2, b_rs2], W=[b_pm])
            self.act(lambda: nc.scalar.copy(out=self.Bmix[:, g, :], in_=pm[:, 0:128]), R=[b_pm], W=[self.b_Bmix])

        w2f, b_w2f = cv.F("w2f", [128, 2, 128])
        self.dve(lambda: nc.vector.memset(w2f[:, :, :], 0.0), W=[b_w2f])
        for a in range(2):
            for hcol in range(2):
                self.ld(w2f[a * 64:(a + 1) * 64, a, hcol * 64:(hcol + 1) * 64], self.b_w2[0], W=[b_w2f])
        self.dve(lambda: nc.vector.tensor_copy(out=self.W2K[:], in_=w2f[:, :, :]), R=[b_w2f], W=[self.b_W2K])
        w2v, b_w2v = cv.F("w2v", [128, 128])
        self.dve(lambda: nc.vector.memset(w2v, 0.0), W=[b_w2v])
        for a in range(2):
            self.ld(w2v[a * 64:(a + 1) * 64, a * 64:(a + 1) * 64], self.b_w2[1], W=[b_w2v])
        self.dve(lambda: nc.vector.tensor_copy(out=self.W2V[:], in_=w2v), R=[b_w2v], W=[self.b_W2V])
        for a in range(2):
            self.ld(self.b2K[a * 64:(a + 1) * 64, :], self.b_b2[0].rearrange("(p o) -> p o", o=1), W=[self.b_b2K])
        b2vf, b_b2vf = cv.F("b2vf", [1, 128])
        for a in range(2):
            self.ld(b2vf[0:1, a * 64:(a + 1) * 64], self.b_b2[1].rearrange("(o c) -> o c", o=1), W=[b_b2vf])
        self.dve(lambda: nc.vector.tensor_copy(out=self.b2Vrow[:], in_=b2vf), R=[b_b2vf], W=[self.b_b2V])
        pef, b_pef = cv.F("pef", [128, 2, 32])
        peb, b_peb = cv.H("peb", [128, 2, 32])
        b1f, b_b1f = cv.F("b1f", [128, 2])
        for a in range(2):
            self.ld(pef[a * 64:(a + 1) * 64, :, :], self.b_pe.rearrange("k j c -> c k j"), W=[b_pef], slow=True)
            self.ld(b1f[a * 64:(a + 1) * 64, :], self.b_b1.rearrange("k h -> h k"), W=[b_b1f], slow=True)
        self.dve(lambda: nc.vector.tensor_copy(out=peb[:, :, :], in_=pef[:, :, :]), R=[b_pef], W=[b_peb])
        self.S.barrier()
        self.ld(self.wgl[:], self.wB_gl[:, :, :], W=[self.b_wgl])
        for kv in range(2):
            wt, b_wt = self.next_wst()
            w1v = wt[:].rearrange("p a b -> p (a b)").rearrange("p (j c) -> p j c", c=128)
            self.ld(w1v, self.w1bd[kv], R=[self.b_wscr["w1bd"]], W=[b_wt])
            pm, b_pm = self.next_pm()
            for j in range(32):
                self.pe(lambda: nc.tensor.matmul(pm[:, 0:1], lhsT=w1v[:, j, :], rhs=peb[:, kv, j:j + 1],
                                                 start=(j == 0), stop=(j == 31)), R=[b_wt, b_peb], W=[b_pm], signal=(j == 31))
            self.dve(lambda: nc.vector.tensor_tensor(out=self.bias1[:, kv:kv + 1], in0=pm[:, 0:1], in1=b1f[:, kv:kv + 1], op=ALU.add),
                     R=[b_pm, b_b1f], W=[self.b_bias1])
        self.S.barrier()

    def rms_T(self, npart=128, ncol=T):
        nc = self.nc
        nsub = (ncol + 127) // 128
        st4, b_st4 = self.st4, self.b_st4
        for s in range(nsub):
            self.act(lambda: nc.scalar.activation(out=self.xhat[0:npart, s, :], in_=self.x_tm[0:npart, s, :], func=AF.Square,
                                                  accum_out=st4[0:npart, s:s + 1]), R=[self.b_x], W=[self.b_xhat, b_st4])
        self.dve(lambda: nc.vector.tensor_scalar(out=st4[0:npart, 2:2 + nsub], in0=st4[0:npart, 0:nsub], scalar1=1.0 / D, scalar2=EPS,
                                                 op0=ALU.mult, op1=ALU.add), R=[b_st4], W=[b_st4])
        self.act(lambda: nc.scalar.activation(out=st4[0:npart, 2:2 + nsub], in_=st4[0:npart, 2:2 + nsub], func=AF.Sqrt), R=[b_st4], W=[b_st4])
        self.dve(lambda: nc.vector.reciprocal(out=st4[0:npart, 4:4 + nsub], in_=st4[0:npart, 2:2 + nsub]), R=[b_st4], W=[b_st4])

    def xhat_T(self, npart=128, ncol=T):
        nc = self.nc
        nsub = (ncol + 127) // 128
        st4, b_st4 = self.st4, self.b_st4
        for s in range(nsub):
            self.act(lambda: nc.scalar.mul(out=self.xhat[0:npart, s, :], in_=self.x_tm[0:npart, s, :], mul=st4[0:npart, 4 + s:5 + s]),
                     R=[self.b_x, b_st4], W=[self.b_xhat])
        w = min(npart, 128)
        for kc in range(8):
            pt, b_pt = self.next_ptr()
            for s in range(nsub):
                self.pe(lambda: nc.tensor.transpose(out=pt[:, s * 128:s * 128 + w], in_=self.xhat[0:npart, s, kc * 128:(kc + 1) * 128],
                                                    identity=self.ident[0:npart, 0:npart]), R=[self.b_xhat, self.b_ident], W=[b_pt], signal=(s == nsub - 1))
            if kc % 2 == 0:
                self.act(lambda: nc.scalar.copy(out=self.xnT[:, kc, 0:ncol], in_=pt[:, 0:ncol]), R=[b_pt], W=[self.b_xnT])
            else:
                self.dve(lambda: nc.vector.tensor_copy(out=self.xnT[:, kc, 0:ncol], in_=pt[:, 0:ncol]), R=[b_pt], W=[self.b_xnT])

    def layer_a(self, i):
        nc = self.nc
        cv = KB.Carver(self)
        gv, b_gv = cv.H("gv", [128, NSUB, AW])
        gu = [cv.H("gu%d" % k, [128, 4, T]) for k in range(2)]
        sz = [cv.H("sz%d" % k, [128, 4, T]) for k in range(2)]
        gs = [cv.H("gs%d" % k, [128, 4, T]) for k in range(2)]
        yT, b_yT = cv.H("yT", [128, 16, T])
        tmpf = [cv.F("tmpf%d" % k, [128, 128]) for k in range(2)]
        stats, b_stats = cv.F("stats", [128, NSUB, 4, 6])
        mv, b_mv = cv.F("mv", [128, NSUB, 2])
        rv, b_rv = cv.F("rv", [128, 4])
        for s in range(NSUB):
            self.ld(self.x_tm[:, s, :], self.x_p[i * T + s * 128:i * T + (s + 1) * 128, :], W=[self.b_x])
        self.rms_T()
        self.xhat_T()
        for cb in range(4):
            wt, b_wt = self.next_wst()
            self.ld(wt[:], self.wA_in[cb], W=[b_wt])
            for s in range(NSUB):
                pm, b_pm = self.next_pm()
                for kc in range(8):
                    self.pe(lambda: nc.tensor.matmul(pm[:, :], lhsT=self.xnT[:, kc, s * 128:(s + 1) * 128], rhs=wt[:, kc, :],
                                                     start=(kc == 0), stop=(kc == 7)), R=[self.b_xnT, b_wt], W=[b_pm], signal=(kc == 7))
                self.act(lambda: nc.scalar.activation(out=gv[:, s, cb * 512:(cb + 1) * 512], in_=pm[:, :], func=AF.Gelu), R=[b_pm], W=[b_gv])
                self.dve(lambda: nc.vector.bn_stats(out=stats[:, s, cb, :], in_=gv[:, s, cb * 512:(cb + 1) * 512]), R=[b_gv], W=[b_stats])
        for s in range(NSUB):
            self.dve(lambda: nc.vector.bn_aggr(out=mv[:, s, :], in_=stats[:, s, :, :]), R=[b_stats], W=[b_mv])
        self.dve(lambda: nc.vector.tensor_scalar(out=rv[:, 0:NSUB], in0=mv[:, :, 1], scalar1=EPS, scalar2=None, op0=ALU.add), R=[b_mv], W=[b_rv])
        self.act(lambda: nc.scalar.activation(out=rv[:, 0:NSUB], in_=rv[:, 0:NSUB], func=AF.Sqrt), R=[b_rv], W=[b_rv])
        self.dve(lambda: nc.vector.reciprocal(out=rv[:, 2:2 + NSUB], in_=rv[:, 0:NSUB]), R=[b_rv], W=[b_rv])
        for s in range(NSUB):
            self.dve(lambda: nc.vector.tensor_scalar(out=gv[:, s, :], in0=gv[:, s, :], scalar1=mv[:, s, 0:1], scalar2=rv[:, 2 + s:3 + s],
                                                     op0=ALU.subtract, op1=ALU.mult), R=[b_gv, b_mv, b_rv], W=[b_gv])
        for q in range(4):
            (guq, b_gu), (szq, b_sz), (gsq, b_gs) = gu[q % 2], sz[q % 2], gs[q % 2]
            for (slab, dst, b_dst, fn) in ((4 + q, guq, b_gu, AF.Gelu), (8 + q, szq, b_sz, AF.Silu)):
                wt, b_wt = self.next_wst()
                self.ld(wt[:], self.wA_in[slab], W=[b_wt])
                for m in range(4):
                    pm, b_pm = self.next_pm()
                    for kc in range(8):
                        self.pe(lambda: nc.tensor.matmul(pm[:, 0:T], lhsT=wt[:, kc, m * 128:(m + 1) * 128], rhs=self.xnT[:, kc, :],
                                                         start=(kc == 0), stop=(kc == 7)), R=[self.b_xnT, b_wt], W=[b_pm], signal=(kc == 7))
                    self.act(lambda: nc.scalar.activation(out=dst[:, m, :], in_=pm[:, 0:T], func=fn), R=[b_pm], W=[b_dst])
            self.pool(lambda: nc.gpsimd.tensor_tensor(out=gsq[:, :, :], in0=guq[:, :, :], in1=szq[:, :, :], op=ALU.mult), R=[b_gu, b_sz], W=[b_gs])
            for s in range(NSUB):
                pm, b_pm = self.next_pm()
                for m in range(4):
                    g = 4 * q + m
                    self.pe(lambda: nc.tensor.matmul(pm[:, m * 128:(m + 1) * 128], lhsT=gv[:, s, g * 128:(g + 1) * 128], rhs=self.wmT[:, g, :],
                                                     start=(m == 0), stop=True, skip_group_check=True), R=[b_gv, self.b_wmT], W=[b_pm], signal=(m == 3))
                for m in range(4):
                    g = 4 * q + m
                    tf, b_tf = tmpf[m % 2]
                    self.dve(lambda: nc.vector.scalar_tensor_tensor(out=tf, in0=pm[:, m * 128:(m + 1) * 128], scalar=self.lng[:, g:g + 1],
                                                                    in1=self.Bmix[:, g, :], op0=ALU.mult, op1=ALU.add),
                             R=[b_pm, self.b_lng, self.b_Bmix], W=[b_tf])
                    self.dve(lambda: nc.vector.tensor_tensor(out=yT[:, g, s * 128:(s + 1) * 128], in0=tf, in1=gsq[:, m, s * 128:(s + 1) * 128],
                                                             op=ALU.mult), R=[b_tf, b_gs], W=[b_yT])
        for cb in range(2):
            pms = [self.next_pm() for _ in range(NSUB)]
            for fcg in range(4):
                wt, b_wt = self.next_wst()
                self.ld(wt[:, 0:4, :], self.wA_out[fcg * 2 + cb], W=[b_wt])
                for s in range(NSUB):
                    for fl in range(4):
                        fc = fcg * 4 + fl
                        self.pe(lambda: nc.tensor.matmul(pms[s][0][:, :], lhsT=yT[:, fc, s * 128:(s + 1) * 128], rhs=wt[:, fl, :],
                                                         start=(fc == 0), stop=(fc == 15)), R=[b_yT, b_wt], W=[pms[s][1]], signal=(fl == 3))
            for s in range(NSUB):
                self.dve(lambda: nc.vector.tensor_tensor(out=self.x_tm[:, s, cb * 512:(cb + 1) * 512], in0=self.x_tm[:, s, cb * 512:(cb + 1) * 512],
                                                         in1=pms[s][0][:, :], op=ALU.add), R=[pms[s][1], self.b_x], W=[self.b_x])

    def mm8(self, out_ap, b_out, lhs_fn, rhs_fn, R):
        nc = self.nc
        for kc in range(8):
            self.pe(lambda: nc.tensor.matmul(out_ap, lhsT=lhs_fn(kc), rhs=rhs_fn(kc), start=(kc == 0), stop=(kc == 7)),
                    R=R, W=[b_out], signal=(kc == 7))

    def layer_b(self, i):
        nc = self.nc
        cv = KB.Carver(self)
        kvtm, b_kvtm = cv.F("kvtm", [128, NSUB, 1536])
        cosF, b_cosF = cv.F("cosF", [128, T])
        sinF, b_sinF = cv.F("sinF", [128, T])
        ctm, b_ctm = cv.F("ctm", [128, NSUB, 4, 8])
        stm, b_stm = cv.F("stm", [128, NSUB, 4, 8])
        rtmp, b_rtmp = cv.F("rtmp", [128, 4, 4, 8])
        force, b_force = cv.F("force", [128, NSUB, 64])
        tM, b_tM = cv.F("tM", [128, 512])
        tA, b_tA = cv.F("tA", [128, 512])
        tB, b_tB = cv.F("tB", [128, 512])
        yacc, b_yacc = cv.F("yacc", [128, 512])
        impv, b_imp = cv.F("imp", [128, 64])
        score, b_score = cv.F("score", [128, 64])
        scr, b_scr = cv.F("scr", [128, 64])
        m8, b_m8 = cv.F("m8", [128, 16])
        rd, b_rd = cv.F("rd", [128, 8])
        tO_off = cv.f
        qtA, b_qtA = cv.F("qtA", [128, T])
        qtB, b_qtB = cv.F("qtB", [128, T])
        tO, b_tO = self.arenaF[:, tO_off:tO_off + 512], self.nb("tO")
        ktmp, b_ktmp = cv.H("ktmp", [128, NSUB, 1024])
        qrawT, b_qraw = cv.H("qrawT", [128, 8, T])
        qaug, b_qaug = cv.H("qaug", [128, 4, NSUB, 512])
        b_qbias = self.nb("qbias")
        sz64, b_sz64 = cv.H("sz64", [128, 3, NSUB, 512])
        gT, b_gT = cv.H("gT", [128, T])
        PT = [cv.H("PT%d" % k, [128, 512]) for k in range(2)]
        yB64, b_yB = cv.H("yB64", [128, 4, NSUB, 512])
        Btm, b_Btm = cv.H("Btm", [128, 128])
        cbias, b_cbias = cv.H("cbias", [128, NSUB, 2, 128])
        hidT, b_hidT = cv.H("hidT", [128, 4, 16])
        pt3_off = cv.h
        vstage, b_vst = cv.H("vstage", [128, 2, 128])
        qrot, b_qrot = cv.H("qrot", [128, T])
        PT3 = PT + [(self.arenaH[:, pt3_off:pt3_off + 512], self.nb("PT2"))]
        c = self.cst
        t0 = i * T

        self.rms_T()
        self.xhat_T()
        self.ld(cosF, c["c_cosF"][:, t0:t0 + T], W=[b_cosF], q="pool")
        self.ld(sinF, c["c_sinF"][:, t0:t0 + T], W=[b_sinF], q="pool")
        self.ld(ctm, c["c_costm4"][t0:t0 + T].rearrange("(s p) g c -> p s g c", p=128), W=[b_ctm], q="pool")
        self.ld(stm, c["c_sintm4"][t0:t0 + T].rearrange("(s p) g c -> p s g c", p=128), W=[b_stm], q="pool")
        self.ld(force, c["c_force"][2 * i:2 * i + 2].rearrange("q p j -> p q j"), W=[b_force], q="pool")
        self.ld(cbias, c["c_cbias"][2 * i:2 * i + 2].rearrange("q n p c -> p q n c"), W=[b_cbias], q="pool")
        self.dve(lambda: nc.vector.memset(Btm, 0.0), W=[b_Btm])

        for br in range(3):
            wt, b_wt = self.next_wst()
            self.ld(wt[:], self.wB_kv[br], W=[b_wt])
            for s in range(NSUB):
                pm, b_pm = self.next_pm()
                self.mm8(pm[:, :], b_pm, lambda kc: self.xnT[:, kc, s * 128:(s + 1) * 128], lambda kc: wt[:, kc, :], [self.b_xnT, b_wt])
                self.act(lambda: nc.scalar.copy(out=kvtm[:, s, br * 512:(br + 1) * 512], in_=pm[:, :]), R=[b_pm], W=[b_kvtm])
        if getattr(self, 'stop_after', None) == 'kv':
            return
        for s in range(NSUB):
            for br in (1, 2):
                kview = kvtm[:, s, br * 512:br * 512 + 256].rearrange("p (g d) -> p g d", d=64)
                x1 = kview[:, :, 0:8]
                x2 = kview[:, :, 8:16]
                cs = ctm[:, s, :, :]
                sn = stm[:, s, :, :]
                R_ = [b_kvtm, b_ctm, b_stm]
                self.dve(lambda: nc.vector.tensor_tensor(out=rtmp[:, 0, :, :], in0=x1, in1=cs, op=ALU.mult), R=R_, W=[b_rtmp])
                self.dve(lambda: nc.vector.tensor_tensor(out=rtmp[:, 1, :, :], in0=x2, in1=sn, op=ALU.mult), R=R_, W=[b_rtmp])
                self.dve(lambda: nc.vector.tensor_tensor(out=rtmp[:, 2, :, :], in0=x2, in1=cs, op=ALU.mult), R=R_, W=[b_rtmp])
                self.dve(lambda: nc.vector.tensor_tensor(out=rtmp[:, 3, :, :], in0=x1, in1=sn, op=ALU.mult), R=R_, W=[b_rtmp])
                self.dve(lambda: nc.vector.tensor_tensor(out=x1, in0=rtmp[:, 0, :, :], in1=rtmp[:, 1, :, :], op=ALU.subtract), R=[b_rtmp], W=[b_kvtm])
                self.dve(lambda: nc.vector.tensor_tensor(out=x2, in0=rtmp[:, 2, :, :], in1=rtmp[:, 3, :, :], op=ALU.add), R=[b_rtmp], W=[b_kvtm])
        if getattr(self, 'stop_after', None) == 'rope':
            return
        for s in range(NSUB):
            r0 = t0 + s * 128
            self.st(self.o_cmp_p[r0:r0 + 128, :], kvtm[:, s, 0:512], R=[b_kvtm], q="pool")
            self.st(self.o_sel_p[r0:r0 + 128, :], kvtm[:, s, 512:1024], R=[b_kvtm], q="pool")
            if r0 >= SEQ - 512:
                w0 = r0 - (SEQ - 512)
                self.st(self.o_win_p[w0:w0 + 128, :], kvtm[:, s, 1024:1536], R=[b_kvtm], q="pool")
        if getattr(self, 'stop_after', None) == 'out':
            return
        for s in range(NSUB):
            kt = 2 * i + s
            slot = kt % 6
            self.pool(lambda: nc.gpsimd.tensor_copy(out=ktmp[:, s, 0:768], in_=kvtm[:, s, 0:768]), R=[b_kvtm], W=[b_ktmp])
            self.pool(lambda: nc.gpsimd.tensor_copy(out=ktmp[:, s, 768:1024], in_=kvtm[:, s, 1024:1280]), R=[b_kvtm], W=[b_ktmp])
            self.pool(lambda: nc.gpsimd.tensor_copy(out=self.VS[:, kt, :, 0:64], in_=kvtm[:, s, 768:1024].rearrange("p (g d) -> p g d", d=64)),
                      R=[b_kvtm], W=[self.b_VS])
            self.pool(lambda: nc.gpsimd.tensor_copy(out=self.VW[:, slot, :, 0:64], in_=kvtm[:, s, 1280:1536].rearrange("p (g d) -> p g d", d=64)),
                      R=[b_kvtm], W=[self.b_VW])
            pt, b_pt = self.next_ptr()
            for tl in range(4):
                self.pe(lambda: nc.tensor.transpose(out=pt[:, tl * 128:(tl + 1) * 128], in_=ktmp[:, s, tl * 128:(tl + 1) * 128], identity=self.ident[:]),
                        R=[b_ktmp, self.b_ident], W=[b_pt], signal=(tl == 3))
            self.act(lambda: nc.scalar.copy(out=self.KRAW[:, :, 16 + s * 128:16 + (s + 1) * 128], in_=pt[:, 0:512].rearrange("p (a b) -> p a b", b=128)),
                     R=[b_pt], W=[self.b_KRAW])
            for (c0, dst, b_dst, off) in ((512, self.KA, self.b_KA, kt * 128), (768, self.KW, self.b_KW, slot * 128)):
                pt, b_pt = self.next_ptr()
                for g in range(4):
                    self.pe(lambda: nc.tensor.transpose(out=pt[0:64, g * 128:(g + 1) * 128], in_=ktmp[:, s, c0 + g * 64:c0 + (g + 1) * 64], identity=self.ident[:]),
                            R=[b_ktmp, self.b_ident], W=[b_pt], signal=(g == 3))
                self.dve(lambda: nc.vector.tensor_copy(out=dst[0:64, :, off:off + 128], in_=pt[0:64, 0:512].rearrange("p (a b) -> p a b", b=128)),
                         R=[b_pt], W=[b_dst])
        if getattr(self, 'stop_after', None) == 'tr':
            return
        for sl in range(2):
            wt, b_wt = self.next_wst()
            self.ld(wt[:], self.wB_q[sl], W=[b_wt])
            for m4 in range(4):
                m = sl * 4 + m4
                pm, b_pm = self.next_pm()
                self.mm8(pm[:, 0:T], b_pm, lambda kc: wt[:, kc, m4 * 128:(m4 + 1) * 128], lambda kc: self.xnT[:, kc, :], [self.b_xnT, b_wt])
                self.act(lambda: nc.scalar.copy(out=qrawT[:, m, :], in_=pm[:, 0:T]), R=[b_pm], W=[b_qraw])
                px, b_px = self.px
                self.pe(lambda: nc.tensor.matmul(px[:, 0:T], lhsT=self.pswap[:], rhs=qrawT[:, m, :], start=True, stop=True), R=[self.b_pswap, b_qraw], W=[b_px])
                self.pool(lambda: nc.gpsimd.tensor_tensor(out=qtA, in0=qrawT[:, m, :], in1=cosF, op=ALU.mult), R=[b_qraw, b_cosF], W=[b_qtA])
                self.dve(lambda: nc.vector.tensor_tensor(out=qtB, in0=px[:, 0:T], in1=sinF, op=ALU.mult), R=[b_px, b_sinF], W=[b_qtB])
                self.dve(lambda: nc.vector.tensor_tensor(out=qrot, in0=qtA, in1=qtB, op=ALU.add), R=[b_qtA, b_qtB], W=[b_qrot])
                g = m // 2
                for a in range(2):
                    ro = a * 2 + (m % 2)
                    self.act(lambda: nc.scalar.copy(out=qaug[0:64, g, :, ro * 128:(ro + 1) * 128], in_=qrot[a * 64:(a + 1) * 64, :].rearrange("p (s q) -> p s q", q=128)),
                             R=[b_qrot], W=[b_qaug])
        if getattr(self, 'stop_after', None) == 'q':
            return
        pm, b_pm = self.next_pm()
        self.mm8(pm[0:48, 0:T], b_pm, lambda kc: self.wgl[:, kc, :], lambda kc: self.xnT[:, kc, :], [self.b_xnT, self.b_wgl])
        self.act(lambda: nc.scalar.activation(out=gT[0:48, :], in_=pm[0:48, 0:T], func=AF.Sigmoid, bias=self.gateb[:, 0:1]), R=[b_pm, self.b_gateb], W=[b_gT])

        if getattr(self, 'stop_after', None) == 'gl':
            return
        nblk = 15 if i == 0 else 16
        base = 16 if i == 0 else 0
        m0 = 0 if i == 0 else 16 * i - 1
        pmh, b_pmh = self.next_pm()
        for kv in range(2):
            wt, b_wt = self.next_wst()
            w1v = wt[:].rearrange("p a b -> p (a b)").rearrange("p (j c) -> p j c", c=128)
            self.ld(w1v, self.w1bd[kv], W=[b_wt])
            for hp in range(2):
                tl = kv * 2 + hp
                for j in range(32):
                    self.pe(lambda: nc.tensor.matmul(pmh[:, tl * 16:tl * 16 + nblk], lhsT=w1v[:, j, :],
                                                     rhs=self.KRAW[:, tl, base + j:base + j + 16 * (nblk - 1) + 1:16],
                                                     start=(tl == 0 and j == 0), stop=(j == 31), skip_group_check=True),
                            R=[b_wt, self.b_KRAW], W=[b_pmh], signal=(j == 31))
        for kv in range(2):
            self.act(lambda: nc.scalar.activation(out=hidT[:, 2 * kv:2 * kv + 2, 0:nblk],
                                                  in_=pmh[:, 32 * kv:32 * kv + 32].rearrange("p (a b) -> p a b", b=16)[:, :, 0:nblk],
                                                  func=AF.Silu, bias=self.bias1[:, kv:kv + 1]), R=[b_pmh, self.b_bias1], W=[b_hidT])
        pmk, b_pmk = self.next_pm()
        for g in range(4):
            self.pe(lambda: nc.tensor.matmul(pmk[:, g * 16:g * 16 + nblk], lhsT=self.W2K[:, g % 2, :], rhs=hidT[:, g // 2, 0:nblk],
                                             start=(g == 0), stop=True, skip_group_check=True), R=[self.b_W2K, b_hidT], W=[b_pmk], signal=(g == 3))
        self.act(lambda: nc.scalar.activation(out=self.KCT[:, :, m0:m0 + nblk], in_=pmk[:, 0:64].rearrange("p (a b) -> p a b", b=16)[:, :, 0:nblk],
                                              func=AF.Identity, bias=self.b2K[:, 0:1]), R=[b_pmk, self.b_b2K], W=[self.b_KCT])
        pmv, b_pmv = self.next_pm()
        for hp in range(2):
            self.pe(lambda: nc.tensor.matmul(pmv[0:nblk, hp * 128:(hp + 1) * 128], lhsT=hidT[:, 2 + hp, 0:nblk], rhs=self.W2V[:, :],
                                             start=(hp == 0), stop=False, skip_group_check=True), R=[b_hidT, self.b_W2V], W=[b_pmv], signal=False)
            self.pe(lambda: nc.tensor.matmul(pmv[0:nblk, hp * 128:(hp + 1) * 128], lhsT=self.ones_row[0:1, 0:nblk], rhs=self.b2Vrow[0:1, :],
                                             start=False, stop=True, skip_group_check=True), R=[self.b_ones, self.b_b2V], W=[b_pmv], signal=(hp == 1))
        self.act(lambda: nc.scalar.copy(out=vstage[0:nblk, :, :], in_=pmv[0:nblk, 0:256].rearrange("p (a b) -> p a b", b=128)), R=[b_pmv], W=[b_vst])
        blk = m0
        while blk < m0 + nblk:
            nt = blk // 128
            p0 = blk % 128
            cnt = min(m0 + nblk - blk, 128 - p0)
            o = blk - m0
            self.ld(self.VC[p0:p0 + cnt, nt, :, 0:64], vstage[o:o + cnt, :, :].rearrange("p a (h d) -> p (a h) d", d=64), R=[b_vst], W=[self.b_VC], q="pool")
            blk += cnt
        self.dve(lambda: nc.vector.tensor_copy(out=self.KRAW[:, :, 0:16], in_=self.KRAW[:, :, T:T + 16]), R=[self.b_KRAW], W=[self.b_KRAW])

        if getattr(self, 'stop_after', None) == 'cmpr':
            return
        po0, b_po0 = self.po[0]
        po1, b_po1 = self.po[1]
        px, b_px = self.px

        def bias4(pm, b_pm, tile, b_tile, last=True):
            for ro in range(4):
                self.pe(lambda: nc.tensor.matmul(pm[:, ro * 128:(ro + 1) * 128], lhsT=self.ident[:], rhs=tile, start=False,
                                                 stop=(last and ro == 3), skip_group_check=True), R=[self.b_ident, b_tile], W=[b_pm], signal=(ro == 3))

        def merge(b, g, qs, po, b_po, last):
            self.dve(lambda: nc.vector.tensor_scalar(out=tM[0:64, :], in0=po[64:128, :], scalar1=1e-30, scalar2=None, op0=ALU.max), R=[b_po], W=[b_tM])
            self.act(lambda: nc.scalar.copy(out=tO[0:64, :], in_=po[0:64, :]), R=[b_po], W=[b_tO])
            self.dve(lambda: nc.vector.reciprocal(out=tA[0:64, :], in_=tM[0:64, :]), R=[b_tM], W=[b_tA])
            for ro in range(4):
                r = 2 * (ro % 2) + ro // 2
                h = 4 * g + r
                self.pe(lambda: nc.tensor.matmul(px[0:64, ro * 128:(ro + 1) * 128], lhsT=self.esel[0:48, b * 16 + h, :], rhs=gT[0:48, qs * 128:(qs + 1) * 128],
                                                 start=(ro == 0), stop=True, skip_group_check=True), R=[self.b_esel, b_gT], W=[b_px], signal=(ro == 3))
            self.dve(lambda: nc.vector.tensor_tensor(out=tM[0:64, :], in0=tA[0:64, :], in1=px[0:64, :], op=ALU.mult), R=[b_tA, b_px], W=[b_tM])
            self.dve(lambda: nc.vector.tensor_tensor(out=tA[0:64, :], in0=tM[0:64, :], in1=tO[0:64, :], op=ALU.mult), R=[b_tM, b_tO], W=[b_tA])
            if b == 0:
                self.pool(lambda: nc.gpsimd.tensor_tensor(out=yacc[0:64, :], in0=tA[0:64, :], in1=sz64[0:64, b, qs, :], op=ALU.mult), R=[b_tA, b_sz64], W=[b_yacc])
            else:
                self.pool(lambda: nc.gpsimd.tensor_tensor(out=tB[0:64, :], in0=tA[0:64, :], in1=sz64[0:64, b, qs, :], op=ALU.mult), R=[b_tA, b_sz64], W=[b_tB])
                if not last:
                    self.pool(lambda: nc.gpsimd.tensor_tensor(out=yacc[0:64, :], in0=yacc[0:64, :], in1=tB[0:64, :], op=ALU.add), R=[b_tB, b_yacc], W=[b_yacc])
                else:
                    self.pool(lambda: nc.gpsimd.tensor_tensor(out=yB64[0:64, g, qs, :], in0=yacc[0:64, :], in1=tB[0:64, :], op=ALU.add), R=[b_tB, b_yacc], W=[b_yB])

        for g in range(4):
            for b in range(3):
                wt, b_wt = self.next_wst()
                sl = 2 * b + g // 2
                c0 = (g % 2) * 256
                self.ld(wt[:, :, 0:256], self.wB_z[sl, :, :, c0:c0 + 256], W=[b_wt])
                for mm in range(2):
                    pm, b_pm = self.next_pm()
                    self.mm8(pm[:, 0:T], b_pm, lambda kc: wt[:, kc, mm * 128:(mm + 1) * 128], lambda kc: self.xnT[:, kc, :], [self.b_xnT, b_wt])
                    for a in range(2):
                        ro = a * 2 + mm
                        self.act(lambda: nc.scalar.activation(out=sz64[0:64, b, :, ro * 128:(ro + 1) * 128],
                                                              in_=pm[a * 64:(a + 1) * 64, 0:T].rearrange("p (s q) -> p s q", q=128), func=AF.Silu),
                                 R=[b_pm], W=[b_sz64])
            if getattr(self, 'stop_after', None) == 'z':
                return
            for qs in range(NSUB):
                qt = 2 * i + qs
                nts = [0] if i < 8 else [0, 1]
                for nt in nts:
                    need_bias = not (16 * (128 * nt + 127) + 31 <= 128 * qt)
                    pt_, b_ptile = PT[nt]
                    for a in range(2):
                        pm, b_pm = self.next_pm()
                        self.pe(lambda: nc.tensor.matmul(pm[:, 0:256], lhsT=self.KCT[a * 64:(a + 1) * 64, g, nt * 128:(nt + 1) * 128],
                                                         rhs=qrawT[a * 64:(a + 1) * 64, 2 * g:2 * g + 2, qs * 128:(qs + 1) * 128],
                                                         start=True, stop=(not need_bias), skip_group_check=True),
                                R=[self.b_KCT, b_qraw], W=[b_pm])
                        if need_bias:
                            for mm in range(2):
                                self.pe(lambda: nc.tensor.matmul(pm[:, mm * 128:(mm + 1) * 128], lhsT=self.ident[:], rhs=cbias[:, qs, nt, :], start=False,
                                                                 stop=(mm == 1), skip_group_check=True), R=[self.b_ident, b_cbias], W=[b_pm], signal=(mm == 1))
                        self.act(lambda: nc.scalar.activation(out=pt_[:, a * 256:(a + 1) * 256], in_=pm[:, 0:256], func=AF.Exp, scale=0.125), R=[b_pm], W=[b_ptile])
                    self.pe(lambda: nc.tensor.matmul(po0[:, :], lhsT=self.VC[:, nt, g, :], rhs=pt_, start=(nt == 0), stop=(nt == nts[-1])),
                            R=[self.b_VC, b_ptile], W=[b_po0])
                if getattr(self, 'stop_after', None) == 'cS':
                    return
                for ro in range(4):
                    for nt in nts:
                        self.pe(lambda: nc.tensor.matmul(px[:, ro * 65:(ro + 1) * 65], lhsT=PT[nt][0][:, ro * 128:(ro + 1) * 128], rhs=self.ovaug[:, nt, :],
                                                         start=(ro == 0 and nt == 0), stop=(nt == nts[-1]), skip_group_check=True),
                                R=[PT[nt][1], self.b_ovaug], W=[b_px], signal=(ro == 3 and nt == nts[-1]))
                if getattr(self, 'stop_after', None) == 'cI':
                    return
                pxv = px[:, 0:260].rearrange("p (r c) -> p r c", c=65)
                self.dve(lambda: nc.vector.tensor_scalar(out=rd[:, 0:4], in0=pxv[:, :, 64], scalar1=1e-30, scalar2=None, op0=ALU.max), R=[b_px], W=[b_rd])
                self.dve(lambda: nc.vector.reciprocal(out=rd[:, 4:8], in_=rd[:, 0:4]), R=[b_rd], W=[b_rd])
                self.dve(lambda: nc.vector.tensor_scalar(out=impv, in0=pxv[:, 0, 0:64], scalar1=rd[:, 4:5], scalar2=None, op0=ALU.mult), R=[b_px, b_rd], W=[b_imp])
                for ro in range(1, 4):
                    self.dve(lambda: nc.vector.scalar_tensor_tensor(out=impv, in0=pxv[:, ro, 0:64], scalar=rd[:, 4 + ro:5 + ro], in1=impv, op0=ALU.mult, op1=ALU.add),
                             R=[b_px, b_rd, b_imp], W=[b_imp])
                self.dve(lambda: nc.vector.tensor_tensor(out=score, in0=impv, in1=force[:, qs, :], op=ALU.add), R=[b_imp, b_force], W=[b_score])
                self.dve(lambda: nc.vector.max(out=m8[:, 0:8], in_=score), R=[b_score], W=[b_m8])
                self.dve(lambda: nc.vector.match_replace(out=scr, in_to_replace=m8[:, 0:8], in_values=score, imm_value=-1e30), R=[b_score, b_m8], W=[b_scr])
                self.dve(lambda: nc.vector.max(out=m8[:, 8:16], in_=scr), R=[b_scr], W=[b_m8])
                self.dve(lambda: nc.vector.tensor_scalar(out=Btm[:, 64:128], in0=score, scalar1=m8[:, 15:16], scalar2=1.0, op0=ALU.is_ge, op1=ALU.subtract),
                         R=[b_score, b_m8], W=[b_Btm])
                kts = [kt for kt in range(qt - 4, qt + 1) if kt >= 0]
                pend = None
                for j, kt in enumerate(kts):
                    slot = kt % 6
                    pm, b_pm = self.next_pm()
                    nb_ = (kt == qt) or (kt == qt - 4)
                    self.pe(lambda: nc.tensor.matmul(pm[:, :], lhsT=self.KW[0:64, g, slot * 128:(slot + 1) * 128], rhs=qaug[0:64, g, qs, :],
                                                     start=True, stop=not nb_, skip_group_check=True), R=[self.b_KW, b_qaug], W=[b_pm])
                    if kt == qt:
                        bias4(pm, b_pm, self.tri[:], self.b_tri)
                    elif kt == qt - 4:
                        bias4(pm, b_pm, self.atri[:], self.b_atri)
                    if pend is not None:
                        pend()
                    pt_, b_ptile = PT[j % 2]
                    self.act(lambda: nc.scalar.activation(out=pt_, in_=pm[:, :], func=AF.Exp, scale=0.125), R=[b_pm], W=[b_ptile])

                    def pend(slot=slot, pt_=pt_, b_ptile=b_ptile, first=(j == 0), last=(j == len(kts) - 1)):
                        self.pe(lambda: nc.tensor.matmul(po1[:, :], lhsT=self.VW[:, slot, g, :], rhs=pt_, start=first, stop=last),
                                R=[self.b_VW, b_ptile], W=[b_po1])
                pend()
                ptb, b_ptb = self.next_ptr()
                self.pe(lambda: nc.tensor.transpose(out=ptb[:, 0:128], in_=Btm, identity=self.ident[:]), R=[b_Btm, self.b_ident], W=[b_ptb])
                for ro in range(4):
                    self.act(lambda: nc.scalar.copy(out=qaug[64:128, g, qs, ro * 128:(ro + 1) * 128], in_=ptb[64:128, 0:128]), R=[b_ptb], W=[b_qbias])
                merge(0, g, qs, po0, b_po0, False)
                merge(2, g, qs, po1, b_po1, False)
                nk = qt + 1
                for j in range(nk + 2):
                    if j < nk:
                        kt = j
                        pm, b_pm = self.next_pm()
                        self.pe(lambda: nc.tensor.matmul(pm[:, :], lhsT=self.KA[:, g, kt * 128:(kt + 1) * 128], rhs=qaug[:, g, qs, :],
                                                         start=True, stop=(kt != qt), skip_group_check=True), R=[self.b_KA, b_qaug, b_qbias], W=[b_pm])
                        if kt == qt:
                            bias4(pm, b_pm, self.tri[:], self.b_tri)
                        pt_, b_ptile = PT3[j % 3]
                        self.act(lambda: nc.scalar.activation(out=pt_, in_=pm[:, :], func=AF.Exp, scale=0.125), R=[b_pm], W=[b_ptile])
                    if j >= 2:
                        kt = j - 2
                        pt_, b_ptile = PT3[kt % 3]
                        self.pe(lambda: nc.tensor.matmul(po0[:, :], lhsT=self.VS[:, kt, g, :], rhs=pt_, start=(kt == 0), stop=(kt == qt)),
                                R=[self.b_VS, b_ptile], W=[b_po0])
                merge(1, g, qs, po0, b_po0, True)

        if getattr(self, 'stop_after', None) == 'attn':
            return
        for cb in range(2):
            pms = [self.next_pm() for _ in range(NSUB)]
            for hh in range(2):
                wt, b_wt = self.next_wst()
                self.ld(wt[0:64, :, :], self.wB_out[cb * 2 + hh], W=[b_wt])
                for s in range(NSUB):
                    for hl in range(8):
                        h = hh * 8 + hl
                        g, r = h // 4, h % 4
                        ro = (r % 2) * 2 + r // 2
                        self.pe(lambda: nc.tensor.matmul(pms[s][0][:, :], lhsT=yB64[0:64, g, s, ro * 128:(ro + 1) * 128], rhs=wt[0:64, hl, :],
                                                         start=(h == 0), stop=(h == 15)), R=[b_yB, b_wt], W=[pms[s][1]], signal=(hl == 7))
            for s in range(NSUB):
                self.dve(lambda: nc.vector.tensor_tensor(out=self.x_tm[:, s, cb * 512:(cb + 1) * 512], in0=self.x_tm[:, s, cb * 512:(cb + 1) * 512],
                                                         in1=pms[s][0][:, :], op=ALU.add), R=[pms[s][1], self.b_x], W=[self.b_x])
        if getattr(self, 'stop_after', None) == 'wout':
            return
        self.rms_T()
        yout = kvtm[:, :, 0:D]
        for s in range(NSUB):
            self.dve(lambda: nc.vector.scalar_tensor_tensor(out=yout[:, s, :], in0=self.x_tm[:, s, :], scalar=self.st4[:, 4 + s:5 + s], in1=self.fgbc[:, :],
                                                            op0=ALU.mult, op1=ALU.mult), R=[self.b_x, self.b_st4, self.b_fgbc], W=[b_kvtm])
            self.st(self.y_p[t0 + s * 128:t0 + (s + 1) * 128, :], yout[:, s, :], R=[b_kvtm], q="pool")

    def sample_phase(self):
        nc = self.nc
        S = self.S
        S.barrier()
        c = self.cst
        px, b_px = self.px
        po0, b_po0 = self.po[0]
        po1, b_po1 = self.po[1]
        KAf = self.KA[:].rearrange("p a b -> p (a b)")
        VSf = self.VS[:].rearrange("p a b c -> p (a b c)")
        b_KAf, b_VSf = self.b_KA, self.b_VS
        KR2 = [KAf[:, k * 4160:(k + 1) * 4160].rearrange("p (a b) -> p a b", b=1040) for k in range(2)]
        KsTs = KAf[:, 8320:8320 + 4096].rearrange("p (a b) -> p a b", b=2048)
        KCTs = KAf[:, 12352:12352 + 4032]
        VSs = VSf[:, 0:8192].rearrange("p (a b c) -> p a b c", b=4, c=128)
        VCs = VSf[:, 8192:12288].rearrange("p (a b c) -> p a b c", b=4, c=128)
        KCs = VSf[:, 12288:16384].rearrange("p (a b) -> p a b", b=1024)
        cv = KB.Carver(self)
        hS, b_hS = cv.F("hS", [128, 4096])
        small, b_small = cv.F("small", [128, 64])
        mask01, b_mask = cv.F("mask01", [128, 256])
        wgt, b_wgt = cv.F("wgt", [128, 260])
        scoreS, b_scoreS = cv.F("scoreS", [128, 256])
        scrS, b_scrS = cv.F("scrS", [128, 256])
        accS, b_accS = cv.F("accS", [128, 16])
        tS, b_tS = cv.F("tS", [128, 16])
        qb2, b_qb2 = cv.H("qb2", [128, 2, 1024])
        knb, b_knb = cv.H("knb", [128, 2, 256])
        szb, b_szb = cv.H("szb", [128, 3072])
        qTS, b_qTS = cv.H("qTS", [128, 2, 32, SPC])
        kTn, b_kTn = cv.H("kTn", [128, 2, 2, SPC])
        szTS, b_szTS = cv.H("szTS", [128, 48, SPC])
        vnewA, b_vnewA = cv.H("vnewA", [128, 2, 4, 128])
        vrow0, b_vrow0 = cv.H("vrow0", [128, 2, 4, 128])
        zrow, b_zrow = cv.H("zrow", [128, 512])
        yTSall, b_yTSall = cv.H("yTSall", [128, 16, SPC])
        ovS, b_ovS = cv.H("ovS", [128, 8, 258])
        pgb = [cv.H("pgb%d" % k, [128, 512]) for k in range(4)]
        hidTs, b_hidTs = cv.H("hidTs", [128, 4, 128])
        vsts2 = [cv.H("vsts%d" % k, [128, 2, 128]) for k in range(2)]
        PTc, b_PTc = cv.H("PTc", [128, 8, 16])
        PTs, b_PTs = cv.H("PTs", [128, 4, 16, 4])
        M4, b_M4 = KAf[:, 12416:12416 + 2048].rearrange("p (a b c) -> p a b c", b=128, c=4), self.nb("M4")
        KwTs, b_KwTs = cv.H("KwTs", [128, 2, 512])
        VWs, b_VWs = cv.H("VWs", [128, 4, 4, 128])
        pnew, b_pnew = cv.H("pnew", [128, 2, 16])
        b_kr2 = [[self.nb("kr%d_%d" % (b_, k)) for k in range(8)] for b_ in range(2)]
        b_carry2 = [self.nb("kcarry0"), self.nb("kcarry1")]
        b_kst = [self.nb("kst%d" % k) for k in range(16)]
        b_vss = [self.nb("vss%d" % k) for k in range(16)]
        b_VCs = self.nb("VCs")
        b_KCs = self.nb("KCs")
        idxf, b_idxf = cv.F("idxf", [128, 128])
        idxi_t, b_idxi = self.sb("idxi", [128, 128], I32)
        ptbi_t, b_ptbi = self.sb("ptbi", [128, 128], I32)
        iop_t, b_iop = self.sb("iop", [128, 1], I32)
        iopf, b_iopf = cv.F("iopf", [128, 1])
        grps, b_grps = cv.F("grps", [128, 4])
        oh4, b_oh4 = cv.F("oh4", [128, 4, 128])
        forceS, b_forceS = cv.F("forceS", [128, 256])
        ropeS, b_ropeS = cv.F("ropeS", [128, 2, 16, 8])
        rtS, b_rtS = cv.F("rtS", [128, 4, 16, 8])
        w00, b_w00 = cv.F("w00", [128, 2, 16])
        gbrow, b_gbrow = cv.F("gbrow", [128, 48])

        self.ld(ovS, c["c_ovaug_s"].rearrange("n p c -> p n c"), W=[b_ovS], q="pool")
        self.ld(grps[0:16, :], c["c_grpsel"][:, :], W=[b_grps], q="pool")
        self.ld(oh4[0:4, :, :], c["c_onehot4"][:, :, :], W=[b_oh4], q="pool")
        self.ld(forceS[0:4, :], c["c_force_s"][:, :], W=[b_forceS], q="pool")
        self.ld(ropeS[0:SPC], c["c_ropes16"].rearrange("t s h c -> s t h c"), W=[b_ropeS], q="pool")
        self.ld(w00[0:SPC, 0, :], self.a_w_s[:, 0, 0].partition_broadcast(SPC), W=[b_w00], q="pool", slow=True)
        self.ld(w00[0:SPC, 1, :], self.a_b_s[:, 0].partition_broadcast(SPC), W=[b_w00], q="pool", slow=True)
        self.ld(gbrow[0:SPC, :], self.b_gate.partition_broadcast(SPC), W=[b_gbrow], q="pool")
        self.pool(lambda: nc.gpsimd.iota(out=iop_t[:], pattern=[[0, 1]], base=0, channel_multiplier=1), W=[b_iop])
        self.dve(lambda: nc.vector.tensor_copy(out=iopf, in_=iop_t[:]), R=[b_iop], W=[b_iopf])
        self.dve(lambda: nc.vector.memset(zrow, 0.0), W=[b_zrow])
        self.dve(lambda: nc.vector.memset(vnewA[:, :, :, :], 1.0), W=[b_vnewA])

        gvS = hS[:, 0:2048]
        lngS = hS[:, 2048:4096]
        guS, b_guS = szb[:, 0:2048], b_szb
        szS_, b_szS = qb2[:, :, :].rearrange("p a b -> p (a b)"), b_qb2
        self.ld(self.x_tm[0:SPC, 0, :], self.x_s[:, :], W=[self.b_x])
        self.rms_T(SPC, SPC)
        self.xhat_T(SPC, SPC)
        self.ld(lngS[0:SPC, :], self.a_ln_g.partition_broadcast(SPC), W=[b_hS], q="pool")
        for cb in range(12):
            wt, b_wt = self.next_wst()
            self.ld(wt[:], self.wA_in[cb], W=[b_wt])
            pm, b_pm = self.next_pm()
            self.mm8(pm[0:SPC, :], b_pm, lambda kc: self.xnT[:, kc, 0:SPC], lambda kc: wt[:, kc, :], [self.b_xnT, b_wt])
            if cb < 4:
                self.act(lambda: nc.scalar.activation(out=gvS[0:SPC, cb * 512:(cb + 1) * 512], in_=pm[0:SPC, :], func=AF.Gelu), R=[b_pm], W=[b_hS])
            elif cb < 8:
                self.act(lambda: nc.scalar.activation(out=guS[0:SPC, (cb - 4) * 512:(cb - 3) * 512], in_=pm[0:SPC, :], func=AF.Gelu), R=[b_pm], W=[b_guS])
            else:
                self.act(lambda: nc.scalar.activation(out=szS_[0:SPC, (cb - 8) * 512:(cb - 7) * 512], in_=pm[0:SPC, :], func=AF.Silu), R=[b_pm], W=[b_szS])
        stt = small[:, 0:24].rearrange("p (a b) -> p a b", b=6)
        for cb in range(4):
            self.dve(lambda: nc.vector.bn_stats(out=stt[0:SPC, cb, :], in_=gvS[0:SPC, cb * 512:(cb + 1) * 512]), R=[b_hS], W=[b_small])
        self.dve(lambda: nc.vector.bn_aggr(out=small[0:SPC, 24:26], in_=stt[0:SPC, :, :]), R=[b_small], W=[b_small])
        self.dve(lambda: nc.vector.tensor_scalar(out=small[0:SPC, 26:27], in0=small[0:SPC, 25:26], scalar1=EPS, scalar2=None, op0=ALU.add), R=[b_small], W=[b_small])
        self.act(lambda: nc.scalar.activation(out=small[0:SPC, 26:27], in_=small[0:SPC, 26:27], func=AF.Sqrt), R=[b_small], W=[b_small])
        self.dve(lambda: nc.vector.reciprocal(out=small[0:SPC, 27:28], in_=small[0:SPC, 26:27]), R=[b_small], W=[b_small])
        self.dve(lambda: nc.vector.tensor_scalar(out=gvS[0:SPC, :], in0=gvS[0:SPC, :], scalar1=small[0:SPC, 24:25], scalar2=small[0:SPC, 27:28],
                                                 op0=ALU.subtract, op1=ALU.mult), R=[b_hS, b_small], W=[b_hS])
        self.dve(lambda: nc.vector.tensor_tensor(out=gvS[0:SPC, :], in0=gvS[0:SPC, :], in1=lngS[0:SPC, :], op=ALU.mult), R=[b_hS], W=[b_hS])
        self.ld(lngS[0:SPC, :], self.a_ln_b.partition_broadcast(SPC), R=[b_hS], W=[b_hS], q="pool")
        self.dve(lambda: nc.vector.tensor_tensor(out=gvS[0:SPC, :], in0=gvS[0:SPC, :], in1=lngS[0:SPC, :], op=ALU.add), R=[b_hS], W=[b_hS])
        self.st(self.o_chv[:, :], gvS[0:SPC, :], R=[b_hS], q="pool")
        mixS = hS[:, 2048:4096]
        for g in range(16):
            self.dve(lambda: nc.vector.tensor_scalar(out=mixS[0:SPC, g * 128:(g + 1) * 128], in0=gvS[0:SPC, g * 128:(g + 1) * 128], scalar1=w00[0:SPC, 0, g:g + 1],
                                                     scalar2=w00[0:SPC, 1, g:g + 1], op0=ALU.mult, op1=ALU.add), R=[b_hS, b_w00], W=[b_hS])
        self.dve(lambda: nc.vector.tensor_tensor(out=mixS[0:SPC, :], in0=mixS[0:SPC, :], in1=guS[0:SPC, :], op=ALU.mult), R=[b_hS, b_guS], W=[b_hS])
        ySb, b_ySb = KAf[:, 0:2048], b_KAf
        self.dve(lambda: nc.vector.tensor_tensor(out=ySb[0:SPC, :], in0=mixS[0:SPC, :], in1=szS_[0:SPC, :], op=ALU.mult), R=[b_hS, b_szS], W=[b_ySb])
        yTS, b_yTS = cv.H("yTS", [128, 16, SPC])
        pt, b_pt = self.next_ptr()
        for fc in range(16):
            self.pe(lambda: nc.tensor.transpose(out=pt[:, fc * SPC:(fc + 1) * SPC], in_=ySb[0:SPC, fc * 128:(fc + 1) * 128], identity=self.ident[0:SPC, 0:SPC]),
                    R=[b_ySb, self.b_ident], W=[b_pt], signal=(fc == 15))
        self.act(lambda: nc.scalar.copy(out=yTS[:, :, :], in_=pt[:, 0:16 * SPC].rearrange("p (a b) -> p a b", b=SPC)), R=[b_pt], W=[b_yTS])
        for cb in range(2):
            pm, b_pm = self.next_pm()
            for fcg in range(4):
                wt, b_wt = self.next_wst()
                self.ld(wt[:, 0:4, :], self.wA_out[fcg * 2 + cb], W=[b_wt])
                for fl in range(4):
                    fc = fcg * 4 + fl
                    self.pe(lambda: nc.tensor.matmul(pm[0:SPC, :], lhsT=yTS[:, fc, :], rhs=wt[:, fl, :], start=(fc == 0), stop=(fc == 15)),
                            R=[b_yTS, b_wt], W=[b_pm], signal=(fl == 3))
            self.dve(lambda: nc.vector.tensor_tensor(out=self.x_tm[0:SPC, 0, cb * 512:(cb + 1) * 512], in0=self.x_tm[0:SPC, 0, cb * 512:(cb + 1) * 512],
                                                     in1=pm[0:SPC, :], op=ALU.add), R=[b_pm, self.b_x], W=[self.b_x])

        self.rms_T(SPC, SPC)
        self.xhat_T(SPC, SPC)
        slabs = [(self.wB_q[0], 0, None), (self.wB_q[1], 512, None)]
        slabs += [(self.wB_kv[k], 1024 + 512 * k, None) for k in range(3)]
        slabs += [(self.wB_z[k], 512 * k, AF.Silu) for k in range(6)]
        for (src, c0, fn) in slabs:
            wt, b_wt = self.next_wst()
            self.ld(wt[:], src, W=[b_wt])
            pm, b_pm = self.next_pm()
            self.mm8(pm[0:SPC, :], b_pm, lambda kc: self.xnT[:, kc, 0:SPC], lambda kc: wt[:, kc, :], [self.b_xnT, b_wt])
            if fn is None:
                self.act(lambda: nc.scalar.copy(out=hS[0:SPC, c0:c0 + 512], in_=pm[0:SPC, :]), R=[b_pm], W=[b_hS])
            else:
                self.act(lambda: nc.scalar.activation(out=szb[0:SPC, c0:c0 + 512], in_=pm[0:SPC, :], func=fn), R=[b_pm], W=[b_szb])
        pm, b_pm = self.next_pm()
        self.mm8(pm[0:SPC, 0:48], b_pm, lambda kc: self.xnT[:, kc, 0:SPC], lambda kc: self.wgl[:, kc, :], [self.b_xnT, self.b_wgl])
        self.dve(lambda: nc.vector.tensor_tensor(out=hS[0:SPC, 2560:2608], in0=pm[0:SPC, 0:48], in1=gbrow[0:SPC, :], op=ALU.add), R=[b_pm, b_gbrow], W=[b_hS])
        self.act(lambda: nc.scalar.activation(out=hS[0:SPC, 2560:2608], in_=hS[0:SPC, 2560:2608], func=AF.Sigmoid), R=[b_hS], W=[b_hS])
        qv = hS[0:SPC, 0:1024].rearrange("s (p a r d) -> s p a r d", p=2, a=2, r=4)
        for p in range(2):
            self.dve(lambda: nc.vector.tensor_copy(out=qb2[0:SPC, 0, p * 512:(p + 1) * 512].rearrange("s (r a d) -> s r a d", r=4, a=2),
                                                   in_=qv[:, p, :, :, :].rearrange("s a r d -> s r a d")), R=[b_hS], W=[b_qb2])
        def rope_tm(view, nh):
            x1 = view[:, :, 0:8]
            x2 = view[:, :, 8:16]
            cs = ropeS[0:SPC, 0, 0:nh, :]
            sn = ropeS[0:SPC, 1, 0:nh, :]
            R_ = [b_hS, b_ropeS]
            self.dve(lambda: nc.vector.tensor_tensor(out=rtS[0:SPC, 0, 0:nh, :], in0=x1, in1=cs, op=ALU.mult), R=R_, W=[b_rtS])
            self.dve(lambda: nc.vector.tensor_tensor(out=rtS[0:SPC, 1, 0:nh, :], in0=x2, in1=sn, op=ALU.mult), R=R_, W=[b_rtS])
            self.dve(lambda: nc.vector.tensor_tensor(out=rtS[0:SPC, 2, 0:nh, :], in0=x2, in1=cs, op=ALU.mult), R=R_, W=[b_rtS])
            self.dve(lambda: nc.vector.tensor_tensor(out=rtS[0:SPC, 3, 0:nh, :], in0=x1, in1=sn, op=ALU.mult), R=R_, W=[b_rtS])
            self.dve(lambda: nc.vector.tensor_tensor(out=x1, in0=rtS[0:SPC, 0, 0:nh, :], in1=rtS[0:SPC, 1, 0:nh, :], op=ALU.subtract), R=[b_rtS], W=[b_hS])
            self.dve(lambda: nc.vector.tensor_tensor(out=x2, in0=rtS[0:SPC, 2, 0:nh, :], in1=rtS[0:SPC, 3, 0:nh, :], op=ALU.add), R=[b_rtS], W=[b_hS])
        rope_tm(hS[0:SPC, 0:1024].rearrange("s (h d) -> s h d", d=64), 16)
        rope_tm(hS[0:SPC, 1536:1792].rearrange("s (h d) -> s h d", d=64), 4)
        rope_tm(hS[0:SPC, 2048:2304].rearrange("s (h d) -> s h d", d=64), 4)
        for p in range(2):
            self.dve(lambda: nc.vector.tensor_copy(out=qb2[0:SPC, 1, p * 512:(p + 1) * 512].rearrange("s (r a d) -> s r a d", r=4, a=2),
                                                   in_=qv[:, p, :, :, :].rearrange("s a r d -> s r a d")), R=[b_hS], W=[b_qb2])
        self.st(self.o_cmp_s[:, :], hS[0:SPC, 1024:1536], R=[b_hS], q="pool")
        self.st(self.o_sel_s[:, :], hS[0:SPC, 1536:2048], R=[b_hS], q="pool")
        self.st(self.o_win_s[:, 511, :], hS[0:SPC, 2048:2560], R=[b_hS], q="pool")
        for s in range(SPC):
            self.dma(lambda: nc.sync.dma_start(out=self.o_win_s[s, 0:511, :], in_=self.state_w[s, 1:512, :]), q="sp", out=True)
        self.dve(lambda: nc.vector.tensor_copy(out=knb[0:SPC, 0, :], in_=hS[0:SPC, 1536:1792]), R=[b_hS], W=[b_knb])
        self.dve(lambda: nc.vector.tensor_copy(out=knb[0:SPC, 1, :], in_=hS[0:SPC, 2048:2304]), R=[b_hS], W=[b_knb])
        self.dve(lambda: nc.vector.tensor_copy(out=vnewA[0:SPC, 0, :, 0:64], in_=hS[0:SPC, 1792:2048].rearrange("s (g d) -> s g d", d=64)), R=[b_hS], W=[b_vnewA])
        self.dve(lambda: nc.vector.tensor_copy(out=vnewA[0:SPC, 1, :, 0:64], in_=hS[0:SPC, 2304:2560].rearrange("s (g d) -> s g d", d=64)), R=[b_hS], W=[b_vnewA])
        idS = self.ident[0:SPC, 0:SPC]
        for v in range(2):
            pt, b_pt = self.next_ptr()
            for k in range(8):
                self.pe(lambda: nc.tensor.transpose(out=pt[:, k * SPC:(k + 1) * SPC], in_=qb2[0:SPC, v, k * 128:(k + 1) * 128], identity=idS),
                        R=[b_qb2, self.b_ident], W=[b_pt], signal=(k == 7))
            self.act(lambda: nc.scalar.copy(out=qTS[:, v, 0:8, :], in_=pt[:, 0:8 * SPC].rearrange("p (a b) -> p a b", b=SPC)), R=[b_pt], W=[b_qTS])
        pt, b_pt = self.next_ptr()
        for v in range(2):
            for p in range(2):
                k = v * 2 + p
                self.pe(lambda: nc.tensor.transpose(out=pt[:, k * SPC:(k + 1) * SPC], in_=knb[0:SPC, v, p * 128:(p + 1) * 128], identity=idS),
                        R=[b_knb, self.b_ident], W=[b_pt], signal=(k == 3))
        self.act(lambda: nc.scalar.copy(out=kTn[:, :, :, :].rearrange("p a b c -> p (a b) c"), in_=pt[:, 0:4 * SPC].rearrange("p (a b) -> p a b", b=SPC)),
                 R=[b_pt], W=[b_kTn])
        pt, b_pt = self.next_ptr()
        for k in range(48):
            self.pe(lambda: nc.tensor.transpose(out=pt[0:64, k * SPC:(k + 1) * SPC], in_=szb[0:SPC, k * 64:(k + 1) * 64], identity=idS),
                    R=[b_szb, self.b_ident], W=[b_pt], signal=(k == 47))
        self.act(lambda: nc.scalar.copy(out=szTS[0:64, :, :], in_=pt[0:64, 0:48 * SPC].rearrange("p (a b) -> p a b", b=SPC)), R=[b_pt], W=[b_szTS])
        S.barrier()

        KCsv = KCs
        for s in range(SPC):
            self.ld(ptbi_t[:], self.page_t[s].partition_broadcast(128), W=[b_ptbi], q="pool")
            self.dve(lambda: nc.vector.tensor_copy(out=idxf, in_=ptbi_t[:]), R=[b_ptbi], W=[b_idxf])
            self.dve(lambda: nc.vector.tensor_scalar(out=idxf, in0=idxf, scalar1=128.0, scalar2=iopf[:, 0:1], op0=ALU.mult, op1=ALU.add), R=[b_idxf, b_iopf], W=[b_idxf])
            self.dve(lambda: nc.vector.tensor_copy(out=idxi_t[:], in_=idxf), R=[b_idxf], W=[b_idxi])
            self.ld(vrow0[0:1, :, :, :], vnewA[s:s + 1, :, :, :], R=[b_vnewA], W=[b_vrow0], q="pool")
            self.pool(lambda: nc.gpsimd.memset(VCs[:, :, :, 0:64], 0.0), W=[b_VCs])
            self.pool(lambda: nc.gpsimd.memset(VCs[:, :, :, 64:128], 1.0), W=[b_VCs])
            self.ld(VCs[127:128, 7, :, :].rearrange("p a b -> p (a b)"), zrow[0:1, :], R=[b_zrow], W=[b_VCs], q="pool")
            self.pool(lambda: nc.gpsimd.memset(KCsv[:, :, :], 0.0), W=[b_KCs])
            if s == 0:
                self.pool(lambda: nc.gpsimd.memset(VSs[:, :, :, 64:128], 1.0), W=b_vss)
            self.pool(lambda: nc.gpsimd.memset(VWs[:, :, :, 64:128], 1.0), W=[b_VWs])
            w1vs = []
            for kv in range(2):
                wt, b_wt = self.wst[kv]
                w1v = wt[:].rearrange("p a b -> p (a b)").rearrange("p (j c) -> p j c", c=128)
                self.ld(w1v, self.w1bd[kv], W=[b_wt])
                w1vs.append((w1v, b_wt))
            deferred = []

            def compress_steps(gi):
                buf = gi % 2
                KRb = KR2[buf]
                nblk = 63 if gi == 0 else 64
                base = 16 if gi == 0 else 0
                m0 = 0 if gi == 0 else 64 * gi - 1
                pmh, b_pmh = self.next_pm()
                steps = []
                for kv in range(2):
                    w1v, b_wt = w1vs[kv]
                    for hp in range(2):
                        tl = kv * 2 + hp
                        for j0 in range(0, 32, 8):
                            def st_(tl=tl, j0=j0, w1v=w1v, b_wt=b_wt):
                                for j in range(j0, j0 + 8):
                                    self.pe(lambda: nc.tensor.matmul(pmh[:, tl * 64:tl * 64 + nblk], lhsT=w1v[:, j, :],
                                                                     rhs=KRb[:, tl, base + j:base + j + 16 * (nblk - 1) + 1:16],
                                                                     start=(tl == 0 and j == 0), stop=(j == 31), skip_group_check=True),
                                            R=[b_wt, b_carry2[buf]] + b_kr2[buf], W=[b_pmh], signal=(j == j0 + 7))
                            steps.append(st_)

                def tail():
                    for kv in range(2):
                        self.act(lambda: nc.scalar.activation(out=hidTs[:, 2 * kv:2 * kv + 2, 0:nblk],
                                                              in_=pmh[:, 128 * kv:128 * kv + 128].rearrange("p (a b) -> p a b", b=64)[:, :, 0:nblk],
                                                              func=AF.Silu, bias=self.bias1[:, kv:kv + 1]), R=[b_pmh, self.b_bias1], W=[b_hidTs])
                    pmk, b_pmk = self.next_pm()
                    for g in range(4):
                        self.pe(lambda: nc.tensor.matmul(pmk[:, g * 64:g * 64 + nblk], lhsT=self.W2K[:, g % 2, :], rhs=hidTs[:, g // 2, 0:nblk],
                                                         start=(g == 0), stop=True, skip_group_check=True), R=[self.b_W2K, b_hidTs], W=[b_pmk], signal=(g == 3))
                    self.act(lambda: nc.scalar.activation(out=KCsv[:, :, m0:m0 + nblk], in_=pmk[:, 0:256].rearrange("p (a b) -> p a b", b=64)[:, :, 0:nblk],
                                                          func=AF.Identity, bias=self.b2K[:, 0:1]), R=[b_pmk, self.b_b2K], W=[b_KCs])
                    pmv, b_pmv = self.next_pm()
                    for hp in range(2):
                        self.pe(lambda: nc.tensor.matmul(pmv[0:nblk, hp * 128:(hp + 1) * 128], lhsT=hidTs[:, 2 + hp, 0:nblk], rhs=self.W2V[:, :],
                                                         start=(hp == 0), stop=False, skip_group_check=True), R=[b_hidTs, self.b_W2V], W=[b_pmv], signal=False)
                        self.pe(lambda: nc.tensor.matmul(pmv[0:nblk, hp * 128:(hp + 1) * 128], lhsT=self.ones_row[0:1, 0:nblk], rhs=self.b2Vrow[0:1, :],
                                                         start=False, stop=True, skip_group_check=True), R=[self.b_ones, self.b_b2V], W=[b_pmv], signal=(hp == 1))
                    vsts, b_vsts_k = vsts2[gi % 2]
                    self.act(lambda: nc.scalar.copy(out=vsts[0:nblk, :, :], in_=pmv[0:nblk, 0:256].rearrange("p (a b) -> p a b", b=128)), R=[b_pmv], W=[b_vsts_k])
                    vv = vsts[:, :, :].rearrange("p a (h d) -> p (a h) d", d=64)

                    def place():
                        blk = m0
                        while blk < m0 + nblk:
                            nt_, p0 = blk // 128, blk % 128
                            cnt = min(m0 + nblk - blk, 128 - p0)
                            o = blk - m0
                            self.ld(VCs[p0:p0 + cnt, nt_, :, 0:64], vv[o:o + cnt], R=[b_vsts_k], W=[b_VCs], q="pool")
                            blk += cnt
                    deferred.append(place)
                steps.append(tail)
                return steps

            pending = []
            for gi in range(16):
                buf = gi % 2
                for pl in range(8):
                    page = gi * 8 + pl
                    pb, b_pb = pgb[page % 4]
                    self.dma(lambda: nc.gpsimd.indirect_dma_start(out=pb, out_offset=None, in_=self.cache_c[:, :],
                                                                  in_offset=bass.IndirectOffsetOnAxis(ap=idxi_t[:, page:page + 1], axis=0)),
                             R=[b_idxi], W=[b_pb], q="pool")
                    if pl == 4:
                        while deferred:
                            deferred.pop(0)()
                    pt, b_pt = self.next_ptr()
                    for tl in range(4):
                        self.pe(lambda: nc.tensor.transpose(out=pt[:, tl * 128:(tl + 1) * 128], in_=pb[:, tl * 128:(tl + 1) * 128], identity=self.ident[:]),
                                R=[b_pb, self.b_ident], W=[b_pt], signal=(tl == 3))
                    dst = KR2[buf][:, :, 16 + pl * 128:16 + (pl + 1) * 128]
                    if page % 2 == 0:
                        self.act(lambda: nc.scalar.copy(out=dst, in_=pt[:, 0:512].rearrange("p (a b) -> p a b", b=128)), R=[b_pt], W=[b_kr2[buf][pl]])
                    else:
                        self.dve(lambda: nc.vector.tensor_copy(out=dst, in_=pt[:, 0:512].rearrange("p (a b) -> p a b", b=128)), R=[b_pt], W=[b_kr2[buf][pl]])
                    share = -(-len(pending) // (8 - pl)) if pending else 0
                    for _ in range(share):
                        pending.pop(0)()
                self.dve(lambda: nc.vector.tensor_copy(out=KR2[1 - buf][:, :, 0:16], in_=KR2[buf][:, :, 1024:1040]), R=[b_kr2[buf][7]], W=[b_carry2[1 - buf]])
                assert not pending
                pending = compress_steps(gi)
            while pending:
                pending.pop(0)()
            while deferred:
                deferred.pop(0)()
            pmA, b_pmA = self.next_pm()
            pmB, b_pmB = self.next_pm()
            banks = [(pmA, b_pmA), (pmB, b_pmB)]
            for nt in range(8):
                for g in range(4):
                    p, a = g // 2, g % 2
                    pmx, b_pmx = banks[a]
                    col = (nt * 2 + p) * 4
                    self.pe(lambda: nc.tensor.matmul(pmx[:, col:col + 4], lhsT=KCsv[a * 64:(a + 1) * 64, g, nt * 128:(nt + 1) * 128],
                                                     rhs=qTS[a * 64:(a + 1) * 64, 0, p * 4:(p + 1) * 4, s], start=(nt == 0 and p == 0), stop=True, skip_group_check=True),
                            R=[b_KCs, b_qTS], W=[b_pmx], signal=(nt == 7 and p == 1))
            PTcv = PTc[:, :, :].rearrange("p n (q a r) -> p n q a r", q=2, a=2)
            for a in range(2):
                pmx, b_pmx = banks[a]
                self.act(lambda: nc.scalar.activation(out=PTcv[:, :, :, a, :], in_=pmx[:, 0:64].rearrange("p (n q r) -> p n q r", q=2, r=4), func=AF.Exp, scale=0.125),
                         R=[b_pmx], W=[b_PTc])
            for g in range(4):
                for nt in range(8):
                    self.pe(lambda: nc.tensor.matmul(po0[:, g * 4:(g + 1) * 4], lhsT=VCs[:, nt, g, :], rhs=PTc[:, nt, g * 4:(g + 1) * 4],
                                                     start=(g == 0 and nt == 0), stop=(nt == 7), skip_group_check=True), R=[b_VCs, b_PTc], W=[b_po0], signal=(nt == 7))
            for nt in range(8):
                self.pe(lambda: nc.tensor.matmul(px[0:16, 0:258], lhsT=PTc[:, nt, :], rhs=ovS[:, nt, :], start=(nt == 0), stop=(nt == 7)),
                        R=[b_PTc, b_ovS], W=[b_px], signal=(nt == 7))
            self.dve(lambda: nc.vector.tensor_scalar(out=small[0:16, 32:33], in0=px[0:16, 257:258], scalar1=1e-30, scalar2=None, op0=ALU.max), R=[b_px], W=[b_small])
            self.dve(lambda: nc.vector.reciprocal(out=small[0:16, 33:34], in_=small[0:16, 32:33]), R=[b_small], W=[b_small])
            self.dve(lambda: nc.vector.tensor_scalar(out=wgt[0:16, 0:256], in0=px[0:16, 0:256], scalar1=small[0:16, 33:34], scalar2=None, op0=ALU.mult), R=[b_px, b_small], W=[b_wgt])
            self.pe(lambda: nc.tensor.matmul(px[0:4, 0:256], lhsT=grps[0:16, :], rhs=wgt[0:16, 0:256], start=True, stop=True), R=[b_grps, b_wgt], W=[b_px])
            self.dve(lambda: nc.vector.tensor_tensor(out=scoreS[0:4, :], in0=px[0:4, 0:256], in1=forceS[0:4, :], op=ALU.add), R=[b_px, b_forceS], W=[b_scoreS])
            self.dve(lambda: nc.vector.max(out=small[0:4, 40:48], in_=scoreS[0:4, :]), R=[b_scoreS], W=[b_small])
            self.dve(lambda: nc.vector.match_replace(out=scrS[0:4, :], in_to_replace=small[0:4, 40:48], in_values=scoreS[0:4, :], imm_value=-1e30), R=[b_scoreS, b_small], W=[b_scrS])
            self.dve(lambda: nc.vector.max(out=small[0:4, 48:56], in_=scrS[0:4, :]), R=[b_scrS], W=[b_small])
            self.dve(lambda: nc.vector.tensor_scalar(out=mask01[0:4, :], in0=scoreS[0:4, :], scalar1=small[0:4, 54:55], scalar2=None, op0=ALU.is_ge), R=[b_scoreS, b_small], W=[b_mask])
            for g in range(4):
                self.pe(lambda: nc.tensor.matmul(px[:, 0:256], lhsT=oh4[0:4, g, :], rhs=mask01[0:4, :], start=True, stop=True), R=[b_oh4, b_mask], W=[b_px])
                pxv = px[:, 0:256].rearrange("p (k two) -> p k two", two=2)
                for r in range(4):
                    self.dve(lambda: nc.vector.tensor_copy(out=M4[0:64, g, :, r], in_=pxv[0:64, :, 0]), R=[b_px], W=[b_M4])
                    self.act(lambda: nc.scalar.copy(out=M4[64:128, g, :, r], in_=pxv[64:128, :, 1]), R=[b_px], W=[b_M4])

            first = True
            for pgp in range(8):
                for pl in range(16):
                    page = pgp * 16 + pl
                    pb, b_pb = pgb[page % 4]
                    self.dma(lambda: nc.gpsimd.indirect_dma_start(out=pb, out_offset=None, in_=self.cache_s[:, :],
                                                                  in_offset=bass.IndirectOffsetOnAxis(ap=idxi_t[:, page:page + 1], axis=0)),
                             R=[b_idxi], W=[b_pb], q="pool")
                    self.dve(lambda: nc.vector.tensor_copy(out=VSs[:, pl, :, 0:64], in_=pb[:, 256:512].rearrange("p (g d) -> p g d", d=64)), R=[b_pb], W=[b_vss[pl]])
                    pt, b_pt = self.next_ptr()
                    for p in range(2):
                        self.pe(lambda: nc.tensor.transpose(out=pt[:, p * 128:(p + 1) * 128], in_=pb[:, p * 128:(p + 1) * 128], identity=self.ident[:]),
                                R=[b_pb, self.b_ident], W=[b_pt], signal=(p == 1))
                    self.act(lambda: nc.scalar.copy(out=KsTs[:, :, pl * 128:(pl + 1) * 128], in_=pt[:, 0:256].rearrange("p (a b) -> p a b", b=128)), R=[b_pt], W=[b_kst[pl]])
                pmA, b_pmA = self.next_pm()
                pmB, b_pmB = self.next_pm()
                banks = [(pmA, b_pmA), (pmB, b_pmB)]
                for g in range(4):
                    p, a = g // 2, g % 2
                    pmx, b_pmx = banks[a]
                    for pl in range(16):
                        col = (p * 16 + pl) * 4
                        self.pe(lambda: nc.tensor.matmul(pmx[:, col:col + 4], lhsT=KsTs[a * 64:(a + 1) * 64, p, pl * 128:(pl + 1) * 128],
                                                         rhs=qTS[a * 64:(a + 1) * 64, 1, p * 4:(p + 1) * 4, s], start=(p == 0 and pl == 0), stop=True, skip_group_check=True),
                                R=[b_kst[pl], b_qTS], W=[b_pmx], signal=(pl == 15))
                PTsv = PTs[:, :, :, :].rearrange("p (q a) k r -> p q a k r", a=2)
                for a in range(2):
                    pmx, b_pmx = banks[a]
                    self.act(lambda: nc.scalar.activation(out=PTsv[:, :, a, :, :], in_=pmx[:, 0:128].rearrange("p (q k r) -> p q k r", q=2, r=4), func=AF.Exp, scale=0.125),
                             R=[b_pmx], W=[b_PTs])
                self.dve(lambda: nc.vector.tensor_tensor(out=PTs[:, :, :, :], in0=PTs[:, :, :, :], in1=M4[:, :, pgp * 16:(pgp + 1) * 16, :], op=ALU.mult), R=[b_PTs, b_M4], W=[b_PTs])
                for g in range(4):
                    for pl in range(16):
                        self.pe(lambda: nc.tensor.matmul(po1[:, g * 4:(g + 1) * 4], lhsT=VSs[:, pl, g, :], rhs=PTs[:, g, pl, :],
                                                         start=first, stop=False, skip_group_check=True), R=[b_vss[pl], b_PTs], W=[b_po1], signal=(pl == 15))
                        first = False
            def new_token(v, po, b_po, col0, qidx):
                pmA, b_pmA = self.next_pm()
                pmB, b_pmB = self.next_pm()
                bk = [(pmA, b_pmA), (pmB, b_pmB)]
                for g in range(4):
                    p, a = g // 2, g % 2
                    pmx, b_pmx = bk[a]
                    self.pe(lambda: nc.tensor.matmul(pmx[0:1, p * 4:(p + 1) * 4], lhsT=kTn[a * 64:(a + 1) * 64, v, p, s:s + 1],
                                                     rhs=qTS[a * 64:(a + 1) * 64, qidx, p * 4:(p + 1) * 4, s], start=(p == 0), stop=True, skip_group_check=True),
                            R=[b_kTn, b_qTS], W=[b_pmx])
                pv = pnew[0:1, v, :].rearrange("o (q a r) -> o q a r", q=2, a=2)
                for a in range(2):
                    pmx, b_pmx = bk[a]
                    self.act(lambda: nc.scalar.activation(out=pv[:, :, a, :], in_=pmx[0:1, 0:8].rearrange("o (q r) -> o q r", r=4), func=AF.Exp, scale=0.125), R=[b_pmx], W=[b_pnew])
                for g in range(4):
                    self.pe(lambda: nc.tensor.matmul(po[:, col0 + g * 4:col0 + (g + 1) * 4], lhsT=vrow0[0:1, v, g, :], rhs=pnew[0:1, v, g * 4:(g + 1) * 4],
                                                     start=False, stop=True, skip_group_check=True), R=[b_vrow0, b_pnew], W=[b_po], signal=(g == 3))
            new_token(0, po1, b_po1, 0, 1)

            wst32 = hS[:, 0:2048].rearrange("p (k c) -> p k c", c=512)
            self.ld(wst32, self.state_w[s].rearrange("(k p) c -> p k c", p=128), W=[b_hS])
            wkb = szb[:, 0:1024].rearrange("p (k c) -> p k c", c=256)
            self.dve(lambda: nc.vector.tensor_copy(out=wkb, in_=wst32[:, :, 0:256]), R=[b_hS], W=[b_szb])
            for kt in range(4):
                self.act(lambda: nc.scalar.copy(out=VWs[:, kt, :, 0:64], in_=wst32[:, kt, 256:512].rearrange("p (g d) -> p g d", d=64)), R=[b_hS], W=[b_VWs])
                pt, b_pt = self.next_ptr()
                for p in range(2):
                    self.pe(lambda: nc.tensor.transpose(out=pt[:, p * 128:(p + 1) * 128], in_=wkb[:, kt, p * 128:(p + 1) * 128], identity=self.ident[:]),
                            R=[b_szb, self.b_ident], W=[b_pt], signal=(p == 1))
                self.act(lambda: nc.scalar.copy(out=KwTs[:, :, kt * 128:(kt + 1) * 128], in_=pt[:, 0:256].rearrange("p (a b) -> p a b", b=128)), R=[b_pt], W=[b_KwTs])
            pmA, b_pmA = self.next_pm()
            pmB, b_pmB = self.next_pm()
            banks = [(pmA, b_pmA), (pmB, b_pmB)]
            for g in range(4):
                p, a = g // 2, g % 2
                pmx, b_pmx = banks[a]
                for kt in range(4):
                    col = (p * 4 + kt) * 4
                    self.pe(lambda: nc.tensor.matmul(pmx[:, col:col + 4], lhsT=KwTs[a * 64:(a + 1) * 64, p, kt * 128:(kt + 1) * 128],
                                                     rhs=qTS[a * 64:(a + 1) * 64, 1, p * 4:(p + 1) * 4, s], start=(p == 0 and kt == 0), stop=True, skip_group_check=True),
                            R=[b_KwTs, b_qTS], W=[b_pmx], signal=(kt == 3))
            PTw = PTs[:, :, 0:4, :]
            PTwv = PTw.rearrange("p (q a) k r -> p q a k r", a=2)
            for a in range(2):
                pmx, b_pmx = banks[a]
                self.act(lambda: nc.scalar.activation(out=PTwv[:, :, a, :, :], in_=pmx[:, 0:32].rearrange("p (q k r) -> p q k r", q=2, r=4), func=AF.Exp, scale=0.125),
                         R=[b_pmx], W=[b_PTs])
            self.dve(lambda: nc.vector.memset(PTs[0:1, :, 0, :], 0.0), W=[b_PTs])
            for g in range(4):
                for kt in range(4):
                    self.pe(lambda: nc.tensor.matmul(po0[:, 16 + g * 4:16 + (g + 1) * 4], lhsT=VWs[:, kt, g, :], rhs=PTs[:, g, kt, :],
                                                     start=False, stop=False, skip_group_check=True), R=[b_VWs, b_PTs], W=[b_po0], signal=(kt == 3))
            new_token(1, po0, b_po0, 16, 1)

            self.pe(lambda: nc.tensor.matmul(px[0:64, 0:48], lhsT=oh4[0:4, s, 0:64], rhs=hS[0:SPC, 2560:2608], start=True, stop=True), R=[b_oh4, b_hS], W=[b_px])
            for bi, (b, po, b_po, c0) in enumerate(((0, po0, b_po0, 0), (1, po1, b_po1, 0), (2, po0, b_po0, 16))):
                self.dve(lambda: nc.vector.tensor_scalar(out=tS[64:128, :], in0=po[64:128, c0:c0 + 16], scalar1=1e-30, scalar2=None, op0=ALU.max), R=[b_po], W=[b_tS])
                self.dve(lambda: nc.vector.reciprocal(out=tS[0:64, :], in_=tS[64:128, :]), R=[b_tS], W=[b_tS])
                self.dve(lambda: nc.vector.tensor_tensor(out=tS[0:64, :], in0=tS[0:64, :], in1=po[0:64, c0:c0 + 16], op=ALU.mult), R=[b_tS, b_po], W=[b_tS])
                self.dve(lambda: nc.vector.tensor_tensor(out=tS[0:64, :], in0=tS[0:64, :], in1=px[0:64, b * 16:(b + 1) * 16], op=ALU.mult), R=[b_tS, b_px], W=[b_tS])
                if bi == 0:
                    self.dve(lambda: nc.vector.tensor_tensor(out=accS[0:64, :], in0=tS[0:64, :], in1=szTS[0:64, b * 16:(b + 1) * 16, s], op=ALU.mult), R=[b_tS, b_szTS], W=[b_accS])
                else:
                    self.dve(lambda: nc.vector.tensor_tensor(out=tS[0:64, :], in0=tS[0:64, :], in1=szTS[0:64, b * 16:(b + 1) * 16, s], op=ALU.mult), R=[b_tS, b_szTS], W=[b_tS])
                    self.dve(lambda: nc.vector.tensor_tensor(out=accS[0:64, :], in0=accS[0:64, :], in1=tS[0:64, :], op=ALU.add), R=[b_tS, b_accS], W=[b_accS])
            self.dve(lambda: nc.vector.tensor_copy(out=yTSall[0:64, :, s], in_=accS[0:64, :]), R=[b_accS], W=[b_yTSall])

        for cb in range(2):
            pm, b_pm = self.next_pm()
            for hh in range(2):
                wt, b_wt = self.next_wst()
                self.ld(wt[0:64, :, :], self.wB_out[cb * 2 + hh], W=[b_wt])
                for hl in range(8):
                    h = hh * 8 + hl
                    self.pe(lambda: nc.tensor.matmul(pm[0:SPC, :], lhsT=yTSall[0:64, h, :], rhs=wt[0:64, hl, :], start=(h == 0), stop=(h == 15)),
                            R=[b_yTSall, b_wt], W=[b_pm], signal=(hl == 7))
            self.dve(lambda: nc.vector.tensor_tensor(out=self.x_tm[0:SPC, 0, cb * 512:(cb + 1) * 512], in0=self.x_tm[0:SPC, 0, cb * 512:(cb + 1) * 512],
                                                     in1=pm[0:SPC, :], op=ALU.add), R=[b_pm, self.b_x], W=[self.b_x])
        self.rms_T(SPC, SPC)
        youtS = hS[:, 2048:3072]
        self.dve(lambda: nc.vector.scalar_tensor_tensor(out=youtS[0:SPC, :], in0=self.x_tm[0:SPC, 0, :], scalar=self.st4[0:SPC, 4:5], in1=self.fgbc[0:SPC, :],
                                                        op0=ALU.mult, op1=ALU.mult), R=[self.b_x, self.b_st4, self.b_fgbc], W=[b_hS])
        self.st(self.y_s[:, :], youtS[0:SPC, :], R=[b_hS], q="pool")

    def build(self, debug_stage=None):
        nc = self.nc
        self.alloc()
        self.prologue()
        for i in range(self.ntiles):
            self.layer_a(i)
            if debug_stage == "A":
                for s in range(NSUB):
                    self.st(self.y_p[i * T + s * 128:i * T + (s + 1) * 128, :], self.x_tm[:, s, :], R=[self.b_x])
                self.S.barrier()
                continue
            self.S.barrier()
            self.layer_b(i)
            self.S.barrier()
        if self.do_sample and debug_stage is None:
            self.sample_phase()
        self.S.finish("sp")
        return nc


def core_inputs(inp, c, n_phys, dummy_cache=False):
    m = {}
    m["x_prompt"] = np.ascontiguousarray(inp["x_prompt"][c])
    m["x_sample"] = np.ascontiguousarray(inp["x_sample"][c * SPC:(c + 1) * SPC, 0, :])
    if dummy_cache:
        m["cache_cmp_kv"] = np.zeros((n_phys * 128, 512), np.float32)
        m["cache_sel_kv"] = np.zeros((n_phys * 128, 512), np.float32)
    else:
        m["cache_cmp_kv"] = inp["cache_cmp_kv"].reshape(n_phys * 128, 512)
        m["cache_sel_kv"] = inp["cache_sel_kv"].reshape(n_phys * 128, 512)
    m["state_win_kv"] = np.ascontiguousarray(inp["state_win_kv"][0, c * SPC:(c + 1) * SPC].reshape(SPC, 512, 512))
    m["page_table"] = np.ascontiguousarray(inp["page_table"][c * SPC:(c + 1) * SPC])
    m["norm_g"] = inp["norm_g"]
    m["final_norm_g"] = inp["final_norm_g"]
    m["a_w_in"] = inp["a_w_in"][0]
    m["a_ln_g"] = inp["a_ln_g"][0]
    m["a_ln_b"] = inp["a_ln_b"][0]
    m["a_w_s"] = inp["a_w_s"][0]
    m["a_b_s"] = inp["a_b_s"][0]
    m["a_w_out"] = inp["a_w_out"][0]
    m["b_w_in"] = inp["b_w_in"][0]
    m["b_cmp_pe"] = inp["b_cmp_pe"][0]
    m["b_cmp_w1"] = inp["b_cmp_w1"][0]
    m["b_cmp_b1"] = inp["b_cmp_b1"][0]
    m["b_cmp_w2"] = inp["b_cmp_w2"][0]
    m["b_cmp_b2"] = inp["b_cmp_b2"][0]
    m["b_gate_b"] = inp["b_gate_b"][0].reshape(48)
    m["b_w_out"] = inp["b_w_out"][0]
    m.update(host_consts())
    return m


_NC_CACHE = {}


def kernel(x_prompt, x_sample, cache_cmp_kv, cache_sel_kv, state_win_kv, page_table,
           norm_g, final_norm_g, a_w_in, a_ln_g, a_ln_b, a_w_s, a_b_s, a_w_out,
           b_w_in, b_cmp_pe, b_cmp_w1, b_cmp_b1, b_cmp_w2, b_cmp_b2, b_gate_b, b_w_out):
    inp = dict(x_prompt=x_prompt, x_sample=x_sample, cache_cmp_kv=cache_cmp_kv, cache_sel_kv=cache_sel_kv,
               state_win_kv=state_win_kv, page_table=page_table, norm_g=norm_g, final_norm_g=final_norm_g,
               a_w_in=a_w_in, a_ln_g=a_ln_g, a_ln_b=a_ln_b, a_w_s=a_w_s, a_b_s=a_b_s, a_w_out=a_w_out,
               b_w_in=b_w_in, b_cmp_pe=b_cmp_pe, b_cmp_w1=b_cmp_w1, b_cmp_b1=b_cmp_b1, b_cmp_w2=b_cmp_w2,
               b_cmp_b2=b_cmp_b2, b_gate_b=b_gate_b, b_w_out=b_w_out)
    inp = {k: np.asarray(v) for k, v in inp.items()}
    n_phys = inp["cache_cmp_kv"].shape[1]
    if n_phys not in _NC_CACHE:
        kb = KB(n_phys=n_phys)
        _NC_CACHE[n_phys] = kb.build()
    nc = _NC_CACHE[n_phys]
    in_maps = [core_inputs(inp, c, n_phys) for c in range(NCORES)]
    res = run_bass_kernel_spmd(nc, in_maps, core_ids=list(range(NCORES)))
    r = res.results
    nb = SPC * NCORES
    y_prompt = np.stack([r[c]["y_prompt"] for c in range(NCORES)], 0)
    y_sample = np.concatenate([r[c]["y_sample"] for c in range(NCORES)], 0).reshape(nb, 1, D)
    cmp_p = np.stack([r[c]["cmp_kv_prompt"] for c in range(NCORES)], 0).reshape(1, NCORES, SEQ, 2, NKV, HD)
    cmp_s = np.concatenate([r[c]["cmp_kv_sample"] for c in range(NCORES)], 0).reshape(1, nb, 1, 2, NKV, HD)
    sel_p = np.stack([r[c]["sel_kv_prompt"] for c in range(NCORES)], 0).reshape(1, NCORES, SEQ, 2, NKV, HD)
    sel_s = np.concatenate([r[c]["sel_kv_sample"] for c in range(NCORES)], 0).reshape(1, nb, 1, 2, NKV, HD)
    win_p = np.stack([r[c]["win_kv_prompt"] for c in range(NCORES)], 0).reshape(1, NCORES, 512, 2, NKV, HD)
    win_s = np.concatenate([r[c]["win_kv_sample"] for c in range(NCORES)], 0).reshape(1, nb, 512, 2, NKV, HD)
    chv = np.concatenate([r[c]["chunk_v_sample"] for c in range(NCORES)], 0).reshape(1, nb, 1, AW)
    outs = (y_prompt, y_sample, cmp_p, cmp_s, sel_p, sel_s, win_p, win_s, chv)
    return tuple(np.ascontiguousarray(o.astype(np.float32)) for o in outs)
```
